# Optimizing a Trainium2 kernel written in Bass

```python
import jax, jax.numpy as jnp
from jax import lax
import numpy as np

D_MODEL = 2048
BATCH = 8
SEQ = 2048
DEPTH = 4

HEAD_DIM = 64
N_FOX_HEADS = D_MODEL // 2 // HEAD_DIM
N_SWA_HEADS = D_MODEL // 2 // HEAD_DIM
N_SWA_KV_HEADS = max(1, N_SWA_HEADS // 8)
FOX_WIDTH = N_FOX_HEADS * HEAD_DIM
SWA_WIDTH = N_SWA_HEADS * HEAD_DIM
SWA_KV_WIDTH = N_SWA_KV_HEADS * HEAD_DIM
MIX_WIDTH = FOX_WIDTH + SWA_WIDTH
IN_SPLIT_SIZES = (FOX_WIDTH, FOX_WIDTH, FOX_WIDTH, N_FOX_HEADS, SWA_WIDTH, SWA_KV_WIDTH, SWA_KV_WIDTH)
IN_PROJ_WIDTH = sum(IN_SPLIT_SIZES)
D_FF = ((8 * D_MODEL // 3 + 255) // 256) * 256
N_META = 16
BLOCK = 128
WINDOW = 128
PAD = BLOCK - N_META
EPS = 1e-6
NEG_INF = -1e30

kernel_name = "hymba_fox_swa_sink_alibi_macaron"


def rms_norm(x, g):
    xf = x.astype(jnp.float32)
    y = xf * lax.rsqrt(jnp.mean(xf * xf, axis=-1, keepdims=True) + EPS)
    return (y * g.astype(jnp.float32)).astype(x.dtype)


def swiglu(x, w_gate, w_up, w_down):
    return (jax.nn.silu(x @ w_gate) * (x @ w_up)) @ w_down


def alibi_slopes(n_heads):
    return jnp.asarray(2.0 ** (-8.0 * np.arange(1, n_heads + 1) / n_heads), dtype=jnp.float32)


def forgetting_attention(q, k, v, log_f):
    L = q.shape[1]
    scale = HEAD_DIM ** -0.5
    c = jnp.cumsum(log_f, axis=1).transpose(0, 2, 1)
    pos = jnp.arange(L)
    outs = []
    for i in range(L // BLOCK):
        q0, q1 = i * BLOCK, (i + 1) * BLOCK
        s = jnp.einsum('bqhd,bkhd->bhqk', q[:, q0:q1], k[:, :q1]).astype(jnp.float32) * scale
        s = s + c[:, :, q0:q1, None] - c[:, :, None, :q1]
        qp = pos[q0:q1][:, None]
        kp = pos[:q1][None, :]
        allowed = (kp <= qp) & (kp >= PAD)
        p = jax.nn.softmax(jnp.where(allowed, s, NEG_INF), axis=-1)
        outs.append(jnp.einsum('bhqk,bkhd->bqhd', p.astype(v.dtype), v[:, :q1]))
    return jnp.concatenate(outs, axis=1)


def sliding_window_sink_attention(q, k, v, sinks):
    B, L, Hq, Dh = q.shape
    Hkv = k.shape[2]
    G = Hq // Hkv
    NB = L // BLOCK
    scale = HEAD_DIM ** -0.5
    qb = q.reshape(B, NB, BLOCK, Hkv, G, Dh)
    kb = k.reshape(B, NB, BLOCK, Hkv, Dh)
    vb = v.reshape(B, NB, BLOCK, Hkv, Dh)
    shift = ((0, 0), (1, 0), (0, 0), (0, 0), (0, 0))
    k_band = jnp.concatenate([jnp.pad(kb, shift)[:, :-1], kb], axis=2)
    v_band = jnp.concatenate([jnp.pad(vb, shift)[:, :-1], vb], axis=2)
    s = jnp.einsum('bnqhgd,bnkhd->bnhgqk', qb, k_band).astype(jnp.float32) * scale
    blk = jnp.arange(NB)[:, None] * BLOCK
    qpos = blk + jnp.arange(BLOCK)[None, :]
    kpos = blk - BLOCK + jnp.arange(2 * BLOCK)[None, :]
    dist = qpos[:, :, None] - kpos[:, None, :]
    allowed = (dist >= 0) & (dist < WINDOW) & (kpos[:, None, :] >= PAD)
    slopes = alibi_slopes(Hq).reshape(Hkv, G)[None, None, :, :, None, None]
    s = s - slopes * dist.astype(jnp.float32)[None, :, None, None]
    s = jnp.where(allowed[None, :, None, None], s, NEG_INF)
    sink = sinks.astype(jnp.float32).reshape(Hkv, G)[None, None, :, :, None, None]
    m = jnp.maximum(jnp.max(s, axis=-1, keepdims=True), sink)
    p = jnp.exp(s - m)
    p = p / (jnp.sum(p, axis=-1, keepdims=True) + jnp.exp(sink - m))
    o = jnp.einsum('bnhgqk,bnkhd->bnqhgd', p.astype(v.dtype), v_band)
    return o.reshape(B, L, Hq, Dh)


def setup_inputs(seed: int = 0) -> dict:
    key = jax.random.key(seed)
    ks = jax.random.split(key, 24)
    f32 = jnp.float32

    def nrm(k, shape, scale):
        return jax.random.normal(k, shape, f32) * scale

    def gain(k, shape):
        return 1.0 + 0.02 * jax.random.normal(k, shape, f32)

    return {
        "x": nrm(ks[0], (BATCH, SEQ, D_MODEL), 1.0),
        "meta_tokens": nrm(ks[1], (N_META, D_MODEL), 1.0),
        "ffn1_norm": gain(ks[2], (DEPTH, D_MODEL)),
        "ffn1_w_gate": nrm(ks[3], (DEPTH, D_MODEL, D_FF), D_MODEL ** -0.5),
        "ffn1_w_up": nrm(ks[4], (DEPTH, D_MODEL, D_FF), D_MODEL ** -0.5),
        "ffn1_w_down": nrm(ks[5], (DEPTH, D_FF, D_MODEL), D_FF ** -0.5),
        "mix_norm": gain(ks[6], (DEPTH, D_MODEL)),
        "w_in": nrm(ks[7], (DEPTH, D_MODEL, IN_PROJ_WIDTH), D_MODEL ** -0.5),
        "b_forget": 2.0 + 0.3 * jax.random.normal(ks[8], (DEPTH, N_FOX_HEADS), f32),
        "fox_q_norm": gain(ks[9], (DEPTH, HEAD_DIM)),
        "fox_k_norm": gain(ks[10], (DEPTH, HEAD_DIM)),
        "swa_q_norm": gain(ks[11], (DEPTH, HEAD_DIM)),
        "swa_k_norm": gain(ks[12], (DEPTH, HEAD_DIM)),
        "swa_sinks": nrm(ks[13], (DEPTH, N_SWA_HEADS), 1.0),
        "fox_out_norm": gain(ks[14], (DEPTH, FOX_WIDTH)),
        "swa_out_norm": gain(ks[15], (DEPTH, SWA_WIDTH)),
        "w_out": nrm(ks[16], (DEPTH, MIX_WIDTH, D_MODEL), MIX_WIDTH ** -0.5),
        "ffn2_norm": gain(ks[17], (DEPTH, D_MODEL)),
        "ffn2_w_gate": nrm(ks[18], (DEPTH, D_MODEL, D_FF), D_MODEL ** -0.5),
        "ffn2_w_up": nrm(ks[19], (DEPTH, D_MODEL, D_FF), D_MODEL ** -0.5),
        "ffn2_w_down": nrm(ks[20], (DEPTH, D_FF, D_MODEL), D_FF ** -0.5),
    }


def reference(x, meta_tokens, ffn1_norm, ffn1_w_gate, ffn1_w_up, ffn1_w_down, mix_norm, w_in,
              b_forget, fox_q_norm, fox_k_norm, swa_q_norm, swa_k_norm, swa_sinks,
              fox_out_norm, swa_out_norm, w_out, ffn2_norm, ffn2_w_gate, ffn2_w_up, ffn2_w_down):
    B, S, D = x.shape
    h = jnp.concatenate([
        jnp.zeros((B, PAD, D), x.dtype),
        jnp.broadcast_to(meta_tokens.astype(x.dtype)[None], (B, N_META, D)),
        x,
    ], axis=1)
    L = h.shape[1]
    split_idx = [int(v) for v in np.cumsum(IN_SPLIT_SIZES)[:-1]]

    for l in range(DEPTH):
        h = h + 0.5 * swiglu(rms_norm(h, ffn1_norm[l]), ffn1_w_gate[l], ffn1_w_up[l], ffn1_w_down[l])

        u = rms_norm(h, mix_norm[l]) @ w_in[l]
        fq, fk, fv, fz, sq, sk, sv = jnp.split(u, split_idx, axis=-1)
        fq = rms_norm(fq.reshape(B, L, N_FOX_HEADS, HEAD_DIM), fox_q_norm[l])
        fk = rms_norm(fk.reshape(B, L, N_FOX_HEADS, HEAD_DIM), fox_k_norm[l])
        fv = fv.reshape(B, L, N_FOX_HEADS, HEAD_DIM)
        log_f = jax.nn.log_sigmoid(fz.astype(jnp.float32) + b_forget[l].astype(jnp.float32))
        sq = rms_norm(sq.reshape(B, L, N_SWA_HEADS, HEAD_DIM), swa_q_norm[l])
        sk = rms_norm(sk.reshape(B, L, N_SWA_KV_HEADS, HEAD_DIM), swa_k_norm[l])
        sv = sv.reshape(B, L, N_SWA_KV_HEADS, HEAD_DIM)

        o_fox = forgetting_attention(fq, fk, fv, log_f).reshape(B, L, FOX_WIDTH)
        o_swa = sliding_window_sink_attention(sq, sk, sv, swa_sinks[l]).reshape(B, L, SWA_WIDTH)
        o = jnp.concatenate([rms_norm(o_fox, fox_out_norm[l]), rms_norm(o_swa, swa_out_norm[l])], axis=-1)
        h = h + o @ w_out[l]

        h = h + 0.5 * swiglu(rms_norm(h, ffn2_norm[l]), ffn2_w_gate[l], ffn2_w_up[l], ffn2_w_down[l])

    return h[:, BLOCK:]
```

```python
import numpy as np
from contextlib import ExitStack

import concourse.bass as bass
import concourse.mybir as mybir
from concourse.bass_utils import run_bass_kernel_spmd

F32 = mybir.dt.float32
BF16 = mybir.dt.bfloat16
AF = mybir.ActivationFunctionType
ALU = mybir.AluOpType
AX = mybir.AxisListType

ENGS = ("pe", "act", "dve", "pool", "sp")


class Op:
    __slots__ = ("eng", "fn", "deps", "dma_key", "signal", "seq", "dma_target", "idx")


class Phase:
    def __init__(self, nc, name):
        self.nc = nc
        self.name = name
        self.ops = []
        self.last_w = {}
        self.readers = {}

    def op(self, eng, fn, reads=(), writes=(), dma=None):
        o = Op()
        o.eng = eng
        o.fn = fn
        o.dma_key = dma
        o.signal = False
        o.seq = 0
        o.dma_target = 0
        o.idx = len(self.ops)
        deps = set()
        for r in reads:
            w = self.last_w.get(r)
            if w is not None:
                deps.add(w)
            if isinstance(r, tuple) and r[0] == "ps":
                for rd in self.readers.get(r, ()):
                    if self.ops[rd].eng != eng:
                        deps.add(rd)
        for w_ in writes:
            w = self.last_w.get(w_)
            if w is not None:
                deps.add(w)
            for rd in self.readers.get(w_, ()):
                deps.add(rd)
        deps.discard(o.idx)
        o.deps = deps
        for r in reads:
            self.readers.setdefault(r, []).append(o.idx)
        for w_ in writes:
            self.last_w[w_] = o.idx
            self.readers[w_] = []
        self.ops.append(o)
        return o

    def emit(self):
        nc = self.nc
        ops = self.ops
        for o in ops:
            nd = set()
            for d in o.deps:
                p = ops[d]
                if p.dma_key is None and o.dma_key is None and p.eng == "pe" and o.eng == "pe":
                    continue
                nd.add(d)
            o.deps = nd
            for d in nd:
                if ops[d].dma_key is None:
                    ops[d].signal = True
        seqc = {e: 0 for e in ENGS}
        dmac = {}
        for o in ops:
            if o.dma_key is not None:
                dmac[o.dma_key] = dmac.get(o.dma_key, 0) + 16
                o.dma_target = dmac[o.dma_key]
            elif o.signal:
                seqc[o.eng] += 1
                o.seq = seqc[o.eng]
        dma_keys = list(dmac.keys())
        dma_issuer = {}
        for o in ops:
            if o.dma_key is not None:
                dma_issuer.setdefault(o.dma_key, o.eng)
        with ExitStack() as es:
            es.enter_context(nc.cleanup_on_exit())
            esem = {e: nc.alloc_semaphore(name=f"{self.name}_s_{e}") for e in ENGS}
            dsem = {k: nc.alloc_semaphore(name=f"{self.name}_d{i}") for i, k in enumerate(dma_keys)}
            block = es.enter_context(nc.Block())

            def stream(ename):
                def body(eng):
                    known = {}
                    for o in ops:
                        if o.eng != ename:
                            continue
                        need = {}
                        for d in o.deps:
                            p = ops[d]
                            if p.dma_key is not None:
                                s, v = dsem[p.dma_key], p.dma_target
                            else:
                                s, v = esem[p.eng], p.seq
                            if need.get(s, 0) < v:
                                need[s] = v
                        for s, v in need.items():
                            if known.get(s, 0) < v:
                                eng.wait_ge(s, v)
                                known[s] = v
                        ins = o.fn(eng)
                        if o.dma_key is not None:
                            ins.then_inc(dsem[o.dma_key], 16)
                        elif o.signal:
                            ins.then_inc(esem[ename], 1)
                    for k in dma_keys:
                        if dma_issuer[k] == ename and known.get(dsem[k], 0) < dmac[k]:
                            eng.wait_ge(dsem[k], dmac[k])
                return body

            if any(o.eng == "pe" for o in ops):
                block.tensor(stream("pe"))
            if any(o.eng == "act" for o in ops):
                block.scalar(stream("act"))
            if any(o.eng == "dve" for o in ops):
                block.vector(stream("dve"))
            if any(o.eng == "pool" for o in ops):
                block.gpsimd(stream("pool"))
            if any(o.eng == "sp" for o in ops):
                block.sync(stream("sp"))


class Cfg:
    def __init__(self, D=2048, DFF=5632, NT=2064, tile=344):
        self.D = D
        self.DFF = DFF
        self.KC = D // 128
        self.FC = DFF // 128
        self.NT = NT
        self.tile = tile
        self.EPS = 1e-6


def _rot(n):
    i = [0]

    def nxt():
        v = i[0] % n
        i[0] += 1
        return v

    return nxt


HD = 64
NFH = 16
NSH = 16
INW = 4368
C_FQ, C_FK, C_FV, C_FZ, C_SQ, C_SK, C_SV = 0, 1024, 2048, 3072, 3088, 4112, 4240
NBLK = 17
NROW = NBLK * 128
VP_G1, VP_GM, VP_G2, VP_FQN, VP_FKN, VP_SQN, VP_SKN, VP_BF, VP_SINK, VP_GO = 0, 16, 32, 48, 49, 50, 51, 52, 68, 84
VP_W = 84
CK_INVD, CK_BD64, CK_ONES, CK_TRI, CK_ID, CK_EPS, CK_ONE = 0, 128, 256, 384, 512, 640, 641
CK_W = 642


def rowof(t):
    return t if t < 16 else t + 112


def blk_cols(b):
    return (0, 16) if b == 0 else (16 + 128 * (b - 1), 128)


def norm_stage(ph, sb, cfg, K, ps, hT, gcol, tiles, offs, xnT, TM):
    KC = cfg.KC
    hT_r = hT.rearrange("c p t -> p c t")
    SM = (TM + 1) // 2 + 1
    hbuf = [sb(f"hbuf{i}", [128, KC, SM], F32) for i in range(2)]
    sq = [sb(f"sq{i}", [128, SM], F32) for i in range(4)]
    rstd = [sb(f"rstd{i}", [128, SM], F32) for i in range(2)]
    ptmp = [sb(f"ptmp{i}", [128, SM], F32) for i in range(2)]
    u = 0
    for ti, (t0, n) in enumerate(tiles):
        h1 = (n // 2 + 1) // 2 * 2
        for (s0, m) in ((0, h1), (h1, n - h1)):
            off = offs[ti] + s0
            hb = hbuf[u % 2]
            r = u % 2
            pb = u % 2
            ph.op("sp", lambda e, hb=hb, a=t0 + s0, m=m: e.dma_start(out=hb[:, :, :m], in_=hT_r[:, :, a:a + m]),
                  reads=[("hT", c, ti) for c in range(KC)], writes=[("hbuf", u % 2, c) for c in range(KC)], dma=("hbuf", u % 2))
            for c in range(KC):
                s = c % 4
                ph.op("act", lambda e, hb=hb, c=c, s=s, m=m: e.activation(out=sq[s][:, :m], in_=hb[:, c, :m], func=AF.Square),
                      reads=[("hbuf", u % 2, c)], writes=[("sq", s)])
                ph.op("pe", lambda e, c=c, s=s, m=m, pb=pb: e.matmul(ps[pb][:, :m], lhsT=K["invD"], rhs=sq[s][:, :m],
                                                                start=(c == 0), stop=(c == KC - 1)),
                      reads=[("sq", s)], writes=[("ps", pb)])
            ph.op("act", lambda e, r=r, m=m, pb=pb: e.activation(out=rstd[r][:, :m], in_=ps[pb][:, :m], func=AF.Ln,
                                                              bias=K["eps"], scale=1.0),
                  reads=[("ps", pb)], writes=[("rstd", r)])
            ph.op("act", lambda e, r=r, m=m: e.activation(out=rstd[r][:, :m], in_=rstd[r][:, :m], func=AF.Exp, scale=-0.5),
                  reads=[("rstd", r)], writes=[("rstd", r)])
            for c in range(KC):
                if True:
                    ph.op("dve", lambda e, hb=hb, c=c, r=r, m=m, off=off: e.scalar_tensor_tensor(
                        out=xnT[:, c, off:off + m], in0=hb[:, c, :m], scalar=gcol[:, c:c + 1], in1=rstd[r][:, :m],
                        op0=ALU.mult, op1=ALU.mult),
                        reads=[("hbuf", u % 2, c), ("rstd", r)], writes=[("xn", ti, c)])
                else:
                    tb = (c // 4) % 2
                    ph.op("pool", lambda e, hb=hb, c=c, r=r, m=m, tb=tb: e.tensor_tensor(
                        out=ptmp[tb][:, :m], in0=hb[:, c, :m], in1=rstd[r][:, :m], op=ALU.mult),
                        reads=[("hbuf", u % 2, c), ("rstd", r)], writes=[("ptmp", tb)])
                    ph.op("pool", lambda e, c=c, m=m, off=off, tb=tb: e.tensor_scalar(
                        out=xnT[:, c, off:off + m], in0=ptmp[tb][:, :m], scalar1=gcol[:, c:c + 1], scalar2=None, op0=ALU.mult),
                        reads=[("ptmp", tb)], writes=[("xn", ti, c)])
            u += 1


def down_stage(ph, sb, cfg, ps, hT, W_r, nin, srcT, src_res, tiles, offs, TM, scale, pre_loaded=0, wd=None):
    KC = cfg.KC
    if wd is None:
        wd = [sb(f"wd{i}", [128, nin, 128], BF16) for i in range(2)]
    hold = [sb(f"hold{i}", [128, TM], F32) for i in range(2)]
    hnew = [sb(f"hnew{i}", [128, TM], F32) for i in range(2)]

    def load_d(dc):
        s = dc % 2
        ph.op("pool", lambda e, dc=dc, s=s: e.dma_start(out=wd[s][:, :, :], in_=W_r[:, :, dc * 128:(dc + 1) * 128]),
              writes=[("wd", s)], dma=("wd", s))

    for dc in range(pre_loaded, min(2, KC)):
        load_d(dc)
    db = _rot(2)
    seq = [(dc, ti) for dc in range(KC) for ti in range(len(tiles))]

    def load_hold(i):
        dc, ti = seq[i]
        t0, n = tiles[ti]
        k = i % 2
        ph.op("sp", lambda e, k=k, dc=dc, t0=t0, n=n: e.dma_start(out=hold[k][:, :n], in_=hT[dc, :, t0:t0 + n]),
              reads=[("hT", dc, ti)], writes=[("hold", k)], dma=("hold", k))

    load_hold(0)
    for i, (dc, ti) in enumerate(seq):
        t0, n = tiles[ti]
        off = offs[ti]
        s = dc % 2
        d_b = 6 + db()
        k = i % 2
        if i + 1 < len(seq):
            load_hold(i + 1)
        for fc in range(nin):
            ph.op("pe", lambda e, fc=fc, s=s, n=n, off=off, d_b=d_b: e.matmul(
                ps[d_b][:, :n], lhsT=wd[s][:, fc, :], rhs=srcT[:, fc, off:off + n], start=(fc == 0), stop=(fc == nin - 1)),
                reads=[("wd", s), src_res(fc, ti)], writes=[("ps", d_b)])
        ph.op("dve", lambda e, k=k, n=n, d_b=d_b: e.scalar_tensor_tensor(
            out=hnew[k][:, :n], in0=ps[d_b][:, :n], scalar=scale, in1=hold[k][:, :n], op0=ALU.mult, op1=ALU.add),
            reads=[("ps", d_b), ("hold", k)], writes=[("hnew", k)])
        ph.op("sp", lambda e, k=k, dc=dc, t0=t0, n=n: e.dma_start(out=hT[dc, :, t0:t0 + n], in_=hnew[k][:, :n]),
              reads=[("hnew", k)], writes=[("hT", dc, ti)], dma=("hnew", k))
        if ti == len(tiles) - 1 and dc + 2 < KC:
            load_d(dc + 2)
    return load_d


def _offs(tiles):
    offs, o = [], 0
    for _, n in tiles:
        offs.append(o)
        o += n
    return offs, o


def ffn_phase(nc, cfg, K, ps, hT, Wg, Wu, Wd, gcol, tiles, name):
    KC, FC = cfg.KC, cfg.FC
    offs, NTp = _offs(tiles)
    TM = max(n for _, n in tiles)
    Wg_r = Wg.rearrange("(c p) f -> p c f", p=128)
    Wu_r = Wu.rearrange("(c p) f -> p c f", p=128)
    Wd_r = Wd.rearrange("(c p) d -> p c d", p=128)
    ph = Phase(nc, name)
    with ExitStack() as es:
        def sb(nm, shape, dt):
            return es.enter_context(nc.sbuf_tensor(f"{name}_{nm}", shape, dt))
        xnT = sb("xnT", [128, KC, NTp], BF16)
        actT = sb("actT", [128, FC, NTp], BF16)
        wg = [sb(f"wg{i}", [128, KC, 128], BF16) for i in range(2)]
        wu = [sb(f"wu{i}", [128, KC, 128], BF16) for i in range(2)]
        wd = [sb(f"wd{i}", [128, FC, 128], BF16) for i in range(2)]
        sg = [sb(f"sg{i}", [128, TM], F32) for i in range(2)]

        def load_gu(fc):
            s = fc % 2
            ph.op("pool", lambda e, fc=fc, s=s: e.dma_start(out=wg[s][:, :, :], in_=Wg_r[:, :, fc * 128:(fc + 1) * 128]),
                  writes=[("wg", s)], dma=("wg", s))
            ph.op("pool", lambda e, fc=fc, s=s: e.dma_start(out=wu[s][:, :, :], in_=Wu_r[:, :, fc * 128:(fc + 1) * 128]),
                  writes=[("wu", s)], dma=("wu", s))

        def load_d(dc):
            s = dc % 2
            ph.op("pool", lambda e, dc=dc, s=s: e.dma_start(out=wd[s][:, :, :], in_=Wd_r[:, :, dc * 128:(dc + 1) * 128]),
                  writes=[("wd", s)], dma=("wd", s))

        load_gu(0)
        load_gu(1)
        norm_stage(ph, sb, cfg, K, ps, hT, gcol, tiles, offs, xnT, TM)
        gb, ub, sgr = _rot(2), _rot(2), _rot(2)
        for fc in range(FC):
            s = fc % 2
            for ti, (t0, n) in enumerate(tiles):
                off = offs[ti]
                g_b = 2 + gb()
                u_b = 4 + ub()
                for c in range(KC):
                    ph.op("pe", lambda e, c=c, s=s, n=n, off=off, g_b=g_b: e.matmul(
                        ps[g_b][:, :n], lhsT=wg[s][:, c, :], rhs=xnT[:, c, off:off + n], start=(c == 0), stop=(c == KC - 1)),
                        reads=[("wg", s), ("xn", ti, c)], writes=[("ps", g_b)])
                for c in range(KC):
                    ph.op("pe", lambda e, c=c, s=s, n=n, off=off, u_b=u_b: e.matmul(
                        ps[u_b][:, :n], lhsT=wu[s][:, c, :], rhs=xnT[:, c, off:off + n], start=(c == 0), stop=(c == KC - 1)),
                        reads=[("wu", s), ("xn", ti, c)], writes=[("ps", u_b)])
                k = sgr()
                ph.op("act", lambda e, k=k, n=n, g_b=g_b: e.activation(out=sg[k][:, :n], in_=ps[g_b][:, :n], func=AF.Silu),
                      reads=[("ps", g_b)], writes=[("sg", k)])
                ph.op("dve", lambda e, k=k, n=n, u_b=u_b, fc=fc, off=off: e.tensor_tensor(
                    out=actT[:, fc, off:off + n], in0=ps[u_b][:, :n], in1=sg[k][:, :n], op=ALU.mult),
                    reads=[("ps", u_b), ("sg", k)], writes=[("act", fc, ti)])
            if fc + 2 < FC:
                load_gu(fc + 2)
            elif fc + 2 == FC:
                load_d(0)
            elif fc + 2 == FC + 1:
                load_d(1)
        down_stage(ph, sb, cfg, ps, hT, Wd_r, FC, actT, lambda fc, ti: ("act", fc, ti), tiles, offs, TM, 0.5,
                   pre_loaded=2, wd=wd)
        ph.emit()


def groups_of(t_lo, t_hi):
    gs = []
    t = t_lo
    while t < t_hi:
        if t < 16:
            ng = 16 - t
        else:
            ng = min(128 - (t - 16) % 128, t_hi - t)
        gs.append((t, ng, rowof(t), t - t_lo))
        t += ng
    return gs


def init_phase(nc, cfg, K, ps, x, meta, hT, name):
    KC = cfg.KC
    hT_r = hT.rearrange("c p t -> p c t")
    ph = Phase(nc, name)
    with ExitStack() as es:
        def sb(nm, shape, dt):
            return es.enter_context(nc.sbuf_tensor(f"{name}_{nm}", shape, dt))
        xt = [sb(f"xt{i}", [128, cfg.D], F32) for i in range(2)]
        ht = [sb(f"ht{i}", [128, KC, 128], F32) for i in range(2)]
        gs = groups_of(0, cfg.NT)

        def load_x(gi):
            t0, ng, row0, off = gs[gi]
            k = gi % 2
            src = meta[0:16, :] if t0 == 0 else x[t0 - 16:t0 - 16 + ng, :]
            ph.op("sp", lambda e: e.dma_start(out=xt[k][:ng, :], in_=src), writes=[("xt", k)], dma=("xt", k))

        load_x(0)
        for gi, (t0, ng, row0, off) in enumerate(gs):
            k = gi % 2
            if gi + 1 < len(gs):
                load_x(gi + 1)
            for q4 in range(KC // 4):
                b = (gi * (KC // 4) + q4) % 8
                for a in range(4):
                    c = q4 * 4 + a
                    ph.op("pe", lambda e, k=k, ng=ng, c=c, a=a, b=b: e.transpose(
                        ps[b][:, a * 128:a * 128 + ng], xt[k][:ng, c * 128:(c + 1) * 128], K["id32"][:ng, :ng]),
                        reads=[("xt", k)], writes=[("ps", b)])
                eng = "act" if q4 % 2 == 0 else "dve"
                if eng == "act":
                    ph.op("act", lambda e, k=k, ng=ng, q4=q4, b=b: e.activation(
                        out=ht[k][:, q4 * 4:q4 * 4 + 4, :ng], in_=ps[b][:, :].rearrange("p (a t) -> p a t", a=4)[:, :, :ng],
                        func=AF.Copy), reads=[("ps", b)], writes=[("ht", k, q4)])
                else:
                    ph.op("dve", lambda e, k=k, ng=ng, q4=q4, b=b: e.tensor_copy(
                        out=ht[k][:, q4 * 4:q4 * 4 + 4, :ng], in_=ps[b][:, :].rearrange("p (a t) -> p a t", a=4)[:, :, :ng]),
                        reads=[("ps", b)], writes=[("ht", k, q4)])
            ph.op("sp", lambda e, k=k, ng=ng, t0=t0: e.dma_start(out=hT_r[:, :, t0:t0 + ng], in_=ht[k][:, :, :ng]),
                  reads=[("ht", k, q4) for q4 in range(KC // 4)], writes=[("hTw", gi)], dma=("ht", k))
        ph.emit()


def final_phase(nc, cfg, K, ps, hT, out, name):
    KC = cfg.KC
    hT_r = hT.rearrange("c p t -> p c t")
    ph = Phase(nc, name)
    with ExitStack() as es:
        def sb(nm, shape, dt):
            return es.enter_context(nc.sbuf_tensor(f"{name}_{nm}", shape, dt))
        ht = [sb(f"ht{i}", [128, KC, 128], F32) for i in range(2)]
        ot = [sb(f"ot{i}", [128, cfg.D], F32) for i in range(2)]
        gs = groups_of(16, cfg.NT)

        def load_h(gi):
            t0, ng, row0, off = gs[gi]
            k = gi % 2
            ph.op("sp", lambda e: e.dma_start(out=ht[k][:, :, :ng], in_=hT_r[:, :, t0:t0 + ng]),
                  writes=[("ht", k)], dma=("ht", k))

        load_h(0)
        for gi, (t0, ng, row0, off) in enumerate(gs):
            k = gi % 2
            if gi + 1 < len(gs):
                load_h(gi + 1)
            for q4 in range(KC // 4):
                b = (gi * (KC // 4) + q4) % 8
                for a in range(4):
                    c = q4 * 4 + a
                    ph.op("pe", lambda e, k=k, ng=ng, c=c, a=a, b=b: e.transpose(
                        ps[b][:ng, a * 128:(a + 1) * 128], ht[k][:, c, :ng], K["id32"]),
                        reads=[("ht", k)], writes=[("ps", b)])
                if q4 % 2 == 0:
                    ph.op("act", lambda e, k=k, ng=ng, q4=q4, b=b: e.activation(
                        out=ot[k][:ng, q4 * 512:(q4 + 1) * 512], in_=ps[b][:ng, :], func=AF.Copy),
                        reads=[("ps", b)], writes=[("ot", k, q4)])
                else:
                    ph.op("dve", lambda e, k=k, ng=ng, q4=q4, b=b: e.tensor_copy(
                        out=ot[k][:ng, q4 * 512:(q4 + 1) * 512], in_=ps[b][:ng, :]),
                        reads=[("ps", b)], writes=[("ot", k, q4)])
            ph.op("sp", lambda e, k=k, ng=ng, t0=t0: e.dma_start(out=out[t0 - 16:t0 - 16 + ng, :], in_=ot[k][:ng, :]),
                  reads=[("ot", k, q4) for q4 in range(KC // 4)], writes=[("outw", gi)], dma=("ot", k))
        ph.emit()


def inproj_phase(nc, cfg, K, ps, hT, Win, vp, qkT, vF_d, vW_d, logf_d, t_lo, t_hi, tiles, name):
    KC = cfg.KC
    offs, NTp = _offs(tiles)
    TM = max(n for _, n in tiles)
    Win_r = Win.rearrange("(c p) f -> p c f", p=128)
    gs = groups_of(t_lo, t_hi)
    ph = Phase(nc, name)
    with ExitStack() as es:
        def sb(nm, shape, dt):
            return es.enter_context(nc.sbuf_tensor(f"{name}_{nm}", shape, dt))
        xnT = sb("xnT", [128, KC, NTp], BF16)
        wb = [sb(f"wb{i}", [128, KC, 128], BF16) for i in range(2)]
        wt = [sb(f"wt{i}", [128, KC, 512], BF16) for i in range(2)]
        qs = [sb(f"qs{i}", [128, TM], F32) for i in range(2)]
        q2 = [sb(f"q2{i}", [128, TM], F32) for i in range(2)]
        rr = [sb(f"rr{i}", [128, TM], F32) for i in range(2)]
        qn = [sb(f"qn{i}", [128, TM], BF16) for i in range(2)]
        vst = [sb(f"vst{i}", [128, 8, 65], BF16) for i in range(2)]
        vsw = [sb(f"vsw{i}", [128, 2, 65], BF16) for i in range(2)]
        fz = [sb(f"fz{i}", [128, 16], F32) for i in range(2)]
        lf = [sb(f"lf{i}", [128, 16], F32) for i in range(2)]

        blocks = []
        for b in range(8):
            blocks.append((b, [(C_FQ + b * 128, 128, 0)], VP_FQN))
        for b in range(8):
            blocks.append((8 + b, [(C_FK + b * 128, 128, 0)], VP_FKN))
        for b in range(8):
            blocks.append((16 + b, [(C_SQ + b * 128, 128, 0)], VP_SQN))
        for g in range(2):
            blocks.append((24 + g, [(C_SK + 64 * g, 64, 0), (C_SK + 64 * g, 64, 64)], VP_SKN))
        ttiles = [(0, [(C_FV, 512, 0)]), (1, [(C_FV + 512, 512, 0)]), (2, [(C_FZ, 16, 0), (C_SV, 128, 16)])]

        def load_b(bi):
            s = bi % 2
            for (c0, ncol, d0) in blocks[bi][1]:
                ph.op("pool", lambda e, s=s, c0=c0, ncol=ncol, d0=d0: e.dma_start(
                    out=wb[s][:, :, d0:d0 + ncol], in_=Win_r[:, :, c0:c0 + ncol]),
                    writes=[("wb", s, d0)], dma=("wb", s))

        def load_t(k):
            s = k % 2
            for (c0, ncol, d0) in ttiles[k][1]:
                ph.op("pool", lambda e, s=s, c0=c0, ncol=ncol, d0=d0: e.dma_start(
                    out=wt[s][:, :, d0:d0 + ncol], in_=Win_r[:, :, c0:c0 + ncol]),
                    writes=[("wt", s, d0)], dma=("wt", s))

        load_b(0)
        load_b(1)
        for i in range(2):
            ph.op("dve", lambda e, i=i: e.memset(vst[i][:, :, :], 1.0), writes=[("vst", i)])
            ph.op("dve", lambda e, i=i: e.memset(vsw[i][:, :, :], 1.0), writes=[("vsw", i)])
        norm_stage(ph, sb, cfg, K, ps, hT, vp[:, VP_GM:VP_GM + KC], tiles, offs, xnT, TM)

        work = [(bi, ti) for bi in range(len(blocks)) for ti in range(len(tiles))]

        def part1(n):
            bi, ti = work[n]
            dst, parts, gcolidx = blocks[bi]
            s = bi % 2
            t0, nn = tiles[ti]
            off = offs[ti]
            q_b = 2 + n % 2
            k = n % 2
            for c in range(KC):
                ph.op("pe", lambda e, c=c: e.matmul(
                    ps[q_b][:, :nn], lhsT=wb[s][:, c, :], rhs=xnT[:, c, off:off + nn], start=(c == 0), stop=(c == KC - 1)),
                    reads=[("wb", s, 0), ("wb", s, 64), ("xn", ti, c)], writes=[("ps", q_b)])
            ph.op("act", lambda e: e.activation(out=qs[k][:, :nn], in_=ps[q_b][:, :nn], func=AF.Copy),
                  reads=[("ps", q_b)], writes=[("qs", k)])
            ph.op("dve", lambda e: e.tensor_tensor(out=q2[k][:, :nn], in0=qs[k][:, :nn], in1=qs[k][:, :nn], op=ALU.mult),
                  reads=[("qs", k)], writes=[("q2", k)])
            if ti == len(tiles) - 1:
                if bi + 2 < len(blocks):
                    load_b(bi + 2)
                elif bi + 2 == len(blocks):
                    load_t(0)
                elif bi + 2 == len(blocks) + 1:
                    load_t(1)

        def part2(n):
            bi, ti = work[n]
            dst, parts, gcolidx = blocks[bi]
            t0, nn = tiles[ti]
            m_b = 4 + n % 2
            k = n % 2
            ph.op("pe", lambda e: e.matmul(ps[m_b][:, :nn], lhsT=K["bd64"], rhs=q2[k][:, :nn], start=True, stop=True),
                  reads=[("q2", k)], writes=[("ps", m_b)])
            ph.op("act", lambda e: e.activation(out=rr[k][:, :nn], in_=ps[m_b][:, :nn], func=AF.Ln, bias=K["eps"], scale=1.0),
                  reads=[("ps", m_b)], writes=[("rr", k)])
            ph.op("act", lambda e: e.activation(out=rr[k][:, :nn], in_=rr[k][:, :nn], func=AF.Exp, scale=-0.5),
                  reads=[("rr", k)], writes=[("rr", k)])
            ph.op("dve", lambda e: e.scalar_tensor_tensor(
                out=qn[k][:, :nn], in0=qs[k][:, :nn], scalar=vp[:, gcolidx:gcolidx + 1], in1=rr[k][:, :nn],
                op0=ALU.mult, op1=ALU.mult),
                reads=[("qs", k), ("rr", k)], writes=[("qn", k)])
            ph.op("sp", lambda e: e.dma_start(out=qkT[dst, :, t0:t0 + nn], in_=qn[k][:, :nn]),
                  reads=[("qn", k)], writes=[("qkT", dst, ti)], dma=("qn", k))

        for n in range(len(work)):
            part1(n)
            if n >= 1:
                part2(n - 1)
        part2(len(work) - 1)

        tb = _rot(2)
        vr, wr, fr = _rot(2), _rot(2), _rot(2)
        for k3, (kk, parts) in enumerate(ttiles):
            s = k3 % 2
            ncols = sum(p[1] for p in parts)
            for (t0, ng, row0, off) in gs:
                t_b = 2 + tb()
                for c in range(KC):
                    ph.op("pe", lambda e, c=c, s=s, ng=ng, off=off, t_b=t_b, ncols=ncols: e.matmul(
                        ps[t_b][:ng, :ncols], lhsT=xnT[:, c, off:off + ng], rhs=wt[s][:, c, :ncols],
                        start=(c == 0), stop=(c == KC - 1)),
                        reads=[("wt", s, 0), ("wt", s, 16)] + [("xn", ti, c) for ti in range(len(tiles))],
                        writes=[("ps", t_b)])
                if kk < 2:
                    v = vr()
                    ph.op("act", lambda e, v=v, ng=ng, t_b=t_b: e.activation(
                        out=vst[v][:ng, :, 0:64], in_=ps[t_b][:ng, :].rearrange("p (h d) -> p h d", h=8), func=AF.Copy),
                        reads=[("ps", t_b), ("vst", v)], writes=[("vst", v)])
                    ph.op("sp", lambda e, v=v, ng=ng, row0=row0, kk=kk: e.dma_start(
                        out=vF_d[row0:row0 + ng, kk * 520:(kk + 1) * 520], in_=vst[v][:ng, :, :].rearrange("p h d -> p (h d)")),
                        reads=[("vst", v)], writes=[("vF_d", row0, kk)], dma=("vst", v))
                else:
                    w = wr()
                    f = fr()
                    ph.op("dve", lambda e, f=f, ng=ng, t_b=t_b: e.tensor_tensor(
                        out=fz[f][:ng, :], in0=ps[t_b][:ng, 0:16], in1=vp[:ng, VP_BF:VP_BF + 16], op=ALU.add),
                        reads=[("ps", t_b)], writes=[("fz", f)])
                    ph.op("act", lambda e, w=w, ng=ng, t_b=t_b: e.activation(
                        out=vsw[w][:ng, :, 0:64], in_=ps[t_b][:ng, 16:144].rearrange("p (h d) -> p h d", h=2), func=AF.Copy),
                        reads=[("ps", t_b), ("vsw", w)], writes=[("vsw", w)])
                    ph.op("act", lambda e, f=f, ng=ng: e.activation(out=fz[f][:ng, :], in_=fz[f][:ng, :], func=AF.Exp, scale=-1.0),
                          reads=[("fz", f)], writes=[("fz", f)])
                    ph.op("act", lambda e, f=f, ng=ng: e.activation(out=fz[f][:ng, :], in_=fz[f][:ng, :], func=AF.Ln,
                                                                 bias=K["one"][:ng, :], scale=1.0),
                          reads=[("fz", f)], writes=[("fz", f)])
                    ph.op("dve", lambda e, f=f, ng=ng: e.tensor_scalar(out=lf[f][:ng, :], in0=fz[f][:ng, :], scalar1=-1.0,
                                                                    scalar2=None, op0=ALU.mult),
                          reads=[("fz", f)], writes=[("lf", f)])
                    ph.op("sp", lambda e, f=f, ng=ng, row0=row0: e.dma_start(out=logf_d[row0:row0 + ng, :], in_=lf[f][:ng, :]),
                          reads=[("lf", f)], writes=[("logf_d", row0)], dma=("lf", f))
                    ph.op("sp", lambda e, w=w, ng=ng, row0=row0: e.dma_start(
                        out=vW_d[row0:row0 + ng, :], in_=vsw[w][:ng, :, :].rearrange("p h d -> p (h d)")),
                        reads=[("vsw", w)], writes=[("vW_d", row0)], dma=("vsw", w))
            if k3 + 2 < len(ttiles):
                load_t(k3 + 2)
        ph.emit()


def decay_phase(nc, cfg, K, ps, logf_d, Cb_hm, c_hm, name):
    ph = Phase(nc, name)
    with ExitStack() as es:
        L = es.enter_context(nc.sbuf_tensor(f"{name}_L", [128, NBLK, 16], F32))
        ph.op("dve", lambda e: e.memset(L[:, :, :], 0.0), writes=["L"])
        ph.op("sp", lambda e: e.dma_start(out=L[0:16, 0, :], in_=logf_d[0:16, :]), reads=["L"], writes=["L0"], dma="L0")
        ph.op("sp", lambda e: e.dma_start(out=L[:, 1:NBLK, :], in_=logf_d[128:NROW, :].rearrange("(b p) h -> p b h", p=128)),
              reads=["L"], writes=["L1"], dma="L1")
        Lf = L[:, :, :].rearrange("p b h -> p (b h)")
        ph.op("pe", lambda e: e.matmul(ps[0][:, 0:NBLK * 16], lhsT=K["ones32"], rhs=Lf, start=True, stop=True),
              reads=["L0", "L1"], writes=[("ps", 0)])
        ph.op("pe", lambda e: e.matmul(ps[1][:, 0:NBLK * 16], lhsT=K["tri32"], rhs=Lf, start=True, stop=True),
              reads=["L0", "L1"], writes=[("ps", 1)])
        ph.op("dve", lambda e: e.memset(Cb_hm[:, :, :], 0.0), writes=["Cb"])
        for i in range(1, NBLK):
            ph.op("dve", lambda e, i=i: e.tensor_tensor(out=Cb_hm[:, :, i], in0=Cb_hm[:, :, i - 1],
                                                     in1=ps[0][:, (i - 1) * 16:i * 16], op=ALU.add),
                  reads=["Cb", ("ps", 0)], writes=["Cb"])
        ph.op("dve", lambda e: e.tensor_tensor(out=c_hm[:, :, :], in0=ps[1][:, 0:NBLK * 16].rearrange("p (j h) -> p h j", h=16),
                                            in1=Cb_hm[:, :, :], op=ALU.add),
              reads=["Cb", ("ps", 1)], writes=["chm"])
        ph.emit()


QCHUNKS = [[0], [1, 2, 3, 4], [5, 6, 7, 8], [9, 10, 11, 12], [13, 14, 15, 16]]


def _load_tokmajor(ph, dst, src_d, width, key):
    ph.op("sp", lambda e: e.dma_start(out=dst[0:16, 0, :], in_=src_d[0:16, :]), writes=[(key, 0)], dma=(key, 0))
    ph.op("sp", lambda e: e.dma_start(out=dst[:, 1:NBLK, :], in_=src_d[128:NROW, :].rearrange("(b p) c -> p b c", p=128)),
          writes=[(key, 1)], dma=(key, 1))


def _store_ost(ph, ost_k, k, Oscr, col0):
    ph.op("sp", lambda e: e.dma_start(out=Oscr[0:16, col0:col0 + 128], in_=ost_k[0:16, 0, :]),
          reads=[("ost", k, i) for i in range(NBLK)], writes=[("Oscr", col0, 0)], dma=("ost", k))
    ph.op("sp", lambda e: e.dma_start(out=Oscr[128:NROW, col0:col0 + 128].rearrange("(b p) c -> p b c", p=128),
                                     in_=ost_k[:, 1:NBLK, :]),
          reads=[("ost", k, i) for i in range(NBLK)], writes=[("Oscr", col0, 1)], dma=("ost", k))


def fox_phase(nc, cfg, K, ps, qkT, vF_d, Cb_hm, c_hm, Oscr, name, npairs=8):
    NT = cfg.NT
    ph = Phase(nc, name)
    with ExitStack() as es:
        def sb(nm, shape, dt):
            return es.enter_context(nc.sbuf_tensor(f"{name}_{nm}", shape, dt))
        vF = sb("vF", [128, NBLK, NFH * 65], BF16)
        qT = [sb(f"qT{i}", [128, NT], BF16) for i in range(2)]
        kT = [sb(f"kT{i}", [128, NT], BF16) for i in range(2)]
        NB = [sb(f"NB{i}", [128, NBLK, NBLK], F32) for i in range(2)]
        pt = [sb(f"pt{i}", [128, 128], BF16) for i in range(8)]
        rec = [sb(f"rec{i}", [128, 4], F32) for i in range(4)]
        ost = [sb(f"ost{i}", [128, NBLK, 128], F32) for i in range(2)]
        _load_tokmajor(ph, vF, vF_d, NFH * 65, "vF")

        def load_pair(hp):
            k = hp % 2
            ph.op("sp", lambda e: e.dma_start(out=qT[k][:, :], in_=qkT[hp, :, :]), writes=[("qT", k)], dma=("qT", k))
            ph.op("sp", lambda e: e.dma_start(out=kT[k][:, :], in_=qkT[8 + hp, :, :]), writes=[("kT", k)], dma=("kT", k))

        def make_nb(h):
            nb = h % 2
            ph.op("dve", lambda e: e.tensor_tensor(
                out=NB[nb][:, :, :], in0=Cb_hm[:, h, :].unsqueeze(1).to_broadcast([128, NBLK, NBLK]),
                in1=c_hm[:, h, :].unsqueeze(2).to_broadcast([128, NBLK, NBLK]), op=ALU.subtract),
                writes=[("NB", nb)])

        items = []
        for hp in range(npairs):
            for hh in range(2):
                for ci, qc in enumerate(QCHUNKS):
                    for j in range(0, qc[-1] + 1):
                        items.append((hp, hh, ci, qc, j))
        ptr, rcr = _rot(8), _rot(4)
        NS = 3

        def emit_st(n):
            hp, hh, ci, qc, j = items[n]
            k, pb = hp % 2, 64 * hh
            if hh == 0 and ci == 0 and j == 0:
                if hp == 0:
                    load_pair(0)
                if hp + 1 < npairs:
                    load_pair(hp + 1)
            if ci == 0 and j == 0:
                make_nb(2 * hp + hh)
            iv = [i for i in qc if i >= j]
            k0, nk = blk_cols(j)
            c_lo = blk_cols(iv[0])[0]
            c_hi = blk_cols(iv[-1])[0] + blk_cols(iv[-1])[1]
            s_b = n % NS
            ph.op("pe", lambda e: e.matmul(
                ps[s_b][:nk, 0:c_hi - c_lo], lhsT=kT[k][pb:pb + 64, k0:k0 + nk], rhs=qT[k][pb:pb + 64, c_lo:c_hi],
                start=True, stop=True),
                reads=[("qT", k), ("kT", k)], writes=[("ps", s_b)])

        def emit_rest(n):
            hp, hh, ci, qc, j = items[n]
            h = 2 * hp + hh
            k, nb = hp % 2, h % 2
            o_b = 4 + (n_chunk[n] % 2)
            iv = [i for i in qc if i >= j]
            k0, nk = blk_cols(j)
            c_lo = blk_cols(iv[0])[0]
            s_b = n % NS
            for i in iv:
                q0, nq = blk_cols(i)
                a = qc.index(i)
                p = ptr()
                first = (j == 0 and i == qc[0])
                ph.op("act", lambda e, p=p, nq=nq, q0=q0, i=i: e.activation(
                    out=pt[p][:nk, :nq], in_=ps[s_b][:nk, q0 - c_lo:q0 - c_lo + nq], func=AF.Exp,
                    bias=NB[nb][:nk, j, i:i + 1], scale=0.125),
                    reads=[("ps", s_b), ("NB", nb)], writes=[("pt", p)])
                if i == j:
                    ph.op("pool", lambda e, p=p, nq=nq: e.tensor_tensor(
                        out=pt[p][:nk, :nq], in0=pt[p][:nk, :nq], in1=K["trimask"][:nk, :nq], op=ALU.mult),
                        reads=[("pt", p)], writes=[("pt", p)])
                ph.op("pe", lambda e, p=p, nq=nq, a=a, i=i, first=first: e.matmul(
                    ps[o_b][:nq, a * 65:(a + 1) * 65], lhsT=pt[p][:nk, :nq], rhs=vF[:nk, j, h * 65:(h + 1) * 65],
                    start=first, stop=(j == i), skip_group_check=True),
                    reads=[("pt", p), ("vF", 0), ("vF", 1)], writes=[("ps", o_b)])
            if j == qc[-1]:
                nqm = blk_cols(qc[0])[1]
                na = len(qc)
                r = rcr()
                ob3 = ps[o_b][:nqm, 0:na * 65].rearrange("p (a c) -> p a c", c=65)
                ph.op("dve", lambda e, r=r: e.reciprocal(out=rec[r][:nqm, 0:na], in_=ob3[:, :, 64]),
                      reads=[("ps", o_b)], writes=[("rec", r)])
                ph.op("dve", lambda e, r=r: e.tensor_tensor(
                    out=ost[k][:nqm, qc[0]:qc[0] + na, hh * 64:(hh + 1) * 64], in0=ob3[:, :, 0:64],
                    in1=rec[r][:nqm, 0:na].unsqueeze(2).to_broadcast([nqm, na, 64]), op=ALU.mult),
                    reads=[("ps", o_b), ("rec", r)], writes=[("ost", k, i) for i in qc])
                if hh == 1 and ci == len(QCHUNKS) - 1:
                    _store_ost(ph, ost[k], k, Oscr, hp * 128)

        n_chunk = []
        cc = -1
        for (hp, hh, ci, qc, j) in items:
            if j == 0:
                cc += 1
            n_chunk.append(cc)
        LA = 2
        for n in range(min(LA, len(items))):
            emit_st(n)
        for n in range(len(items)):
            if n + LA < len(items):
                emit_st(n + LA)
            emit_rest(n)
        ph.emit()


def swa_phase(nc, cfg, K, ps, qkT, vW_d, vp, tab2_d, tab0_d, Oscr, name, npairs=8):
    NT = cfg.NT
    ph = Phase(nc, name)
    with ExitStack() as es:
        def sb(nm, shape, dt):
            return es.enter_context(nc.sbuf_tensor(f"{name}_{nm}", shape, dt))
        vW = sb("vW", [128, NBLK, 2 * 65], BF16)
        qT = [sb(f"qT{i}", [128, NT], BF16) for i in range(2)]
        kd = [sb(f"kd{i}", [128, NT], BF16) for i in range(2)]
        tab2 = sb("tab2", [128, NSH, 256], F32)
        tab0 = sb("tab0", [128, NSH, 144], F32)
        esink = sb("esink", [128, NSH], F32)
        tt = [sb(f"tt{i}", [128, 256], F32) for i in range(3)]
        pt = [sb(f"pt{i}", [128, 256], BF16) for i in range(3)]
        den = [sb(f"den{i}", [128, 4], F32) for i in range(4)]
        ost = [sb(f"ost{i}", [128, NBLK, 128], F32) for i in range(2)]
        _load_tokmajor(ph, vW, vW_d, 2 * 65, "vW")
        ph.op("sp", lambda e: e.dma_start(out=tab2[:, :, :], in_=tab2_d), writes=["tab2"], dma="tab2")
        ph.op("sp", lambda e: e.dma_start(out=tab0[:, :, :], in_=tab0_d), writes=["tab0"], dma="tab0")
        ph.op("act", lambda e: e.activation(out=esink[:, :], in_=vp[:, VP_SINK:VP_SINK + NSH], func=AF.Exp), writes=["esink"])

        def load_pair(hp):
            k = hp % 2
            ph.op("sp", lambda e: e.dma_start(out=qT[k][:, :], in_=qkT[16 + hp, :, :]), writes=[("qT", k)], dma=("qT", k))

        def load_kd(g):
            ph.op("sp", lambda e: e.dma_start(out=kd[g][:, :], in_=qkT[24 + g, :, :]), writes=[("kd", g)], dma=("kd", g))

        load_kd(0)
        load_kd(1)
        load_pair(0)
        ttr, ptr, dnr = _rot(3), _rot(3), _rot(4)
        items = [(hp, hh, j) for hp in range(npairs) for hh in range(2) for j in range(NBLK)]
        NS = 3

        def obank(i):
            return 4 + ((i // 4) % 2), i % 4

        def geom(j):
            iv = [i for i in (j, j + 1) if i < NBLK]
            k0, nk = blk_cols(j)
            c_lo = blk_cols(iv[0])[0]
            c_hi = blk_cols(iv[-1])[0] + blk_cols(iv[-1])[1]
            return iv, k0, nk, c_lo, c_hi

        def emit_st(n):
            hp, hh, j = items[n]
            k, pb, g = hp % 2, 64 * hh, (2 * hp + hh) // 8
            if hh == 0 and j == 0 and hp + 1 < npairs:
                load_pair(hp + 1)
            iv, k0, nk, c_lo, c_hi = geom(j)
            s_b = n % NS
            ph.op("pe", lambda e: e.matmul(
                ps[s_b][:nk, 0:c_hi - c_lo], lhsT=kd[g][pb:pb + 64, k0:k0 + nk], rhs=qT[k][pb:pb + 64, c_lo:c_hi],
                start=True, stop=True),
                reads=[("qT", k), ("kd", g)], writes=[("ps", s_b)])

        def emit_rest(n):
            hp, hh, j = items[n]
            h = 2 * hp + hh
            k, g = hp % 2, h // 8
            iv, k0, nk, c_lo, c_hi = geom(j)
            ncol = c_hi - c_lo
            s_b = n % NS
            t = ttr()
            p = ptr()
            tab = tab0 if j == 0 else tab2
            ph.op("dve", lambda e: e.scalar_tensor_tensor(
                out=tt[t][:nk, :ncol], in0=ps[s_b][:nk, :ncol], scalar=0.125, in1=tab[:nk, h, :ncol],
                op0=ALU.mult, op1=ALU.add),
                reads=[("ps", s_b), "tab2", "tab0"], writes=[("tt", t)])
            ph.op("act", lambda e: e.activation(out=pt[p][:nk, :ncol], in_=tt[t][:nk, :ncol], func=AF.Exp),
                  reads=[("tt", t)], writes=[("pt", p)])
            for i in iv:
                q0, nq = blk_cols(i)
                o_b, a = obank(i)
                ph.op("pe", lambda e, nq=nq, q0=q0, o_b=o_b, a=a, st=(j == max(i - 1, 0)), i=i: e.matmul(
                    ps[o_b][:nq, a * 65:(a + 1) * 65], lhsT=pt[p][:nk, q0 - c_lo:q0 - c_lo + nq],
                    rhs=vW[:nk, j, g * 65:(g + 1) * 65], start=st, stop=(j == i), skip_group_check=True),
                    reads=[("pt", p), ("vW", 0), ("vW", 1)], writes=[("ps", o_b)])
            pending.append((n + DELAY, hp, hh, j))
            while pending and (pending[0][0] <= n or j == NBLK - 1):
                _, hp_f, hh_f, j_f = pending.pop(0)
                finalize(hp_f, hh_f, j_f)

        def finalize(hp, hh, j):
            h = 2 * hp + hh
            k = hp % 2
            if j % 4 == 3 or j == NBLK - 1:
                blks = [i for i in range(NBLK) if (i // 4) == (j // 4)]
                na = len(blks)
                o_b = obank(blks[0])[0]
                d = dnr()
                ob3 = ps[o_b][:, 0:na * 65].rearrange("p (a c) -> p a c", c=65)
                ph.op("dve", lambda e, d=d: e.tensor_scalar(out=den[d][:, 0:na], in0=ob3[:, :, 64], scalar1=esink[:, h:h + 1],
                                                          scalar2=None, op0=ALU.add),
                      reads=[("ps", o_b), "esink"], writes=[("den", d)])
                ph.op("dve", lambda e, d=d: e.reciprocal(out=den[d][:, 0:na], in_=den[d][:, 0:na]),
                      reads=[("den", d)], writes=[("den", d)])
                ph.op("dve", lambda e, d=d: e.tensor_tensor(
                    out=ost[k][:, blks[0]:blks[0] + na, hh * 64:(hh + 1) * 64], in0=ob3[:, :, 0:64],
                    in1=den[d][:, 0:na].unsqueeze(2).to_broadcast([128, na, 64]), op=ALU.mult),
                    reads=[("ps", o_b), ("den", d)], writes=[("ost", k, i) for i in blks])
            if hh == 1 and j == NBLK - 1:
                _store_ost(ph, ost[k], k, Oscr, 1024 + hp * 128)

        LA = 2
        DELAY = 2
        pending = []
        for n in range(min(LA, len(items))):
            emit_st(n)
        for n in range(len(items)):
            if n + LA < len(items):
                emit_st(n + LA)
            emit_rest(n)
        while pending:
            _, hp_f, hh_f, j_f = pending.pop(0)
            finalize(hp_f, hh_f, j_f)
        ph.emit()


def outproj_phase(nc, cfg, K, ps, hT, Wout, gout_d, Oscr, t_lo, t_hi, tiles, name):
    KC = cfg.KC
    offs, NTp = _offs(tiles)
    TM = max(n for _, n in tiles)
    Wo_r = Wout.rearrange("(c p) d -> p c d", p=128)
    gs = groups_of(t_lo, t_hi)
    ph = Phase(nc, name)
    with ExitStack() as es:
        def sb(nm, shape, dt):
            return es.enter_context(nc.sbuf_tensor(f"{name}_{nm}", shape, dt))
        onT = sb("onT", [128, KC, NTp], BF16)
        Ob = [sb(f"Ob{i}", [128, 2048], F32) for i in range(2)]
        on = [sb(f"on{i}", [128, 2048], BF16) for i in range(2)]
        junk = sb("junk", [128, 1024], BF16)
        ssq = [sb(f"ssq{i}", [128, 2], F32) for i in range(2)]
        wd = [sb(f"wd{i}", [128, KC, 128], BF16) for i in range(2)]
        gout = sb("gout", [128, 2048], F32)
        ph.op("sp", lambda e: e.dma_start(out=gout[:, :], in_=gout_d), writes=["gout"], dma="gout")
        for dc in range(2):
            ph.op("pool", lambda e, dc=dc: e.dma_start(out=wd[dc][:, :, :], in_=Wo_r[:, :, dc * 128:(dc + 1) * 128]),
                  writes=[("wd", dc)], dma=("wd", dc))
        tbr = _rot(2)

        def load_o(gi):
            t0, ng, row0, off = gs[gi]
            k = gi % 2
            ph.op("sp", lambda e: e.dma_start(out=Ob[k][:ng, :], in_=Oscr[row0:row0 + ng, :]), writes=[("Ob", k)], dma=("Ob", k))

        def front(gi):
            t0, ng, row0, off = gs[gi]
            k = gi % 2
            ph.op("dve", lambda e: e.memset(ssq[k][:, :], 0.0), writes=[("ssq", k)])
            for hf in range(2):
                ph.op("act", lambda e, hf=hf: e.activation(
                    out=junk[:ng, :], in_=Ob[k][:ng, hf * 1024:(hf + 1) * 1024], func=AF.Square,
                    accum_out=ssq[k][:ng, hf:hf + 1]),
                    reads=[("Ob", k), ("ssq", k)], writes=[("ssq", k), "junk"])
            ph.op("act", lambda e: e.activation(out=ssq[k][:ng, :], in_=ssq[k][:ng, :], func=AF.Ln,
                                                bias=K["eps"][:ng, :], scale=1.0 / 1024.0),
                  reads=[("ssq", k)], writes=[("ssq", k)])
            ph.op("act", lambda e: e.activation(out=ssq[k][:ng, :], in_=ssq[k][:ng, :], func=AF.Exp, scale=-0.5),
                  reads=[("ssq", k)], writes=[("ssq", k)])

        def back(gi):
            t0, ng, row0, off = gs[gi]
            k = gi % 2
            for hf in range(2):
                ph.op("dve", lambda e, hf=hf: e.scalar_tensor_tensor(
                    out=on[k][:ng, hf * 1024:(hf + 1) * 1024], in0=Ob[k][:ng, hf * 1024:(hf + 1) * 1024],
                    scalar=ssq[k][:ng, hf:hf + 1], in1=gout[:ng, hf * 1024:(hf + 1) * 1024],
                    op0=ALU.mult, op1=ALU.mult),
                    reads=[("Ob", k), ("ssq", k), "gout"], writes=[("on", k, hf)])
            for hf in range(2):
                t_b = 2 + tbr()
                pbf = ps[t_b][:, :].bitcast(BF16)
                for a in range(8):
                    f = hf * 8 + a
                    ph.op("pe", lambda e, f=f, a=a, pbf=pbf: e.transpose(
                        pbf[:, a * 128:a * 128 + ng], on[k][:ng, f * 128:(f + 1) * 128], K["idbf"][:ng, :ng]),
                        reads=[("on", k, hf)], writes=[("ps", t_b)])
                if hf == 0:
                    ph.op("act", lambda e, pbf=pbf, hf=hf: e.activation(
                        out=onT[:, hf * 8:hf * 8 + 8, off:off + ng], in_=pbf.rearrange("p (a t) -> p a t", a=8)[:, :, :ng],
                        func=AF.Copy), reads=[("ps", t_b)], writes=[("onT", gi, hf)])
                else:
                    ph.op("dve", lambda e, pbf=pbf, hf=hf: e.tensor_copy(
                        out=onT[:, hf * 8:hf * 8 + 8, off:off + ng], in_=pbf.rearrange("p (a t) -> p a t", a=8)[:, :, :ng]),
                        reads=[("ps", t_b)], writes=[("onT", gi, hf)])

        load_o(0)
        if len(gs) > 1:
            load_o(1)
        front(0)
        for gi in range(len(gs)):
            if gi + 1 < len(gs):
                front(gi + 1)
            back(gi)
            if gi + 2 < len(gs):
                load_o(gi + 2)

        def src_res(fc, ti):
            return ("onTall", fc // 8)
        for hf in range(2):
            ph.op("dve" if hf else "act", (lambda e: e.tensor_copy(out=junk[0:1, 0:2], in_=junk[0:1, 2:4])) if hf else
                  (lambda e: e.activation(out=junk[0:1, 4:6], in_=junk[0:1, 6:8], func=AF.Copy)),
                  reads=[("onT", gi, hf) for gi in range(len(gs))] + ["junk"], writes=[("onTall", hf)])
        down_stage(ph, sb, cfg, ps, hT, Wo_r, KC, onT, src_res, tiles, offs, TM, 1.0, pre_loaded=2, wd=wd)
        ph.emit()


HALVES = [
    (0, 1040, [(0, 344), (344, 344), (688, 352)]),
    (1040, 2064, [(1040, 344), (1384, 344), (1728, 336)]),
]
WNAMES = ["ffn1_w_gate", "ffn1_w_up", "ffn1_w_down", "w_in", "w_out", "ffn2_w_gate", "ffn2_w_up", "ffn2_w_down"]


def build_program(depth=4, debug=False, stages=None):
    cfg = Cfg()
    nc = bass.Bass("TRN2", target_bir_lowering=False)
    D, DFF, NT = cfg.D, cfg.DFF, cfg.NT

    def din(name, shape, dt=F32):
        return nc.dram_tensor(name, shape, dt, kind="ExternalInput").ap()

    def dscr(name, shape, dt):
        return nc.dram_tensor(name, shape, dt, kind="ExternalOutput" if debug else "Internal").ap()

    x = din("x", [2048, D])
    meta = din("meta_tokens", [16, D])
    W = {
        "ffn1_w_gate": din("ffn1_w_gate", [depth, D, DFF]), "ffn1_w_up": din("ffn1_w_up", [depth, D, DFF]),
        "ffn1_w_down": din("ffn1_w_down", [depth, DFF, D]), "w_in": din("w_in", [depth, D, INW]),
        "w_out": din("w_out", [depth, D, D]), "ffn2_w_gate": din("ffn2_w_gate", [depth, D, DFF]),
        "ffn2_w_up": din("ffn2_w_up", [depth, D, DFF]), "ffn2_w_down": din("ffn2_w_down", [depth, DFF, D]),
    }
    vpack = din("vpack", [depth, 128, VP_W])
    goutp = din("goutp", [depth, 128, 2048])
    ck_d = din("ck", [128, CK_W])
    cbf_d = din("cbf", [128, 256], BF16)
    tab2_d = din("tab2", [128, NSH, 256])
    tab0_d = din("tab0", [128, NSH, 144])
    out = nc.dram_tensor("out", [2048, D], F32, kind="ExternalOutput").ap()
    hT = dscr("hT", [cfg.KC, 128, NT], F32)
    qkT = dscr("qkT", [26, 128, NT], BF16)
    vF_d = dscr("vF_d", [NROW, NFH * 65], BF16)
    vW_d = dscr("vW_d", [NROW, 2 * 65], BF16)
    logf_d = dscr("logf_d", [NROW, 16], F32)
    Oscr = dscr("Oscr", [NROW, 2048], F32)

    with ExitStack() as es:
        ps = [es.enter_context(nc.psum_tensor(f"ps{i}", [128, 512], F32)) for i in range(8)]
        ck = es.enter_context(nc.sbuf_tensor("ck_sb", [128, CK_W], F32))
        cbf = es.enter_context(nc.sbuf_tensor("cbf_sb", [128, 256], BF16))
        vp = es.enter_context(nc.sbuf_tensor("vp_sb", [128, VP_W], F32))
        Cb_hm = es.enter_context(nc.sbuf_tensor("Cb_hm", [128, NFH, NBLK], F32))
        c_hm = es.enter_context(nc.sbuf_tensor("c_hm", [128, NFH, NBLK], F32))
        K = {
            "invD": ck[:, CK_INVD:CK_INVD + 128], "bd64": ck[:, CK_BD64:CK_BD64 + 128],
            "ones32": ck[:, CK_ONES:CK_ONES + 128], "tri32": ck[:, CK_TRI:CK_TRI + 128],
            "id32": ck[:, CK_ID:CK_ID + 128], "eps": ck[:, CK_EPS:CK_EPS + 1], "one": ck[:, CK_ONE:CK_ONE + 1],
            "idbf": cbf[:, 0:128], "trimask": cbf[:, 128:256],
        }
        ph = Phase(nc, "c0")
        ph.op("sp", lambda e: e.dma_start(out=ck[:, :], in_=ck_d), writes=["ck"], dma="ck")
        ph.op("sp", lambda e: e.dma_start(out=cbf[:, :], in_=cbf_d), writes=["cbf"], dma="cbf")
        ph.emit()
        init_phase(nc, cfg, K, ps, x, meta, hT, "ini")
        for l in range(depth):
            ph = Phase(nc, f"v{l}")
            ph.op("sp", lambda e, l=l: e.dma_start(out=vp[:, :], in_=vpack[l]), writes=["vp"], dma="vp")
            ph.emit()
            if stages is None or "ffn1" in stages:
                for hi, (lo, hi_, tiles) in enumerate(HALVES):
                    ffn_phase(nc, cfg, K, ps, hT, W["ffn1_w_gate"][l], W["ffn1_w_up"][l], W["ffn1_w_down"][l],
                              vp[:, VP_G1:VP_G1 + cfg.KC], tiles, f"a{l}{hi}")
            def on(st):
                return stages is None or "mix" in stages or st in stages
            if on("inproj"):
                for hi, (lo, hi_, tiles) in enumerate(HALVES):
                    inproj_phase(nc, cfg, K, ps, hT, W["w_in"][l], vp, qkT, vF_d, vW_d, logf_d, lo, hi_, tiles, f"i{l}{hi}")
            if on("decay"):
                decay_phase(nc, cfg, K, ps, logf_d, Cb_hm, c_hm, f"d{l}")
            if on("fox"):
                fox_phase(nc, cfg, K, ps, qkT, vF_d, Cb_hm, c_hm, Oscr, f"x{l}")
            if on("swa"):
                swa_phase(nc, cfg, K, ps, qkT, vW_d, vp, tab2_d, tab0_d, Oscr, f"w{l}")
            if on("outproj"):
                for hi, (lo, hi_, tiles) in enumerate(HALVES):
                    outproj_phase(nc, cfg, K, ps, hT, W["w_out"][l], goutp[l], Oscr, lo, hi_, tiles, f"o{l}{hi}")
            if stages is None or "ffn2" in stages:
                for hi, (lo, hi_, tiles) in enumerate(HALVES):
                    ffn_phase(nc, cfg, K, ps, hT, W["ffn2_w_gate"][l], W["ffn2_w_up"][l], W["ffn2_w_down"][l],
                              vp[:, VP_G2:VP_G2 + cfg.KC], tiles, f"b{l}{hi}")
        final_phase(nc, cfg, K, ps, hT, out, "fin")
    return nc


def host_consts():
    import ml_dtypes
    ck = np.zeros((128, CK_W), np.float32)
    ck[:, CK_INVD:CK_INVD + 128] = 1.0 / 2048.0
    bd = np.zeros((128, 128), np.float32)
    bd[:64, :64] = 1.0 / 64.0
    bd[64:, 64:] = 1.0 / 64.0
    ck[:, CK_BD64:CK_BD64 + 128] = bd
    ck[:, CK_ONES:CK_ONES + 128] = 1.0
    p = np.arange(128)
    tri = (p[:, None] <= p[None, :]).astype(np.float32)
    ck[:, CK_TRI:CK_TRI + 128] = tri
    ck[:, CK_ID:CK_ID + 128] = np.eye(128, dtype=np.float32)
    ck[:, CK_EPS] = 1e-6
    ck[:, CK_ONE] = 1.0
    cbf = np.zeros((128, 256), np.float32)
    cbf[:, 0:128] = np.eye(128)
    cbf[:, 128:256] = tri
    cbf = cbf.astype(ml_dtypes.bfloat16)
    slopes = (2.0 ** (-8.0 * np.arange(1, NSH + 1) / NSH)).astype(np.float64)
    NEG = -30000.0
    k = np.arange(128)[:, None]
    c = np.arange(256)[None, :]
    dist = np.where(c < 128, c - k, 128 + (c - 128) - k)
    ok = np.where(c < 128, c >= k, k > (c - 128))
    tab2 = np.where(ok[:, None, :], -slopes[None, :, None] * dist[:, None, :], NEG).astype(np.float32)
    c0 = np.arange(144)[None, :]
    dist0 = np.where(c0 < 16, c0 - k, 16 + (c0 - 16) - k)
    ok0 = np.where(c0 < 16, c0 >= k, dist0 < 128) & (k < 16)
    tab0 = np.where(ok0[:, None, :], -slopes[None, :, None] * dist0[:, None, :], NEG).astype(np.float32)
    return ck, cbf, np.ascontiguousarray(tab2), np.ascontiguousarray(tab0)


def host_vpack(inp, depth):
    vp = np.zeros((depth, 128, VP_W), np.float32)
    for l in range(depth):
        vp[l, :, VP_G1:VP_G1 + 16] = inp["ffn1_norm"][l].reshape(16, 128).T
        vp[l, :, VP_GM:VP_GM + 16] = inp["mix_norm"][l].reshape(16, 128).T
        vp[l, :, VP_G2:VP_G2 + 16] = inp["ffn2_norm"][l].reshape(16, 128).T
        vp[l, :, VP_FQN] = np.tile(inp["fox_q_norm"][l], 2)
        vp[l, :, VP_FKN] = np.tile(inp["fox_k_norm"][l], 2)
        vp[l, :, VP_SQN] = np.tile(inp["swa_q_norm"][l], 2)
        vp[l, :, VP_SKN] = np.tile(inp["swa_k_norm"][l], 2)
        vp[l, :, VP_BF:VP_BF + 16] = inp["b_forget"][l][None, :]
        vp[l, :, VP_SINK:VP_SINK + 16] = inp["swa_sinks"][l][None, :]
    return vp


def host_gout(inp, depth):
    g = np.zeros((depth, 128, 2048), np.float32)
    for l in range(depth):
        g[l, :, 0:1024] = inp["fox_out_norm"][l][None, :]
        g[l, :, 1024:2048] = inp["swa_out_norm"][l][None, :]
    return g


_NC_CACHE = {}


def run(inputs, depth=4, debug=False, stages=None, ncores=8, trace=False):
    inp = {k: np.asarray(v) for k, v in inputs.items()}
    key = (depth, debug, None if stages is None else tuple(stages))
    if key not in _NC_CACHE:
        _NC_CACHE[key] = build_program(depth, debug, stages)
    nc = _NC_CACHE[key]
    ck, cbf, tab2, tab0 = host_consts()
    vp = host_vpack(inp, depth)
    shared = {"meta_tokens": np.ascontiguousarray(inp["meta_tokens"], dtype=np.float32), "vpack": vp, "ck": ck, "cbf": cbf,
              "goutp": host_gout(inp, depth),
              "tab2": tab2, "tab0": tab0}
    for w in WNAMES:
        shared[w] = np.ascontiguousarray(inp[w][:depth], dtype=np.float32)
    in_maps = []
    for b in range(ncores):
        m = dict(shared)
        m["x"] = np.ascontiguousarray(inp["x"][b], dtype=np.float32)
        in_maps.append(m)
    res = run_bass_kernel_spmd(nc, in_maps, core_ids=list(range(ncores)), trace=trace)
    return res


def kernel(**inputs):
    res = run(inputs, depth=4)
    out = np.stack([np.asarray(r["out"], dtype=np.float32) for r in res.results], axis=0)
    return out
```

```python
import numpy as np
from contextlib import ExitStack

import concourse.bass as bass
import concourse.mybir as mybir
from concourse.bass_utils import run_bass_kernel_spmd

F32 = mybir.dt.float32
BF16 = mybir.dt.bfloat16
AF = mybir.ActivationFunctionType
ALU = mybir.AluOpType
AX = mybir.AxisListType

ENGS = ("pe", "act", "dve", "pool", "sp")


class Op:
    __slots__ = ("eng", "fn", "deps", "dma_key", "signal", "seq", "dma_target", "idx")


class Phase:
    def __init__(self, nc, name):
        self.nc = nc
        self.name = name
        self.ops = []
        self.last_w = {}
        self.readers = {}

    def op(self, eng, fn, reads=(), writes=(), dma=None):
        o = Op()
        o.eng = eng
        o.fn = fn
        o.dma_key = dma
        o.signal = False
        o.seq = 0
        o.dma_target = 0
        o.idx = len(self.ops)
        deps = set()
        for r in reads:
            w = self.last_w.get(r)
            if w is not None:
                deps.add(w)
            if isinstance(r, tuple) and r[0] == "ps":
                for rd in self.readers.get(r, ()):
                    if self.ops[rd].eng != eng:
                        deps.add(rd)
        for w_ in writes:
            w = self.last_w.get(w_)
            if w is not None:
                deps.add(w)
            for rd in self.readers.get(w_, ()):
                deps.add(rd)
        deps.discard(o.idx)
        o.deps = deps
        for r in reads:
            self.readers.setdefault(r, []).append(o.idx)
        for w_ in writes:
            self.last_w[w_] = o.idx
            self.readers[w_] = []
        self.ops.append(o)
        return o

    def emit(self):
        nc = self.nc
        ops = self.ops
        for o in ops:
            nd = set()
            for d in o.deps:
                p = ops[d]
                if p.dma_key is None and o.dma_key is None and p.eng == "pe" and o.eng == "pe":
                    continue
                nd.add(d)
            o.deps = nd
            for d in nd:
                if ops[d].dma_key is None:
                    ops[d].signal = True
        seqc = {e: 0 for e in ENGS}
        dmac = {}
        for o in ops:
            if o.dma_key is not None:
                dmac[o.dma_key] = dmac.get(o.dma_key, 0) + 16
                o.dma_target = dmac[o.dma_key]
            elif o.signal:
                seqc[o.eng] += 1
                o.seq = seqc[o.eng]
        dma_keys = list(dmac.keys())
        dma_issuer = {}
        for o in ops:
            if o.dma_key is not None:
                dma_issuer.setdefault(o.dma_key, o.eng)
        with ExitStack() as es:
            es.enter_context(nc.cleanup_on_exit())
            esem = {e: nc.alloc_semaphore(name=f"{self.name}_s_{e}") for e in ENGS}
            dsem = {k: nc.alloc_semaphore(name=f"{self.name}_d{i}") for i, k in enumerate(dma_keys)}
            block = es.enter_context(nc.Block())

            def stream(ename):
                def body(eng):
                    known = {}
                    for o in ops:
                        if o.eng != ename:
                            continue
                        need = {}
                        for d in o.deps:
                            p = ops[d]
                            if p.dma_key is not None:
                                s, v = dsem[p.dma_key], p.dma_target
                            else:
                                s, v = esem[p.eng], p.seq
                            if need.get(s, 0) < v:
                                need[s] = v
                        for s, v in need.items():
                            if known.get(s, 0) < v:
                                eng.wait_ge(s, v)
                                known[s] = v
                        ins = o.fn(eng)
                        if o.dma_key is not None:
                            ins.then_inc(dsem[o.dma_key], 16)
                        elif o.signal:
                            ins.then_inc(esem[ename], 1)
                    for k in dma_keys:
                        if dma_issuer[k] == ename and known.get(dsem[k], 0) < dmac[k]:
                            eng.wait_ge(dsem[k], dmac[k])
                return body

            if any(o.eng == "pe" for o in ops):
                block.tensor(stream("pe"))
            if any(o.eng == "act" for o in ops):
                block.scalar(stream("act"))
            if any(o.eng == "dve" for o in ops):
                block.vector(stream("dve"))
            if any(o.eng == "pool" for o in ops):
                block.gpsimd(stream("pool"))
            if any(o.eng == "sp" for o in ops):
                block.sync(stream("sp"))


class Cfg:
    def __init__(self, D=2048, DFF=5632, NT=2064, tile=344):
        self.D = D
        self.DFF = DFF
        self.KC = D // 128
        self.FC = DFF // 128
        self.NT = NT
        self.tile = tile
        self.EPS = 1e-6


def _rot(n):
    i = [0]

    def nxt():
        v = i[0] % n
        i[0] += 1
        return v

    return nxt


HD = 64
NFH = 16
NSH = 16
INW = 4368
C_FQ, C_FK, C_FV, C_FZ, C_SQ, C_SK, C_SV = 0, 1024, 2048, 3072, 3088, 4112, 4240
NBLK = 17
NROW = NBLK * 128
VP_G1, VP_GM, VP_G2, VP_FQN, VP_FKN, VP_SQN, VP_SKN, VP_BF, VP_SINK, VP_GO = 0, 16, 32, 48, 49, 50, 51, 52, 68, 84
VP_W = 84
CK_INVD, CK_BD64, CK_ONES, CK_TRI, CK_ID, CK_EPS, CK_ONE = 0, 128, 256, 384, 512, 640, 641
CK_W = 642


def rowof(t):
    return t if t < 16 else t + 112


def blk_cols(b):
    return (0, 16) if b == 0 else (16 + 128 * (b - 1), 128)


def norm_stage(ph, sb, cfg, K, ps, hT, gcol, tiles, offs, xnT, TM):
    KC = cfg.KC
    hT_r = hT.rearrange("c p t -> p c t")
    SM = (TM + 1) // 2 + 1
    hbuf = [sb(f"hbuf{i}", [128, KC, SM], F32) for i in range(2)]
    sq = [sb(f"sq{i}", [128, SM], F32) for i in range(4)]
    rstd = [sb(f"rstd{i}", [128, SM], F32) for i in range(2)]
    ptmp = [sb(f"ptmp{i}", [128, SM], F32) for i in range(2)]
    u = 0
    for ti, (t0, n) in enumerate(tiles):
        h1 = (n // 2 + 1) // 2 * 2
        for (s0, m) in ((0, h1), (h1, n - h1)):
            off = offs[ti] + s0
            hb = hbuf[u % 2]
            r = u % 2
            pb = u % 2
            ph.op("sp", lambda e, hb=hb, a=t0 + s0, m=m: e.dma_start(out=hb[:, :, :m], in_=hT_r[:, :, a:a + m]),
                  reads=[("hT", c, ti) for c in range(KC)], writes=[("hbuf", u % 2, c) for c in range(KC)], dma=("hbuf", u % 2))
            for c in range(KC):
                s = c % 4
                ph.op("act", lambda e, hb=hb, c=c, s=s, m=m: e.activation(out=sq[s][:, :m], in_=hb[:, c, :m], func=AF.Square),
                      reads=[("hbuf", u % 2, c)], writes=[("sq", s)])
                ph.op("pe", lambda e, c=c, s=s, m=m, pb=pb: e.matmul(ps[pb][:, :m], lhsT=K["invD"], rhs=sq[s][:, :m],
                                                                start=(c == 0), stop=(c == KC - 1)),
                      reads=[("sq", s)], writes=[("ps", pb)])
            ph.op("act", lambda e, r=r, m=m, pb=pb: e.activation(out=rstd[r][:, :m], in_=ps[pb][:, :m], func=AF.Ln,
                                                              bias=K["eps"], scale=1.0),
                  reads=[("ps", pb)], writes=[("rstd", r)])
            ph.op("act", lambda e, r=r, m=m: e.activation(out=rstd[r][:, :m], in_=rstd[r][:, :m], func=AF.Exp, scale=-0.5),
                  reads=[("rstd", r)], writes=[("rstd", r)])
            for c in range(KC):
                if True:
                    ph.op("dve", lambda e, hb=hb, c=c, r=r, m=m, off=off: e.scalar_tensor_tensor(
                        out=xnT[:, c, off:off + m], in0=hb[:, c, :m], scalar=gcol[:, c:c + 1], in1=rstd[r][:, :m],
                        op0=ALU.mult, op1=ALU.mult),
                        reads=[("hbuf", u % 2, c), ("rstd", r)], writes=[("xn", ti, c)])
                else:
                    tb = (c // 4) % 2
                    ph.op("pool", lambda e, hb=hb, c=c, r=r, m=m, tb=tb: e.tensor_tensor(
                        out=ptmp[tb][:, :m], in0=hb[:, c, :m], in1=rstd[r][:, :m], op=ALU.mult),
                        reads=[("hbuf", u % 2, c), ("rstd", r)], writes=[("ptmp", tb)])
                    ph.op("pool", lambda e, c=c, m=m, off=off, tb=tb: e.tensor_scalar(
                        out=xnT[:, c, off:off + m], in0=ptmp[tb][:, :m], scalar1=gcol[:, c:c + 1], scalar2=None, op0=ALU.mult),
                        reads=[("ptmp", tb)], writes=[("xn", ti, c)])
            u += 1


def down_stage(ph, sb, cfg, ps, hT, W_r, nin, srcT, src_res, tiles, offs, TM, scale, pre_loaded=0, wd=None):
    KC = cfg.KC
    if wd is None:
        wd = [sb(f"wd{i}", [128, nin, 128], BF16) for i in range(2)]
    hold = [sb(f"hold{i}", [128, TM], F32) for i in range(2)]
    hnew = [sb(f"hnew{i}", [128, TM], F32) for i in range(2)]

    def load_d(dc):
        s = dc % 2
        ph.op("pool", lambda e, dc=dc, s=s: e.dma_start(out=wd[s][:, :, :], in_=W_r[:, :, dc * 128:(dc + 1) * 128]),
              writes=[("wd", s)], dma=("wd", s))

    for dc in range(pre_loaded, min(2, KC)):
        load_d(dc)
    db = _rot(2)
    seq = [(dc, ti) for dc in range(KC) for ti in range(len(tiles))]

    def load_hold(i):
        dc, ti = seq[i]
        t0, n = tiles[ti]
        k = i % 2
        ph.op("sp", lambda e, k=k, dc=dc, t0=t0, n=n: e.dma_start(out=hold[k][:, :n], in_=hT[dc, :, t0:t0 + n]),
              reads=[("hT", dc, ti)], writes=[("hold", k)], dma=("hold", k))

    load_hold(0)
    for i, (dc, ti) in enumerate(seq):
        t0, n = tiles[ti]
        off = offs[ti]
        s = dc % 2
        d_b = 6 + db()
        k = i % 2
        if i + 1 < len(seq):
            load_hold(i + 1)
        for fc in range(nin):
            ph.op("pe", lambda e, fc=fc, s=s, n=n, off=off, d_b=d_b: e.matmul(
                ps[d_b][:, :n], lhsT=wd[s][:, fc, :], rhs=srcT[:, fc, off:off + n], start=(fc == 0), stop=(fc == nin - 1)),
                reads=[("wd", s), src_res(fc, ti)], writes=[("ps", d_b)])
        ph.op("dve", lambda e, k=k, n=n, d_b=d_b: e.scalar_tensor_tensor(
            out=hnew[k][:, :n], in0=ps[d_b][:, :n], scalar=scale, in1=hold[k][:, :n], op0=ALU.mult, op1=ALU.add),
            reads=[("ps", d_b), ("hold", k)], writes=[("hnew", k)])
        ph.op("sp", lambda e, k=k, dc=dc, t0=t0, n=n: e.dma_start(out=hT[dc, :, t0:t0 + n], in_=hnew[k][:, :n]),
              reads=[("hnew", k)], writes=[("hT", dc, ti)], dma=("hnew", k))
        if ti == len(tiles) - 1 and dc + 2 < KC:
            load_d(dc + 2)
    return load_d


def _offs(tiles):
    offs, o = [], 0
    for _, n in tiles:
        offs.append(o)
        o += n
    return offs, o


def ffn_phase(nc, cfg, K, ps, hT, Wg, Wu, Wd, gcol, tiles, name):
    KC, FC = cfg.KC, cfg.FC
    offs, NTp = _offs(tiles)
    TM = max(n for _, n in tiles)
    Wg_r = Wg.rearrange("(c p) f -> p c f", p=128)
    Wu_r = Wu.rearrange("(c p) f -> p c f", p=128)
    Wd_r = Wd.rearrange("(c p) d -> p c d", p=128)
    ph = Phase(nc, name)
    with ExitStack() as es:
        def sb(nm, shape, dt):
            return es.enter_context(nc.sbuf_tensor(f"{name}_{nm}", shape, dt))
        xnT = sb("xnT", [128, KC, NTp], BF16)
        actT = sb("actT", [128, FC, NTp], BF16)
        wg = [sb(f"wg{i}", [128, KC, 128], BF16) for i in range(2)]
        wu = [sb(f"wu{i}", [128, KC, 128], BF16) for i in range(2)]
        wd = [sb(f"wd{i}", [128, FC, 128], BF16) for i in range(2)]
        sg = [sb(f"sg{i}", [128, TM], F32) for i in range(2)]

        def load_gu(fc):
            s = fc % 2
            ph.op("pool", lambda e, fc=fc, s=s: e.dma_start(out=wg[s][:, :, :], in_=Wg_r[:, :, fc * 128:(fc + 1) * 128]),
                  writes=[("wg", s)], dma=("wg", s))
            ph.op("pool", lambda e, fc=fc, s=s: e.dma_start(out=wu[s][:, :, :], in_=Wu_r[:, :, fc * 128:(fc + 1) * 128]),
                  writes=[("wu", s)], dma=("wu", s))

        def load_d(dc):
            s = dc % 2
            ph.op("pool", lambda e, dc=dc, s=s: e.dma_start(out=wd[s][:, :, :], in_=Wd_r[:, :, dc * 128:(dc + 1) * 128]),
                  writes=[("wd", s)], dma=("wd", s))

        load_gu(0)
        load_gu(1)
        norm_stage(ph, sb, cfg, K, ps, hT, gcol, tiles, offs, xnT, TM)
        gb, ub, sgr = _rot(2), _rot(2), _rot(2)
        for fc in range(FC):
            s = fc % 2
            for ti, (t0, n) in enumerate(tiles):
                off = offs[ti]
                g_b = 2 + gb()
                u_b = 4 + ub()
                for c in range(KC):
                    ph.op("pe", lambda e, c=c, s=s, n=n, off=off, g_b=g_b: e.matmul(
                        ps[g_b][:, :n], lhsT=wg[s][:, c, :], rhs=xnT[:, c, off:off + n], start=(c == 0), stop=(c == KC - 1)),
                        reads=[("wg", s), ("xn", ti, c)], writes=[("ps", g_b)])
                for c in range(KC):
                    ph.op("pe", lambda e, c=c, s=s, n=n, off=off, u_b=u_b: e.matmul(
                        ps[u_b][:, :n], lhsT=wu[s][:, c, :], rhs=xnT[:, c, off:off + n], start=(c == 0), stop=(c == KC - 1)),
                        reads=[("wu", s), ("xn", ti, c)], writes=[("ps", u_b)])
                k = sgr()
                ph.op("act", lambda e, k=k, n=n, g_b=g_b: e.activation(out=sg[k][:, :n], in_=ps[g_b][:, :n], func=AF.Silu),
                      reads=[("ps", g_b)], writes=[("sg", k)])
                ph.op("dve", lambda e, k=k, n=n, u_b=u_b, fc=fc, off=off: e.tensor_tensor(
                    out=actT[:, fc, off:off + n], in0=ps[u_b][:, :n], in1=sg[k][:, :n], op=ALU.mult),
                    reads=[("ps", u_b), ("sg", k)], writes=[("act", fc, ti)])
            if fc + 2 < FC:
                load_gu(fc + 2)
            elif fc + 2 == FC:
                load_d(0)
            elif fc + 2 == FC + 1:
                load_d(1)
        down_stage(ph, sb, cfg, ps, hT, Wd_r, FC, actT, lambda fc, ti: ("act", fc, ti), tiles, offs, TM, 0.5,
                   pre_loaded=2, wd=wd)
        ph.emit()


def groups_of(t_lo, t_hi):
    gs = []
    t = t_lo
    while t < t_hi:
        if t < 16:
            ng = 16 - t
        else:
            ng = min(128 - (t - 16) % 128, t_hi - t)
        gs.append((t, ng, rowof(t), t - t_lo))
        t += ng
    return gs


def init_phase(nc, cfg, K, ps, x, meta, hT, name):
    KC = cfg.KC
    hT_r = hT.rearrange("c p t -> p c t")
    ph = Phase(nc, name)
    with ExitStack() as es:
        def sb(nm, shape, dt):
            return es.enter_context(nc.sbuf_tensor(f"{name}_{nm}", shape, dt))
        xt = [sb(f"xt{i}", [128, cfg.D], F32) for i in range(2)]
        ht = [sb(f"ht{i}", [128, KC, 128], F32) for i in range(2)]
        gs = groups_of(0, cfg.NT)

        def load_x(gi):
            t0, ng, row0, off = gs[gi]
            k = gi % 2
            src = meta[0:16, :] if t0 == 0 else x[t0 - 16:t0 - 16 + ng, :]
            ph.op("sp", lambda e: e.dma_start(out=xt[k][:ng, :], in_=src), writes=[("xt", k)], dma=("xt", k))

        load_x(0)
        for gi, (t0, ng, row0, off) in enumerate(gs):
            k = gi % 2
            if gi + 1 < len(gs):
                load_x(gi + 1)
            for q4 in range(KC // 4):
                b = (gi * (KC // 4) + q4) % 8
                for a in range(4):
                    c = q4 * 4 + a
                    ph.op("pe", lambda e, k=k, ng=ng, c=c, a=a, b=b: e.transpose(
                        ps[b][:, a * 128:a * 128 + ng], xt[k][:ng, c * 128:(c + 1) * 128], K["id32"][:ng, :ng]),
                        reads=[("xt", k)], writes=[("ps", b)])
                eng = "act" if q4 % 2 == 0 else "dve"
                if eng == "act":
                    ph.op("act", lambda e, k=k, ng=ng, q4=q4, b=b: e.activation(
                        out=ht[k][:, q4 * 4:q4 * 4 + 4, :ng], in_=ps[b][:, :].rearrange("p (a t) -> p a t", a=4)[:, :, :ng],
                        func=AF.Copy), reads=[("ps", b)], writes=[("ht", k, q4)])
                else:
                    ph.op("dve", lambda e, k=k, ng=ng, q4=q4, b=b: e.tensor_copy(
                        out=ht[k][:, q4 * 4:q4 * 4 + 4, :ng], in_=ps[b][:, :].rearrange("p (a t) -> p a t", a=4)[:, :, :ng]),
                        reads=[("ps", b)], writes=[("ht", k, q4)])
            ph.op("sp", lambda e, k=k, ng=ng, t0=t0: e.dma_start(out=hT_r[:, :, t0:t0 + ng], in_=ht[k][:, :, :ng]),
                  reads=[("ht", k, q4) for q4 in range(KC // 4)], writes=[("hTw", gi)], dma=("ht", k))
        ph.emit()


def final_phase(nc, cfg, K, ps, hT, out, name):
    KC = cfg.KC
    hT_r = hT.rearrange("c p t -> p c t")
    ph = Phase(nc, name)
    with ExitStack() as es:
        def sb(nm, shape, dt):
            return es.enter_context(nc.sbuf_tensor(f"{name}_{nm}", shape, dt))
        ht = [sb(f"ht{i}", [128, KC, 128], F32) for i in range(2)]
        ot = [sb(f"ot{i}", [128, cfg.D], F32) for i in range(2)]
        gs = groups_of(16, cfg.NT)

        def load_h(gi):
            t0, ng, row0, off = gs[gi]
            k = gi % 2
            ph.op("sp", lambda e: e.dma_start(out=ht[k][:, :, :ng], in_=hT_r[:, :, t0:t0 + ng]),
                  writes=[("ht", k)], dma=("ht", k))

        load_h(0)
        for gi, (t0, ng, row0, off) in enumerate(gs):
            k = gi % 2
            if gi + 1 < len(gs):
                load_h(gi + 1)
            for q4 in range(KC // 4):
                b = (gi * (KC // 4) + q4) % 8
                for a in range(4):
                    c = q4 * 4 + a
                    ph.op("pe", lambda e, k=k, ng=ng, c=c, a=a, b=b: e.transpose(
                        ps[b][:ng, a * 128:(a + 1) * 128], ht[k][:, c, :ng], K["id32"]),
                        reads=[("ht", k)], writes=[("ps", b)])
                if q4 % 2 == 0:
                    ph.op("act", lambda e, k=k, ng=ng, q4=q4, b=b: e.activation(
                        out=ot[k][:ng, q4 * 512:(q4 + 1) * 512], in_=ps[b][:ng, :], func=AF.Copy),
                        reads=[("ps", b)], writes=[("ot", k, q4)])
                else:
                    ph.op("dve", lambda e, k=k, ng=ng, q4=q4, b=b: e.tensor_copy(
                        out=ot[k][:ng, q4 * 512:(q4 + 1) * 512], in_=ps[b][:ng, :]),
                        reads=[("ps", b)], writes=[("ot", k, q4)])
            ph.op("sp", lambda e, k=k, ng=ng, t0=t0: e.dma_start(out=out[t0 - 16:t0 - 16 + ng, :], in_=ot[k][:ng, :]),
                  reads=[("ot", k, q4) for q4 in range(KC // 4)], writes=[("outw", gi)], dma=("ot", k))
        ph.emit()


def inproj_phase(nc, cfg, K, ps, hT, Win, vp, qkT, vF_d, vW_d, logf_d, t_lo, t_hi, tiles, name):
    KC = cfg.KC
    offs, NTp = _offs(tiles)
    TM = max(n for _, n in tiles)
    Win_r = Win.rearrange("(c p) f -> p c f", p=128)
    gs = groups_of(t_lo, t_hi)
    ph = Phase(nc, name)
    with ExitStack() as es:
        def sb(nm, shape, dt):
            return es.enter_context(nc.sbuf_tensor(f"{name}_{nm}", shape, dt))
        xnT = sb("xnT", [128, KC, NTp], BF16)
        wb = [sb(f"wb{i}", [128, KC, 128], BF16) for i in range(2)]
        wt = [sb(f"wt{i}", [128, KC, 512], BF16) for i in range(2)]
        qs = [sb(f"qs{i}", [128, TM], F32) for i in range(2)]
        q2 = [sb(f"q2{i}", [128, TM], F32) for i in range(2)]
        rr = [sb(f"rr{i}", [128, TM], F32) for i in range(2)]
        qn = [sb(f"qn{i}", [128, TM], BF16) for i in range(2)]
        vst = [sb(f"vst{i}", [128, 8, 65], BF16) for i in range(2)]
        vsw = [sb(f"vsw{i}", [128, 2, 65], BF16) for i in range(2)]
        fz = [sb(f"fz{i}", [128, 16], F32) for i in range(2)]
        lf = [sb(f"lf{i}", [128, 16], F32) for i in range(2)]

        blocks = []
        for b in range(8):
            blocks.append((b, [(C_FQ + b * 128, 128, 0)], VP_FQN))
        for b in range(8):
            blocks.append((8 + b, [(C_FK + b * 128, 128, 0)], VP_FKN))
        for b in range(8):
            blocks.append((16 + b, [(C_SQ + b * 128, 128, 0)], VP_SQN))
        for g in range(2):
            blocks.append((24 + g, [(C_SK + 64 * g, 64, 0), (C_SK + 64 * g, 64, 64)], VP_SKN))
        ttiles = [(0, [(C_FV, 512, 0)]), (1, [(C_FV + 512, 512, 0)]), (2, [(C_FZ, 16, 0), (C_SV, 128, 16)])]

        def load_b(bi):
            s = bi % 2
            for (c0, ncol, d0) in blocks[bi][1]:
                ph.op("pool", lambda e, s=s, c0=c0, ncol=ncol, d0=d0: e.dma_start(
                    out=wb[s][:, :, d0:d0 + ncol], in_=Win_r[:, :, c0:c0 + ncol]),
                    writes=[("wb", s, d0)], dma=("wb", s))

        def load_t(k):
            s = k % 2
            for (c0, ncol, d0) in ttiles[k][1]:
                ph.op("pool", lambda e, s=s, c0=c0, ncol=ncol, d0=d0: e.dma_start(
                    out=wt[s][:, :, d0:d0 + ncol], in_=Win_r[:, :, c0:c0 + ncol]),
                    writes=[("wt", s, d0)], dma=("wt", s))

        load_b(0)
        load_b(1)
        for i in range(2):
            ph.op("dve", lambda e, i=i: e.memset(vst[i][:, :, :], 1.0), writes=[("vst", i)])
            ph.op("dve", lambda e, i=i: e.memset(vsw[i][:, :, :], 1.0), writes=[("vsw", i)])
        norm_stage(ph, sb, cfg, K, ps, hT, vp[:, VP_GM:VP_GM + KC], tiles, offs, xnT, TM)

        work = [(bi, ti) for bi in range(len(blocks)) for ti in range(len(tiles))]

        def part1(n):
            bi, ti = work[n]
            dst, parts, gcolidx = blocks[bi]
            s = bi % 2
            t0, nn = tiles[ti]
            off = offs[ti]
            q_b = 2 + n % 2
            k = n % 2
            for c in range(KC):
                ph.op("pe", lambda e, c=c: e.matmul(
                    ps[q_b][:, :nn], lhsT=wb[s][:, c, :], rhs=xnT[:, c, off:off + nn], start=(c == 0), stop=(c == KC - 1)),
                    reads=[("wb", s, 0), ("wb", s, 64), ("xn", ti, c)], writes=[("ps", q_b)])
            ph.op("act", lambda e: e.activation(out=qs[k][:, :nn], in_=ps[q_b][:, :nn], func=AF.Copy),
                  reads=[("ps", q_b)], writes=[("qs", k)])
            ph.op("dve", lambda e: e.tensor_tensor(out=q2[k][:, :nn], in0=qs[k][:, :nn], in1=qs[k][:, :nn], op=ALU.mult),
                  reads=[("qs", k)], writes=[("q2", k)])
            if ti == len(tiles) - 1:
                if bi + 2 < len(blocks):
                    load_b(bi + 2)
                elif bi + 2 == len(blocks):
                    load_t(0)
                elif bi + 2 == len(blocks) + 1:
                    load_t(1)

        def part2(n):
            bi, ti = work[n]
            dst, parts, gcolidx = blocks[bi]
            t0, nn = tiles[ti]
            m_b = 4 + n % 2
            k = n % 2
            ph.op("pe", lambda e: e.matmul(ps[m_b][:, :nn], lhsT=K["bd64"], rhs=q2[k][:, :nn], start=True, stop=True),
                  reads=[("q2", k)], writes=[("ps", m_b)])
            ph.op("act", lambda e: e.activation(out=rr[k][:, :nn], in_=ps[m_b][:, :nn], func=AF.Ln, bias=K["eps"], scale=1.0),
                  reads=[("ps", m_b)], writes=[("rr", k)])
            ph.op("act", lambda e: e.activation(out=rr[k][:, :nn], in_=rr[k][:, :nn], func=AF.Exp, scale=-0.5),
                  reads=[("rr", k)], writes=[("rr", k)])
            ph.op("dve", lambda e: e.scalar_tensor_tensor(
                out=qn[k][:, :nn], in0=qs[k][:, :nn], scalar=vp[:, gcolidx:gcolidx + 1], in1=rr[k][:, :nn],
                op0=ALU.mult, op1=ALU.mult),
                reads=[("qs", k), ("rr", k)], writes=[("qn", k)])
            ph.op("sp", lambda e: e.dma_start(out=qkT[dst, :, t0:t0 + nn], in_=qn[k][:, :nn]),
                  reads=[("qn", k)], writes=[("qkT", dst, ti)], dma=("qn", k))

        for n in range(len(work)):
            part1(n)
            if n >= 1:
                part2(n - 1)
        part2(len(work) - 1)

        tb = _rot(2)
        vr, wr, fr = _rot(2), _rot(2), _rot(2)
        for k3, (kk, parts) in enumerate(ttiles):
            s = k3 % 2
            ncols = sum(p[1] for p in parts)
            for (t0, ng, row0, off) in gs:
                t_b = 2 + tb()
                for c in range(KC):
                    ph.op("pe", lambda e, c=c, s=s, ng=ng, off=off, t_b=t_b, ncols=ncols: e.matmul(
                        ps[t_b][:ng, :ncols], lhsT=xnT[:, c, off:off + ng], rhs=wt[s][:, c, :ncols],
                        start=(c == 0), stop=(c == KC - 1)),
                        reads=[("wt", s, 0), ("wt", s, 16)] + [("xn", ti, c) for ti in range(len(tiles))],
                        writes=[("ps", t_b)])
                if kk < 2:
                    v = vr()
                    ph.op("act", lambda e, v=v, ng=ng, t_b=t_b: e.activation(
                        out=vst[v][:ng, :, 0:64], in_=ps[t_b][:ng, :].rearrange("p (h d) -> p h d", h=8), func=AF.Copy),
                        reads=[("ps", t_b), ("vst", v)], writes=[("vst", v)])
                    ph.op("sp", lambda e, v=v, ng=ng, row0=row0, kk=kk: e.dma_start(
                        out=vF_d[row0:row0 + ng, kk * 520:(kk + 1) * 520], in_=vst[v][:ng, :, :].rearrange("p h d -> p (h d)")),
                        reads=[("vst", v)], writes=[("vF_d", row0, kk)], dma=("vst", v))
                else:
                    w = wr()
                    f = fr()
                    ph.op("dve", lambda e, f=f, ng=ng, t_b=t_b: e.tensor_tensor(
                        out=fz[f][:ng, :], in0=ps[t_b][:ng, 0:16], in1=vp[:ng, VP_BF:VP_BF + 16], op=ALU.add),
                        reads=[("ps", t_b)], writes=[("fz", f)])
                    ph.op("act", lambda e, w=w, ng=ng, t_b=t_b: e.activation(
                        out=vsw[w][:ng, :, 0:64], in_=ps[t_b][:ng, 16:144].rearrange("p (h d) -> p h d", h=2), func=AF.Copy),
                        reads=[("ps", t_b), ("vsw", w)], writes=[("vsw", w)])
                    ph.op("act", lambda e, f=f, ng=ng: e.activation(out=fz[f][:ng, :], in_=fz[f][:ng, :], func=AF.Exp, scale=-1.0),
                          reads=[("fz", f)], writes=[("fz", f)])
                    ph.op("act", lambda e, f=f, ng=ng: e.activation(out=fz[f][:ng, :], in_=fz[f][:ng, :], func=AF.Ln,
                                                                 bias=K["one"][:ng, :], scale=1.0),
                          reads=[("fz", f)], writes=[("fz", f)])
                    ph.op("dve", lambda e, f=f, ng=ng: e.tensor_scalar(out=lf[f][:ng, :], in0=fz[f][:ng, :], scalar1=-1.0,
                                                                    scalar2=None, op0=ALU.mult),
                          reads=[("fz", f)], writes=[("lf", f)])
                    ph.op("sp", lambda e, f=f, ng=ng, row0=row0: e.dma_start(out=logf_d[row0:row0 + ng, :], in_=lf[f][:ng, :]),
                          reads=[("lf", f)], writes=[("logf_d", row0)], dma=("lf", f))
                    ph.op("sp", lambda e, w=w, ng=ng, row0=row0: e.dma_start(
                        out=vW_d[row0:row0 + ng, :], in_=vsw[w][:ng, :, :].rearrange("p h d -> p (h d)")),
                        reads=[("vsw", w)], writes=[("vW_d", row0)], dma=("vsw", w))
            if k3 + 2 < len(ttiles):
                load_t(k3 + 2)
        ph.emit()


def decay_phase(nc, cfg, K, ps, logf_d, Cb_hm, c_hm, name):
    ph = Phase(nc, name)
    with ExitStack() as es:
        L = es.enter_context(nc.sbuf_tensor(f"{name}_L", [128, NBLK, 16], F32))
        ph.op("dve", lambda e: e.memset(L[:, :, :], 0.0), writes=["L"])
        ph.op("sp", lambda e: e.dma_start(out=L[0:16, 0, :], in_=logf_d[0:16, :]), reads=["L"], writes=["L0"], dma="L0")
        ph.op("sp", lambda e: e.dma_start(out=L[:, 1:NBLK, :], in_=logf_d[128:NROW, :].rearrange("(b p) h -> p b h", p=128)),
              reads=["L"], writes=["L1"], dma="L1")
        Lf = L[:, :, :].rearrange("p b h -> p (b h)")
        ph.op("pe", lambda e: e.matmul(ps[0][:, 0:NBLK * 16], lhsT=K["ones32"], rhs=Lf, start=True, stop=True),
              reads=["L0", "L1"], writes=[("ps", 0)])
        ph.op("pe", lambda e: e.matmul(ps[1][:, 0:NBLK * 16], lhsT=K["tri32"], rhs=Lf, start=True, stop=True),
              reads=["L0", "L1"], writes=[("ps", 1)])
        ph.op("dve", lambda e: e.memset(Cb_hm[:, :, :], 0.0), writes=["Cb"])
        for i in range(1, NBLK):
            ph.op("dve", lambda e, i=i: e.tensor_tensor(out=Cb_hm[:, :, i], in0=Cb_hm[:, :, i - 1],
                                                     in1=ps[0][:, (i - 1) * 16:i * 16], op=ALU.add),
                  reads=["Cb", ("ps", 0)], writes=["Cb"])
        ph.op("dve", lambda e: e.tensor_tensor(out=c_hm[:, :, :], in0=ps[1][:, 0:NBLK * 16].rearrange("p (j h) -> p h j", h=16),
                                            in1=Cb_hm[:, :, :], op=ALU.add),
              reads=["Cb", ("ps", 1)], writes=["chm"])
        ph.emit()


QCHUNKS = [[0], [1, 2, 3, 4], [5, 6, 7, 8], [9, 10, 11, 12], [13, 14, 15, 16]]


def _load_tokmajor(ph, dst, src_d, width, key):
    ph.op("sp", lambda e: e.dma_start(out=dst[0:16, 0, :], in_=src_d[0:16, :]), writes=[(key, 0)], dma=(key, 0))
    ph.op("sp", lambda e: e.dma_start(out=dst[:, 1:NBLK, :], in_=src_d[128:NROW, :].rearrange("(b p) c -> p b c", p=128)),
          writes=[(key, 1)], dma=(key, 1))


def _store_ost(ph, ost_k, k, Oscr, col0):
    ph.op("sp", lambda e: e.dma_start(out=Oscr[0:16, col0:col0 + 128], in_=ost_k[0:16, 0, :]),
          reads=[("ost", k, i) for i in range(NBLK)], writes=[("Oscr", col0, 0)], dma=("ost", k))
    ph.op("sp", lambda e: e.dma_start(out=Oscr[128:NROW, col0:col0 + 128].rearrange("(b p) c -> p b c", p=128),
                                     in_=ost_k[:, 1:NBLK, :]),
          reads=[("ost", k, i) for i in range(NBLK)], writes=[("Oscr", col0, 1)], dma=("ost", k))


def fox_phase(nc, cfg, K, ps, qkT, vF_d, Cb_hm, c_hm, Oscr, name, npairs=8):
    NT = cfg.NT
    ph = Phase(nc, name)
    with ExitStack() as es:
        def sb(nm, shape, dt):
            return es.enter_context(nc.sbuf_tensor(f"{name}_{nm}", shape, dt))
        vF = sb("vF", [128, NBLK, NFH * 65], BF16)
        qT = [sb(f"qT{i}", [128, NT], BF16) for i in range(2)]
        kT = [sb(f"kT{i}", [128, NT], BF16) for i in range(2)]
        NB = [sb(f"NB{i}", [128, NBLK, NBLK], F32) for i in range(2)]
        pt = [sb(f"pt{i}", [128, 128], BF16) for i in range(12)]
        rec = [sb(f"rec{i}", [128, 4], F32) for i in range(4)]
        ost = [sb(f"ost{i}", [128, NBLK, 128], F32) for i in range(2)]
        _load_tokmajor(ph, vF, vF_d, NFH * 65, "vF")

        def load_pair(hp):
            k = hp % 2
            ph.op("sp", lambda e: e.dma_start(out=qT[k][:, :], in_=qkT[hp, :, :]), writes=[("qT", k)], dma=("qT", k))
            ph.op("sp", lambda e: e.dma_start(out=kT[k][:, :], in_=qkT[8 + hp, :, :]), writes=[("kT", k)], dma=("kT", k))

        def make_nb(h):
            nb = h % 2
            ph.op("dve", lambda e: e.tensor_tensor(
                out=NB[nb][:, :, :], in0=Cb_hm[:, h, :].unsqueeze(1).to_broadcast([128, NBLK, NBLK]),
                in1=c_hm[:, h, :].unsqueeze(2).to_broadcast([128, NBLK, NBLK]), op=ALU.subtract),
                writes=[("NB", nb)])

        items = []
        for hp in range(npairs):
            for hh in range(2):
                for ci, qc in enumerate(QCHUNKS):
                    for j in range(0, qc[-1] + 1):
                        items.append((hp, hh, ci, qc, j))
        ptr, rcr = _rot(12), _rot(4)
        SB = [0, 1, 2, 3, 6, 7]
        NS = len(SB)

        def emit_st(n):
            hp, hh, ci, qc, j = items[n]
            k, pb = hp % 2, 64 * hh
            if hh == 0 and ci == 0 and j == 0:
                if hp == 0:
                    load_pair(0)
                if hp + 1 < npairs:
                    load_pair(hp + 1)
            if ci == 0 and j == 0:
                make_nb(2 * hp + hh)
            iv = [i for i in qc if i >= j]
            k0, nk = blk_cols(j)
            c_lo = blk_cols(iv[0])[0]
            c_hi = blk_cols(iv[-1])[0] + blk_cols(iv[-1])[1]
            s_b = SB[n % NS]
            ph.op("pe", lambda e: e.matmul(
                ps[s_b][:nk, 0:c_hi - c_lo], lhsT=kT[k][pb:pb + 64, k0:k0 + nk], rhs=qT[k][pb:pb + 64, c_lo:c_hi],
                start=True, stop=True),
                reads=[("qT", k), ("kT", k)], writes=[("ps", s_b)])

        def emit_rest(n):
            hp, hh, ci, qc, j = items[n]
            h = 2 * hp + hh
            k, nb = hp % 2, h % 2
            o_b = 4 + (n_chunk[n] % 2)
            iv = [i for i in qc if i >= j]
            k0, nk = blk_cols(j)
            c_lo = blk_cols(iv[0])[0]
            s_b = SB[n % NS]
            for i in iv:
                q0, nq = blk_cols(i)
                a = qc.index(i)
                p = ptr()
                first = (j == 0 and i == qc[0])
                ph.op("act", lambda e, p=p, nq=nq, q0=q0, i=i: e.activation(
                    out=pt[p][:nk, :nq], in_=ps[s_b][:nk, q0 - c_lo:q0 - c_lo + nq], func=AF.Exp,
                    bias=NB[nb][:nk, j, i:i + 1], scale=0.125),
                    reads=[("ps", s_b), ("NB", nb)], writes=[("pt", p)])
                if i == j:
                    ph.op("pool", lambda e, p=p, nq=nq: e.tensor_tensor(
                        out=pt[p][:nk, :nq], in0=pt[p][:nk, :nq], in1=K["trimask"][:nk, :nq], op=ALU.mult),
                        reads=[("pt", p)], writes=[("pt", p)])
                ph.op("pe", lambda e, p=p, nq=nq, a=a, i=i, first=first: e.matmul(
                    ps[o_b][:nq, a * 65:(a + 1) * 65], lhsT=pt[p][:nk, :nq], rhs=vF[:nk, j, h * 65:(h + 1) * 65],
                    start=first, stop=(j == i), skip_group_check=True),
                    reads=[("pt", p), ("vF", 0), ("vF", 1)], writes=[("ps", o_b)])
            if j == qc[-1]:
                nqm = blk_cols(qc[0])[1]
                na = len(qc)
                r = rcr()
                ob3 = ps[o_b][:nqm, 0:na * 65].rearrange("p (a c) -> p a c", c=65)
                ph.op("dve", lambda e, r=r: e.reciprocal(out=rec[r][:nqm, 0:na], in_=ob3[:, :, 64]),
                      reads=[("ps", o_b)], writes=[("rec", r)])
                ph.op("dve", lambda e, r=r: e.tensor_tensor(
                    out=ost[k][:nqm, qc[0]:qc[0] + na, hh * 64:(hh + 1) * 64], in0=ob3[:, :, 0:64],
                    in1=rec[r][:nqm, 0:na].unsqueeze(2).to_broadcast([nqm, na, 64]), op=ALU.mult),
                    reads=[("ps", o_b), ("rec", r)], writes=[("ost", k, i) for i in qc])
                if hh == 1 and ci == len(QCHUNKS) - 1:
                    _store_ost(ph, ost[k], k, Oscr, hp * 128)

        n_chunk = []
        cc = -1
        for (hp, hh, ci, qc, j) in items:
            if j == 0:
                cc += 1
            n_chunk.append(cc)
        LA = 4
        for n in range(min(LA, len(items))):
            emit_st(n)
        for n in range(len(items)):
            if n + LA < len(items):
                emit_st(n + LA)
            emit_rest(n)
        ph.emit()


def swa_phase(nc, cfg, K, ps, qkT, vW_d, vp, tab2_d, tab0_d, Oscr, name, npairs=8):
    NT = cfg.NT
    ph = Phase(nc, name)
    with ExitStack() as es:
        def sb(nm, shape, dt):
            return es.enter_context(nc.sbuf_tensor(f"{name}_{nm}", shape, dt))
        vW = sb("vW", [128, NBLK, 2 * 65], BF16)
        qT = [sb(f"qT{i}", [128, NT], BF16) for i in range(2)]
        kd = [sb(f"kd{i}", [128, NT], BF16) for i in range(2)]
        tab2 = sb("tab2", [128, NSH, 256], F32)
        tab0 = sb("tab0", [128, NSH, 144], F32)
        esink = sb("esink", [128, NSH], F32)
        tt = [sb(f"tt{i}", [128, 256], F32) for i in range(5)]
        pt = [sb(f"pt{i}", [128, 256], BF16) for i in range(5)]
        den = [sb(f"den{i}", [128, 4], F32) for i in range(4)]
        ost = [sb(f"ost{i}", [128, NBLK, 128], F32) for i in range(2)]
        _load_tokmajor(ph, vW, vW_d, 2 * 65, "vW")
        ph.op("sp", lambda e: e.dma_start(out=tab2[:, :, :], in_=tab2_d), writes=["tab2"], dma="tab2")
        ph.op("sp", lambda e: e.dma_start(out=tab0[:, :, :], in_=tab0_d), writes=["tab0"], dma="tab0")
        ph.op("act", lambda e: e.activation(out=esink[:, :], in_=vp[:, VP_SINK:VP_SINK + NSH], func=AF.Exp), writes=["esink"])

        def load_pair(hp):
            k = hp % 2
            ph.op("sp", lambda e: e.dma_start(out=qT[k][:, :], in_=qkT[16 + hp, :, :]), writes=[("qT", k)], dma=("qT", k))

        def load_kd(g):
            ph.op("sp", lambda e: e.dma_start(out=kd[g][:, :], in_=qkT[24 + g, :, :]), writes=[("kd", g)], dma=("kd", g))

        load_kd(0)
        load_kd(1)
        load_pair(0)
        ttr, ptr, dnr = _rot(5), _rot(5), _rot(4)
        items = [(hp, hh, j) for hp in range(npairs) for hh in range(2) for j in range(NBLK)]
        SB = [0, 1, 2, 3, 6, 7]
        NS = len(SB)

        def obank(i):
            return 4 + ((i // 4) % 2), i % 4

        def geom(j):
            iv = [i for i in (j, j + 1) if i < NBLK]
            k0, nk = blk_cols(j)
            c_lo = blk_cols(iv[0])[0]
            c_hi = blk_cols(iv[-1])[0] + blk_cols(iv[-1])[1]
            return iv, k0, nk, c_lo, c_hi

        def emit_st(n):
            hp, hh, j = items[n]
            k, pb, g = hp % 2, 64 * hh, (2 * hp + hh) // 8
            if hh == 0 and j == 0 and hp + 1 < npairs:
                load_pair(hp + 1)
            iv, k0, nk, c_lo, c_hi = geom(j)
            s_b = SB[n % NS]
            ph.op("pe", lambda e: e.matmul(
                ps[s_b][:nk, 0:c_hi - c_lo], lhsT=kd[g][pb:pb + 64, k0:k0 + nk], rhs=qT[k][pb:pb + 64, c_lo:c_hi],
                start=True, stop=True),
                reads=[("qT", k), ("kd", g)], writes=[("ps", s_b)])

        def emit_rest(n):
            hp, hh, j = items[n]
            h = 2 * hp + hh
            k, g = hp % 2, h // 8
            iv, k0, nk, c_lo, c_hi = geom(j)
            ncol = c_hi - c_lo
            s_b = SB[n % NS]
            t = ttr()
            p = ptr()
            tab = tab0 if j == 0 else tab2
            ph.op("dve", lambda e: e.scalar_tensor_tensor(
                out=tt[t][:nk, :ncol], in0=ps[s_b][:nk, :ncol], scalar=0.125, in1=tab[:nk, h, :ncol],
                op0=ALU.mult, op1=ALU.add),
                reads=[("ps", s_b), "tab2", "tab0"], writes=[("tt", t)])
            ph.op("act", lambda e: e.activation(out=pt[p][:nk, :ncol], in_=tt[t][:nk, :ncol], func=AF.Exp),
                  reads=[("tt", t)], writes=[("pt", p)])
            for i in iv:
                q0, nq = blk_cols(i)
                o_b, a = obank(i)
                ph.op("pe", lambda e, nq=nq, q0=q0, o_b=o_b, a=a, st=(j == max(i - 1, 0)), i=i: e.matmul(
                    ps[o_b][:nq, a * 65:(a + 1) * 65], lhsT=pt[p][:nk, q0 - c_lo:q0 - c_lo + nq],
                    rhs=vW[:nk, j, g * 65:(g + 1) * 65], start=st, stop=(j == i), skip_group_check=True),
                    reads=[("pt", p), ("vW", 0), ("vW", 1)], writes=[("ps", o_b)])
            pending.append((n + DELAY, hp, hh, j))
            while pending and (pending[0][0] <= n or j == NBLK - 1):
                _, hp_f, hh_f, j_f = pending.pop(0)
                finalize(hp_f, hh_f, j_f)

        def finalize(hp, hh, j):
            h = 2 * hp + hh
            k = hp % 2
            if j % 4 == 3 or j == NBLK - 1:
                blks = [i for i in range(NBLK) if (i // 4) == (j // 4)]
                na = len(blks)
                o_b = obank(blks[0])[0]
                d = dnr()
                ob3 = ps[o_b][:, 0:na * 65].rearrange("p (a c) -> p a c", c=65)
                ph.op("dve", lambda e, d=d: e.tensor_scalar(out=den[d][:, 0:na], in0=ob3[:, :, 64], scalar1=esink[:, h:h + 1],
                                                          scalar2=None, op0=ALU.add),
                      reads=[("ps", o_b), "esink"], writes=[("den", d)])
                ph.op("dve", lambda e, d=d: e.reciprocal(out=den[d][:, 0:na], in_=den[d][:, 0:na]),
                      reads=[("den", d)], writes=[("den", d)])
                ph.op("dve", lambda e, d=d: e.tensor_tensor(
                    out=ost[k][:, blks[0]:blks[0] + na, hh * 64:(hh + 1) * 64], in0=ob3[:, :, 0:64],
                    in1=den[d][:, 0:na].unsqueeze(2).to_broadcast([128, na, 64]), op=ALU.mult),
                    reads=[("ps", o_b), ("den", d)], writes=[("ost", k, i) for i in blks])
            if hh == 1 and j == NBLK - 1:
                _store_ost(ph, ost[k], k, Oscr, 1024 + hp * 128)

        LA = 4
        DELAY = 2
        pending = []
        for n in range(min(LA, len(items))):
            emit_st(n)
        for n in range(len(items)):
            if n + LA < len(items):
                emit_st(n + LA)
            emit_rest(n)
        while pending:
            _, hp_f, hh_f, j_f = pending.pop(0)
            finalize(hp_f, hh_f, j_f)
        ph.emit()


def outproj_phase(nc, cfg, K, ps, hT, Wout, gout_d, Oscr, t_lo, t_hi, tiles, name):
    KC = cfg.KC
    offs, NTp = _offs(tiles)
    TM = max(n for _, n in tiles)
    Wo_r = Wout.rearrange("(c p) d -> p c d", p=128)
    gs = groups_of(t_lo, t_hi)
    ph = Phase(nc, name)
    with ExitStack() as es:
        def sb(nm, shape, dt):
            return es.enter_context(nc.sbuf_tensor(f"{name}_{nm}", shape, dt))
        onT = sb("onT", [128, KC, NTp], BF16)
        Ob = [sb(f"Ob{i}", [128, 2048], F32) for i in range(2)]
        on = [sb(f"on{i}", [128, 2048], BF16) for i in range(2)]
        junk = sb("junk", [128, 1024], BF16)
        ssq = [sb(f"ssq{i}", [128, 2], F32) for i in range(2)]
        wd = [sb(f"wd{i}", [128, KC, 128], BF16) for i in range(2)]
        gout = sb("gout", [128, 2048], F32)
        ph.op("sp", lambda e: e.dma_start(out=gout[:, :], in_=gout_d), writes=["gout"], dma="gout")
        for dc in range(2):
            ph.op("pool", lambda e, dc=dc: e.dma_start(out=wd[dc][:, :, :], in_=Wo_r[:, :, dc * 128:(dc + 1) * 128]),
                  writes=[("wd", dc)], dma=("wd", dc))
        tbr = _rot(2)

        def load_o(gi):
            t0, ng, row0, off = gs[gi]
            k = gi % 2
            ph.op("sp", lambda e: e.dma_start(out=Ob[k][:ng, :], in_=Oscr[row0:row0 + ng, :]), writes=[("Ob", k)], dma=("Ob", k))

        def front(gi):
            t0, ng, row0, off = gs[gi]
            k = gi % 2
            ph.op("dve", lambda e: e.memset(ssq[k][:, :], 0.0), writes=[("ssq", k)])
            for hf in range(2):
                ph.op("act", lambda e, hf=hf: e.activation(
                    out=junk[:ng, :], in_=Ob[k][:ng, hf * 1024:(hf + 1) * 1024], func=AF.Square,
                    accum_out=ssq[k][:ng, hf:hf + 1]),
                    reads=[("Ob", k), ("ssq", k)], writes=[("ssq", k), "junk"])
            ph.op("act", lambda e: e.activation(out=ssq[k][:ng, :], in_=ssq[k][:ng, :], func=AF.Ln,
                                                bias=K["eps"][:ng, :], scale=1.0 / 1024.0),
                  reads=[("ssq", k)], writes=[("ssq", k)])
            ph.op("act", lambda e: e.activation(out=ssq[k][:ng, :], in_=ssq[k][:ng, :], func=AF.Exp, scale=-0.5),
                  reads=[("ssq", k)], writes=[("ssq", k)])

        def back(gi):
            t0, ng, row0, off = gs[gi]
            k = gi % 2
            for hf in range(2):
                ph.op("dve", lambda e, hf=hf: e.scalar_tensor_tensor(
                    out=on[k][:ng, hf * 1024:(hf + 1) * 1024], in0=Ob[k][:ng, hf * 1024:(hf + 1) * 1024],
                    scalar=ssq[k][:ng, hf:hf + 1], in1=gout[:ng, hf * 1024:(hf + 1) * 1024],
                    op0=ALU.mult, op1=ALU.mult),
                    reads=[("Ob", k), ("ssq", k), "gout"], writes=[("on", k, hf)])
            for hf in range(2):
                t_b = 2 + tbr()
                pbf = ps[t_b][:, :].bitcast(BF16)
                for a in range(8):
                    f = hf * 8 + a
                    ph.op("pe", lambda e, f=f, a=a, pbf=pbf: e.transpose(
                        pbf[:, a * 128:a * 128 + ng], on[k][:ng, f * 128:(f + 1) * 128], K["idbf"][:ng, :ng]),
                        reads=[("on", k, hf)], writes=[("ps", t_b)])
                if hf == 0:
                    ph.op("act", lambda e, pbf=pbf, hf=hf: e.activation(
                        out=onT[:, hf * 8:hf * 8 + 8, off:off + ng], in_=pbf.rearrange("p (a t) -> p a t", a=8)[:, :, :ng],
                        func=AF.Copy), reads=[("ps", t_b)], writes=[("onT", gi, hf)])
                else:
                    ph.op("dve", lambda e, pbf=pbf, hf=hf: e.tensor_copy(
                        out=onT[:, hf * 8:hf * 8 + 8, off:off + ng], in_=pbf.rearrange("p (a t) -> p a t", a=8)[:, :, :ng]),
                        reads=[("ps", t_b)], writes=[("onT", gi, hf)])

        load_o(0)
        if len(gs) > 1:
            load_o(1)
        front(0)
        for gi in range(len(gs)):
            if gi + 1 < len(gs):
                front(gi + 1)
            back(gi)
            if gi + 2 < len(gs):
                load_o(gi + 2)

        def src_res(fc, ti):
            return ("onTall", fc // 8)
        for hf in range(2):
            ph.op("dve" if hf else "act", (lambda e: e.tensor_copy(out=junk[0:1, 0:2], in_=junk[0:1, 2:4])) if hf else
                  (lambda e: e.activation(out=junk[0:1, 4:6], in_=junk[0:1, 6:8], func=AF.Copy)),
                  reads=[("onT", gi, hf) for gi in range(len(gs))] + ["junk"], writes=[("onTall", hf)])
        down_stage(ph, sb, cfg, ps, hT, Wo_r, KC, onT, src_res, tiles, offs, TM, 1.0, pre_loaded=2, wd=wd)
        ph.emit()


HALVES = [
    (0, 1040, [(0, 344), (344, 344), (688, 352)]),
    (1040, 2064, [(1040, 344), (1384, 344), (1728, 336)]),
]
WNAMES = ["ffn1_w_gate", "ffn1_w_up", "ffn1_w_down", "w_in", "w_out", "ffn2_w_gate", "ffn2_w_up", "ffn2_w_down"]


def build_program(depth=4, debug=False, stages=None):
    cfg = Cfg()
    nc = bass.Bass("TRN2", target_bir_lowering=False)
    D, DFF, NT = cfg.D, cfg.DFF, cfg.NT

    def din(name, shape, dt=F32):
        return nc.dram_tensor(name, shape, dt, kind="ExternalInput").ap()

    def dscr(name, shape, dt):
        return nc.dram_tensor(name, shape, dt, kind="ExternalOutput" if debug else "Internal").ap()

    x = din("x", [2048, D])
    meta = din("meta_tokens", [16, D])
    W = {
        "ffn1_w_gate": din("ffn1_w_gate", [depth, D, DFF]), "ffn1_w_up": din("ffn1_w_up", [depth, D, DFF]),
        "ffn1_w_down": din("ffn1_w_down", [depth, DFF, D]), "w_in": din("w_in", [depth, D, INW]),
        "w_out": din("w_out", [depth, D, D]), "ffn2_w_gate": din("ffn2_w_gate", [depth, D, DFF]),
        "ffn2_w_up": din("ffn2_w_up", [depth, D, DFF]), "ffn2_w_down": din("ffn2_w_down", [depth, DFF, D]),
    }
    vpack = din("vpack", [depth, 128, VP_W])
    goutp = din("goutp", [depth, 128, 2048])
    ck_d = din("ck", [128, CK_W])
    cbf_d = din("cbf", [128, 256], BF16)
    tab2_d = din("tab2", [128, NSH, 256])
    tab0_d = din("tab0", [128, NSH, 144])
    out = nc.dram_tensor("out", [2048, D], F32, kind="ExternalOutput").ap()
    hT = dscr("hT", [cfg.KC, 128, NT], F32)
    qkT = dscr("qkT", [26, 128, NT], BF16)
    vF_d = dscr("vF_d", [NROW, NFH * 65], BF16)
    vW_d = dscr("vW_d", [NROW, 2 * 65], BF16)
    logf_d = dscr("logf_d", [NROW, 16], F32)
    Oscr = dscr("Oscr", [NROW, 2048], F32)

    with ExitStack() as es:
        ps = [es.enter_context(nc.psum_tensor(f"ps{i}", [128, 512], F32)) for i in range(8)]
        ck = es.enter_context(nc.sbuf_tensor("ck_sb", [128, CK_W], F32))
        cbf = es.enter_context(nc.sbuf_tensor("cbf_sb", [128, 256], BF16))
        vp = es.enter_context(nc.sbuf_tensor("vp_sb", [128, VP_W], F32))
        Cb_hm = es.enter_context(nc.sbuf_tensor("Cb_hm", [128, NFH, NBLK], F32))
        c_hm = es.enter_context(nc.sbuf_tensor("c_hm", [128, NFH, NBLK], F32))
        K = {
            "invD": ck[:, CK_INVD:CK_INVD + 128], "bd64": ck[:, CK_BD64:CK_BD64 + 128],
            "ones32": ck[:, CK_ONES:CK_ONES + 128], "tri32": ck[:, CK_TRI:CK_TRI + 128],
            "id32": ck[:, CK_ID:CK_ID + 128], "eps": ck[:, CK_EPS:CK_EPS + 1], "one": ck[:, CK_ONE:CK_ONE + 1],
            "idbf": cbf[:, 0:128], "trimask": cbf[:, 128:256],
        }
        ph = Phase(nc, "c0")
        ph.op("sp", lambda e: e.dma_start(out=ck[:, :], in_=ck_d), writes=["ck"], dma="ck")
        ph.op("sp", lambda e: e.dma_start(out=cbf[:, :], in_=cbf_d), writes=["cbf"], dma="cbf")
        ph.emit()
        init_phase(nc, cfg, K, ps, x, meta, hT, "ini")
        for l in range(depth):
            ph = Phase(nc, f"v{l}")
            ph.op("sp", lambda e, l=l: e.dma_start(out=vp[:, :], in_=vpack[l]), writes=["vp"], dma="vp")
            ph.emit()
            if stages is None or "ffn1" in stages:
                for hi, (lo, hi_, tiles) in enumerate(HALVES):
                    ffn_phase(nc, cfg, K, ps, hT, W["ffn1_w_gate"][l], W["ffn1_w_up"][l], W["ffn1_w_down"][l],
                              vp[:, VP_G1:VP_G1 + cfg.KC], tiles, f"a{l}{hi}")
            def on(st):
                return stages is None or "mix" in stages or st in stages
            if on("inproj"):
                for hi, (lo, hi_, tiles) in enumerate(HALVES):
                    inproj_phase(nc, cfg, K, ps, hT, W["w_in"][l], vp, qkT, vF_d, vW_d, logf_d, lo, hi_, tiles, f"i{l}{hi}")
            if on("decay"):
                decay_phase(nc, cfg, K, ps, logf_d, Cb_hm, c_hm, f"d{l}")
            if on("fox"):
                fox_phase(nc, cfg, K, ps, qkT, vF_d, Cb_hm, c_hm, Oscr, f"x{l}")
            if on("swa"):
                swa_phase(nc, cfg, K, ps, qkT, vW_d, vp, tab2_d, tab0_d, Oscr, f"w{l}")
            if on("outproj"):
                for hi, (lo, hi_, tiles) in enumerate(HALVES):
                    outproj_phase(nc, cfg, K, ps, hT, W["w_out"][l], goutp[l], Oscr, lo, hi_, tiles, f"o{l}{hi}")
            if stages is None or "ffn2" in stages:
                for hi, (lo, hi_, tiles) in enumerate(HALVES):
                    ffn_phase(nc, cfg, K, ps, hT, W["ffn2_w_gate"][l], W["ffn2_w_up"][l], W["ffn2_w_down"][l],
                              vp[:, VP_G2:VP_G2 + cfg.KC], tiles, f"b{l}{hi}")
        final_phase(nc, cfg, K, ps, hT, out, "fin")
    return nc


def host_consts():
    import ml_dtypes
    ck = np.zeros((128, CK_W), np.float32)
    ck[:, CK_INVD:CK_INVD + 128] = 1.0 / 2048.0
    bd = np.zeros((128, 128), np.float32)
    bd[:64, :64] = 1.0 / 64.0
    bd[64:, 64:] = 1.0 / 64.0
    ck[:, CK_BD64:CK_BD64 + 128] = bd
    ck[:, CK_ONES:CK_ONES + 128] = 1.0
    p = np.arange(128)
    tri = (p[:, None] <= p[None, :]).astype(np.float32)
    ck[:, CK_TRI:CK_TRI + 128] = tri
    ck[:, CK_ID:CK_ID + 128] = np.eye(128, dtype=np.float32)
    ck[:, CK_EPS] = 1e-6
    ck[:, CK_ONE] = 1.0
    cbf = np.zeros((128, 256), np.float32)
    cbf[:, 0:128] = np.eye(128)
    cbf[:, 128:256] = tri
    cbf = cbf.astype(ml_dtypes.bfloat16)
    slopes = (2.0 ** (-8.0 * np.arange(1, NSH + 1) / NSH)).astype(np.float64)
    NEG = -30000.0
    k = np.arange(128)[:, None]
    c = np.arange(256)[None, :]
    dist = np.where(c < 128, c - k, 128 + (c - 128) - k)
    ok = np.where(c < 128, c >= k, k > (c - 128))
    tab2 = np.where(ok[:, None, :], -slopes[None, :, None] * dist[:, None, :], NEG).astype(np.float32)
    c0 = np.arange(144)[None, :]
    dist0 = np.where(c0 < 16, c0 - k, 16 + (c0 - 16) - k)
    ok0 = np.where(c0 < 16, c0 >= k, dist0 < 128) & (k < 16)
    tab0 = np.where(ok0[:, None, :], -slopes[None, :, None] * dist0[:, None, :], NEG).astype(np.float32)
    return ck, cbf, np.ascontiguousarray(tab2), np.ascontiguousarray(tab0)


def host_vpack(inp, depth):
    vp = np.zeros((depth, 128, VP_W), np.float32)
    for l in range(depth):
        vp[l, :, VP_G1:VP_G1 + 16] = inp["ffn1_norm"][l].reshape(16, 128).T
        vp[l, :, VP_GM:VP_GM + 16] = inp["mix_norm"][l].reshape(16, 128).T
        vp[l, :, VP_G2:VP_G2 + 16] = inp["ffn2_norm"][l].reshape(16, 128).T
        vp[l, :, VP_FQN] = np.tile(inp["fox_q_norm"][l], 2)
        vp[l, :, VP_FKN] = np.tile(inp["fox_k_norm"][l], 2)
        vp[l, :, VP_SQN] = np.tile(inp["swa_q_norm"][l], 2)
        vp[l, :, VP_SKN] = np.tile(inp["swa_k_norm"][l], 2)
        vp[l, :, VP_BF:VP_BF + 16] = inp["b_forget"][l][None, :]
        vp[l, :, VP_SINK:VP_SINK + 16] = inp["swa_sinks"][l][None, :]
    return vp


def host_gout(inp, depth):
    g = np.zeros((depth, 128, 2048), np.float32)
    for l in range(depth):
        g[l, :, 0:1024] = inp["fox_out_norm"][l][None, :]
        g[l, :, 1024:2048] = inp["swa_out_norm"][l][None, :]
    return g


_NC_CACHE = {}


def run(inputs, depth=4, debug=False, stages=None, ncores=8, trace=False):
    inp = {k: np.asarray(v) for k, v in inputs.items()}
    key = (depth, debug, None if stages is None else tuple(stages))
    if key not in _NC_CACHE:
        _NC_CACHE[key] = build_program(depth, debug, stages)
    nc = _NC_CACHE[key]
    ck, cbf, tab2, tab0 = host_consts()
    vp = host_vpack(inp, depth)
    shared = {"meta_tokens": np.ascontiguousarray(inp["meta_tokens"], dtype=np.float32), "vpack": vp, "ck": ck, "cbf": cbf,
              "goutp": host_gout(inp, depth),
              "tab2": tab2, "tab0": tab0}
    for w in WNAMES:
        shared[w] = np.ascontiguousarray(inp[w][:depth], dtype=np.float32)
    in_maps = []
    for b in range(ncores):
        m = dict(shared)
        m["x"] = np.ascontiguousarray(inp["x"][b], dtype=np.float32)
        in_maps.append(m)
    res = run_bass_kernel_spmd(nc, in_maps, core_ids=list(range(ncores)), trace=trace)
    return res


def kernel(**inputs):
    res = run(inputs, depth=4)
    out = np.stack([np.asarray(r["out"], dtype=np.float32) for r in res.results], axis=0)
    return out
```

```python
import numpy as np
from contextlib import ExitStack

import concourse.bass as bass
import concourse.mybir as mybir
from concourse.bass_utils import run_bass_kernel_spmd

F32 = mybir.dt.float32
BF16 = mybir.dt.bfloat16
AF = mybir.ActivationFunctionType
ALU = mybir.AluOpType
AX = mybir.AxisListType

ENGS = ("pe", "act", "dve", "pool", "sp")


class Op:
    __slots__ = ("eng", "fn", "deps", "dma_key", "signal", "seq", "dma_target", "idx")


class Phase:
    def __init__(self, nc, name):
        self.nc = nc
        self.name = name
        self.ops = []
        self.last_w = {}
        self.readers = {}

    def op(self, eng, fn, reads=(), writes=(), dma=None):
        o = Op()
        o.eng = eng
        o.fn = fn
        o.dma_key = dma
        o.signal = False
        o.seq = 0
        o.dma_target = 0
        o.idx = len(self.ops)
        deps = set()
        for r in reads:
            w = self.last_w.get(r)
            if w is not None:
                deps.add(w)
            if isinstance(r, tuple) and r[0] == "ps":
                for rd in self.readers.get(r, ()):
                    if self.ops[rd].eng != eng:
                        deps.add(rd)
        for w_ in writes:
            w = self.last_w.get(w_)
            if w is not None:
                deps.add(w)
            for rd in self.readers.get(w_, ()):
                deps.add(rd)
        deps.discard(o.idx)
        o.deps = deps
        for r in reads:
            self.readers.setdefault(r, []).append(o.idx)
        for w_ in writes:
            self.last_w[w_] = o.idx
            self.readers[w_] = []
        self.ops.append(o)
        return o

    def emit(self):
        nc = self.nc
        ops = self.ops
        for o in ops:
            nd = set()
            for d in o.deps:
                p = ops[d]
                if p.dma_key is None and o.dma_key is None and p.eng == "pe" and o.eng == "pe":
                    continue
                nd.add(d)
            o.deps = nd
            for d in nd:
                if ops[d].dma_key is None:
                    ops[d].signal = True
        seqc = {e: 0 for e in ENGS}
        dmac = {}
        for o in ops:
            if o.dma_key is not None:
                dmac[o.dma_key] = dmac.get(o.dma_key, 0) + 16
                o.dma_target = dmac[o.dma_key]
            elif o.signal:
                seqc[o.eng] += 1
                o.seq = seqc[o.eng]
        dma_keys = list(dmac.keys())
        dma_issuer = {}
        for o in ops:
            if o.dma_key is not None:
                dma_issuer.setdefault(o.dma_key, o.eng)
        with ExitStack() as es:
            es.enter_context(nc.cleanup_on_exit())
            esem = {e: nc.alloc_semaphore(name=f"{self.name}_s_{e}") for e in ENGS}
            dsem = {k: nc.alloc_semaphore(name=f"{self.name}_d{i}") for i, k in enumerate(dma_keys)}
            block = es.enter_context(nc.Block())

            def stream(ename):
                def body(eng):
                    known = {}
                    for o in ops:
                        if o.eng != ename:
                            continue
                        need = {}
                        for d in o.deps:
                            p = ops[d]
                            if p.dma_key is not None:
                                s, v = dsem[p.dma_key], p.dma_target
                            else:
                                s, v = esem[p.eng], p.seq
                            if need.get(s, 0) < v:
                                need[s] = v
                        for s, v in need.items():
                            if known.get(s, 0) < v:
                                eng.wait_ge(s, v)
                                known[s] = v
                        ins = o.fn(eng)
                        if o.dma_key is not None:
                            ins.then_inc(dsem[o.dma_key], 16)
                        elif o.signal:
                            ins.then_inc(esem[ename], 1)
                    for k in dma_keys:
                        if dma_issuer[k] == ename and known.get(dsem[k], 0) < dmac[k]:
                            eng.wait_ge(dsem[k], dmac[k])
                return body

            if any(o.eng == "pe" for o in ops):
                block.tensor(stream("pe"))
            if any(o.eng == "act" for o in ops):
                block.scalar(stream("act"))
            if any(o.eng == "dve" for o in ops):
                block.vector(stream("dve"))
            if any(o.eng == "pool" for o in ops):
                block.gpsimd(stream("pool"))
            if any(o.eng == "sp" for o in ops):
                block.sync(stream("sp"))


class Cfg:
    def __init__(self, D=2048, DFF=5632, NT=2064, tile=344):
        self.D = D
        self.DFF = DFF
        self.KC = D // 128
        self.FC = DFF // 128
        self.NT = NT
        self.tile = tile
        self.EPS = 1e-6


def _rot(n):
    i = [0]

    def nxt():
        v = i[0] % n
        i[0] += 1
        return v

    return nxt


HD = 64
NFH = 16
NSH = 16
INW = 4368
C_FQ, C_FK, C_FV, C_FZ, C_SQ, C_SK, C_SV = 0, 1024, 2048, 3072, 3088, 4112, 4240
NBLK = 17
NROW = NBLK * 128
VP_G1, VP_GM, VP_G2, VP_FQN, VP_FKN, VP_SQN, VP_SKN, VP_BF, VP_SINK, VP_GO = 0, 16, 32, 48, 49, 50, 51, 52, 68, 84
VP_W = 84
CK_INVD, CK_BD64, CK_ONES, CK_TRI, CK_ID, CK_EPS, CK_ONE = 0, 128, 256, 384, 512, 640, 641
CK_W = 642


def rowof(t):
    return t if t < 16 else t + 112


def blk_cols(b):
    return (0, 16) if b == 0 else (16 + 128 * (b - 1), 128)


def norm_stage(ph, sb, cfg, K, ps, hT, gcol, tiles, offs, xnT, TM):
    KC = cfg.KC
    hT_r = hT.rearrange("c p t -> p c t")
    SM = (TM + 1) // 2 + 1
    hbuf = [sb(f"hbuf{i}", [128, KC, SM], F32) for i in range(2)]
    sq = [sb(f"sq{i}", [128, SM], F32) for i in range(4)]
    rstd = [sb(f"rstd{i}", [128, SM], F32) for i in range(2)]
    ptmp = [sb(f"ptmp{i}", [128, SM], F32) for i in range(2)]
    u = 0
    for ti, (t0, n) in enumerate(tiles):
        h1 = (n // 2 + 1) // 2 * 2
        for (s0, m) in ((0, h1), (h1, n - h1)):
            off = offs[ti] + s0
            hb = hbuf[u % 2]
            r = u % 2
            pb = u % 2
            ph.op("sp", lambda e, hb=hb, a=t0 + s0, m=m: e.dma_start(out=hb[:, :, :m], in_=hT_r[:, :, a:a + m]),
                  reads=[("hT", c, ti) for c in range(KC)], writes=[("hbuf", u % 2, c) for c in range(KC)], dma=("hbuf", u % 2))
            for c in range(KC):
                s = c % 4
                ph.op("act", lambda e, hb=hb, c=c, s=s, m=m: e.activation(out=sq[s][:, :m], in_=hb[:, c, :m], func=AF.Square),
                      reads=[("hbuf", u % 2, c)], writes=[("sq", s)])
                ph.op("pe", lambda e, c=c, s=s, m=m, pb=pb: e.matmul(ps[pb][:, :m], lhsT=K["invD"], rhs=sq[s][:, :m],
                                                                start=(c == 0), stop=(c == KC - 1)),
                      reads=[("sq", s)], writes=[("ps", pb)])
            ph.op("act", lambda e, r=r, m=m, pb=pb: e.activation(out=rstd[r][:, :m], in_=ps[pb][:, :m], func=AF.Ln,
                                                              bias=K["eps"], scale=1.0),
                  reads=[("ps", pb)], writes=[("rstd", r)])
            ph.op("act", lambda e, r=r, m=m: e.activation(out=rstd[r][:, :m], in_=rstd[r][:, :m], func=AF.Exp, scale=-0.5),
                  reads=[("rstd", r)], writes=[("rstd", r)])
            for c in range(KC):
                if True:
                    ph.op("dve", lambda e, hb=hb, c=c, r=r, m=m, off=off: e.scalar_tensor_tensor(
                        out=xnT[:, c, off:off + m], in0=hb[:, c, :m], scalar=gcol[:, c:c + 1], in1=rstd[r][:, :m],
                        op0=ALU.mult, op1=ALU.mult),
                        reads=[("hbuf", u % 2, c), ("rstd", r)], writes=[("xn", ti, c)])
                else:
                    tb = (c // 4) % 2
                    ph.op("pool", lambda e, hb=hb, c=c, r=r, m=m, tb=tb: e.tensor_tensor(
                        out=ptmp[tb][:, :m], in0=hb[:, c, :m], in1=rstd[r][:, :m], op=ALU.mult),
                        reads=[("hbuf", u % 2, c), ("rstd", r)], writes=[("ptmp", tb)])
                    ph.op("pool", lambda e, c=c, m=m, off=off, tb=tb: e.tensor_scalar(
                        out=xnT[:, c, off:off + m], in0=ptmp[tb][:, :m], scalar1=gcol[:, c:c + 1], scalar2=None, op0=ALU.mult),
                        reads=[("ptmp", tb)], writes=[("xn", ti, c)])
            u += 1


def down_stage(ph, sb, cfg, ps, hT, W_r, nin, srcT, src_res, tiles, offs, TM, scale, pre_loaded=0, wd=None):
    KC = cfg.KC
    if wd is None:
        wd = [sb(f"wd{i}", [128, nin, 128], BF16) for i in range(2)]
    hold = [sb(f"hold{i}", [128, TM], F32) for i in range(2)]
    hnew = [sb(f"hnew{i}", [128, TM], F32) for i in range(2)]

    def load_d(dc):
        s = dc % 2
        ph.op("pool", lambda e, dc=dc, s=s: e.dma_start(out=wd[s][:, :, :], in_=W_r[:, :, dc * 128:(dc + 1) * 128]),
              writes=[("wd", s)], dma=("wd", s))

    for dc in range(pre_loaded, min(2, KC)):
        load_d(dc)
    db = _rot(2)
    seq = [(dc, ti) for dc in range(KC) for ti in range(len(tiles))]

    def load_hold(i):
        dc, ti = seq[i]
        t0, n = tiles[ti]
        k = i % 2
        ph.op("sp", lambda e, k=k, dc=dc, t0=t0, n=n: e.dma_start(out=hold[k][:, :n], in_=hT[dc, :, t0:t0 + n]),
              reads=[("hT", dc, ti)], writes=[("hold", k)], dma=("hold", k))

    load_hold(0)
    for i, (dc, ti) in enumerate(seq):
        t0, n = tiles[ti]
        off = offs[ti]
        s = dc % 2
        d_b = 6 + db()
        k = i % 2
        if i + 1 < len(seq):
            load_hold(i + 1)
        for fc in range(nin):
            ph.op("pe", lambda e, fc=fc, s=s, n=n, off=off, d_b=d_b: e.matmul(
                ps[d_b][:, :n], lhsT=wd[s][:, fc, :], rhs=srcT[:, fc, off:off + n], start=(fc == 0), stop=(fc == nin - 1)),
                reads=[("wd", s), src_res(fc, ti)], writes=[("ps", d_b)])
        ph.op("dve", lambda e, k=k, n=n, d_b=d_b: e.scalar_tensor_tensor(
            out=hnew[k][:, :n], in0=ps[d_b][:, :n], scalar=scale, in1=hold[k][:, :n], op0=ALU.mult, op1=ALU.add),
            reads=[("ps", d_b), ("hold", k)], writes=[("hnew", k)])
        ph.op("sp", lambda e, k=k, dc=dc, t0=t0, n=n: e.dma_start(out=hT[dc, :, t0:t0 + n], in_=hnew[k][:, :n]),
              reads=[("hnew", k)], writes=[("hT", dc, ti)], dma=("hnew", k))
        if ti == len(tiles) - 1 and dc + 2 < KC:
            load_d(dc + 2)
    return load_d


def _offs(tiles):
    offs, o = [], 0
    for _, n in tiles:
        offs.append(o)
        o += n
    return offs, o


def ffn_phase(nc, cfg, K, ps, hT, Wg, Wu, Wd, gcol, tiles, name):
    KC, FC = cfg.KC, cfg.FC
    offs, NTp = _offs(tiles)
    TM = max(n for _, n in tiles)
    Wg_r = Wg.rearrange("(c p) f -> p c f", p=128)
    Wu_r = Wu.rearrange("(c p) f -> p c f", p=128)
    Wd_r = Wd.rearrange("(c p) d -> p c d", p=128)
    ph = Phase(nc, name)
    with ExitStack() as es:
        def sb(nm, shape, dt):
            return es.enter_context(nc.sbuf_tensor(f"{name}_{nm}", shape, dt))
        xnT = sb("xnT", [128, KC, NTp], BF16)
        actT = sb("actT", [128, FC, NTp], BF16)
        wg = [sb(f"wg{i}", [128, KC, 128], BF16) for i in range(2)]
        wu = [sb(f"wu{i}", [128, KC, 128], BF16) for i in range(2)]
        wd = [sb(f"wd{i}", [128, FC, 128], BF16) for i in range(2)]
        sg = [sb(f"sg{i}", [128, TM], F32) for i in range(2)]

        def load_gu(fc):
            s = fc % 2
            ph.op("pool", lambda e, fc=fc, s=s: e.dma_start(out=wg[s][:, :, :], in_=Wg_r[:, :, fc * 128:(fc + 1) * 128]),
                  writes=[("wg", s)], dma=("wg", s))
            ph.op("pool", lambda e, fc=fc, s=s: e.dma_start(out=wu[s][:, :, :], in_=Wu_r[:, :, fc * 128:(fc + 1) * 128]),
                  writes=[("wu", s)], dma=("wu", s))

        def load_d(dc):
            s = dc % 2
            ph.op("pool", lambda e, dc=dc, s=s: e.dma_start(out=wd[s][:, :, :], in_=Wd_r[:, :, dc * 128:(dc + 1) * 128]),
                  writes=[("wd", s)], dma=("wd", s))

        load_gu(0)
        load_gu(1)
        norm_stage(ph, sb, cfg, K, ps, hT, gcol, tiles, offs, xnT, TM)
        gb, ub, sgr = _rot(2), _rot(2), _rot(2)
        for fc in range(FC):
            s = fc % 2
            for ti, (t0, n) in enumerate(tiles):
                off = offs[ti]
                g_b = 2 + gb()
                u_b = 4 + ub()
                for c in range(KC):
                    ph.op("pe", lambda e, c=c, s=s, n=n, off=off, g_b=g_b: e.matmul(
                        ps[g_b][:, :n], lhsT=wg[s][:, c, :], rhs=xnT[:, c, off:off + n], start=(c == 0), stop=(c == KC - 1)),
                        reads=[("wg", s), ("xn", ti, c)], writes=[("ps", g_b)])
                for c in range(KC):
                    ph.op("pe", lambda e, c=c, s=s, n=n, off=off, u_b=u_b: e.matmul(
                        ps[u_b][:, :n], lhsT=wu[s][:, c, :], rhs=xnT[:, c, off:off + n], start=(c == 0), stop=(c == KC - 1)),
                        reads=[("wu", s), ("xn", ti, c)], writes=[("ps", u_b)])
                k = sgr()
                ph.op("act", lambda e, k=k, n=n, g_b=g_b: e.activation(out=sg[k][:, :n], in_=ps[g_b][:, :n], func=AF.Silu),
                      reads=[("ps", g_b)], writes=[("sg", k)])
                ph.op("dve", lambda e, k=k, n=n, u_b=u_b, fc=fc, off=off: e.tensor_tensor(
                    out=actT[:, fc, off:off + n], in0=ps[u_b][:, :n], in1=sg[k][:, :n], op=ALU.mult),
                    reads=[("ps", u_b), ("sg", k)], writes=[("act", fc, ti)])
            if fc + 2 < FC:
                load_gu(fc + 2)
            elif fc + 2 == FC:
                load_d(0)
            elif fc + 2 == FC + 1:
                load_d(1)
        down_stage(ph, sb, cfg, ps, hT, Wd_r, FC, actT, lambda fc, ti: ("act", fc, ti), tiles, offs, TM, 0.5,
                   pre_loaded=2, wd=wd)
        ph.emit()


def groups_of(t_lo, t_hi):
    gs = []
    t = t_lo
    while t < t_hi:
        if t < 16:
            ng = 16 - t
        else:
            ng = min(128 - (t - 16) % 128, t_hi - t)
        gs.append((t, ng, rowof(t), t - t_lo))
        t += ng
    return gs


def init_phase(nc, cfg, K, ps, x, meta, hT, name):
    KC = cfg.KC
    hT_r = hT.rearrange("c p t -> p c t")
    ph = Phase(nc, name)
    with ExitStack() as es:
        def sb(nm, shape, dt):
            return es.enter_context(nc.sbuf_tensor(f"{name}_{nm}", shape, dt))
        xt = [sb(f"xt{i}", [128, cfg.D], F32) for i in range(2)]
        ht = [sb(f"ht{i}", [128, KC, 128], F32) for i in range(2)]
        gs = groups_of(0, cfg.NT)

        def load_x(gi):
            t0, ng, row0, off = gs[gi]
            k = gi % 2
            src = meta[0:16, :] if t0 == 0 else x[t0 - 16:t0 - 16 + ng, :]
            ph.op("sp", lambda e: e.dma_start(out=xt[k][:ng, :], in_=src), writes=[("xt", k)], dma=("xt", k))

        load_x(0)
        for gi, (t0, ng, row0, off) in enumerate(gs):
            k = gi % 2
            if gi + 1 < len(gs):
                load_x(gi + 1)
            for q4 in range(KC // 4):
                b = (gi * (KC // 4) + q4) % 8
                for a in range(4):
                    c = q4 * 4 + a
                    ph.op("pe", lambda e, k=k, ng=ng, c=c, a=a, b=b: e.transpose(
                        ps[b][:, a * 128:a * 128 + ng], xt[k][:ng, c * 128:(c + 1) * 128], K["id32"][:ng, :ng]),
                        reads=[("xt", k)], writes=[("ps", b)])
                eng = "act" if q4 % 2 == 0 else "dve"
                if eng == "act":
                    ph.op("act", lambda e, k=k, ng=ng, q4=q4, b=b: e.activation(
                        out=ht[k][:, q4 * 4:q4 * 4 + 4, :ng], in_=ps[b][:, :].rearrange("p (a t) -> p a t", a=4)[:, :, :ng],
                        func=AF.Copy), reads=[("ps", b)], writes=[("ht", k, q4)])
                else:
                    ph.op("dve", lambda e, k=k, ng=ng, q4=q4, b=b: e.tensor_copy(
                        out=ht[k][:, q4 * 4:q4 * 4 + 4, :ng], in_=ps[b][:, :].rearrange("p (a t) -> p a t", a=4)[:, :, :ng]),
                        reads=[("ps", b)], writes=[("ht", k, q4)])
            ph.op("sp", lambda e, k=k, ng=ng, t0=t0: e.dma_start(out=hT_r[:, :, t0:t0 + ng], in_=ht[k][:, :, :ng]),
                  reads=[("ht", k, q4) for q4 in range(KC // 4)], writes=[("hTw", gi)], dma=("ht", k))
        ph.emit()


def final_phase(nc, cfg, K, ps, hT, out, name):
    KC = cfg.KC
    hT_r = hT.rearrange("c p t -> p c t")
    ph = Phase(nc, name)
    with ExitStack() as es:
        def sb(nm, shape, dt):
            return es.enter_context(nc.sbuf_tensor(f"{name}_{nm}", shape, dt))
        ht = [sb(f"ht{i}", [128, KC, 128], F32) for i in range(2)]
        ot = [sb(f"ot{i}", [128, cfg.D], F32) for i in range(2)]
        gs = groups_of(16, cfg.NT)

        def load_h(gi):
            t0, ng, row0, off = gs[gi]
            k = gi % 2
            ph.op("sp", lambda e: e.dma_start(out=ht[k][:, :, :ng], in_=hT_r[:, :, t0:t0 + ng]),
                  writes=[("ht", k)], dma=("ht", k))

        load_h(0)
        for gi, (t0, ng, row0, off) in enumerate(gs):
            k = gi % 2
            if gi + 1 < len(gs):
                load_h(gi + 1)
            for q4 in range(KC // 4):
                b = (gi * (KC // 4) + q4) % 8
                for a in range(4):
                    c = q4 * 4 + a
                    ph.op("pe", lambda e, k=k, ng=ng, c=c, a=a, b=b: e.transpose(
                        ps[b][:ng, a * 128:(a + 1) * 128], ht[k][:, c, :ng], K["id32"]),
                        reads=[("ht", k)], writes=[("ps", b)])
                if q4 % 2 == 0:
                    ph.op("act", lambda e, k=k, ng=ng, q4=q4, b=b: e.activation(
                        out=ot[k][:ng, q4 * 512:(q4 + 1) * 512], in_=ps[b][:ng, :], func=AF.Copy),
                        reads=[("ps", b)], writes=[("ot", k, q4)])
                else:
                    ph.op("dve", lambda e, k=k, ng=ng, q4=q4, b=b: e.tensor_copy(
                        out=ot[k][:ng, q4 * 512:(q4 + 1) * 512], in_=ps[b][:ng, :]),
                        reads=[("ps", b)], writes=[("ot", k, q4)])
            ph.op("sp", lambda e, k=k, ng=ng, t0=t0: e.dma_start(out=out[t0 - 16:t0 - 16 + ng, :], in_=ot[k][:ng, :]),
                  reads=[("ot", k, q4) for q4 in range(KC // 4)], writes=[("outw", gi)], dma=("ot", k))
        ph.emit()


def inproj_phase(nc, cfg, K, ps, hT, Win, vp, qkT, vF_d, vW_d, logf_d, t_lo, t_hi, tiles, name):
    KC = cfg.KC
    offs, NTp = _offs(tiles)
    TM = max(n for _, n in tiles)
    Win_r = Win.rearrange("(c p) f -> p c f", p=128)
    gs = groups_of(t_lo, t_hi)
    ph = Phase(nc, name)
    with ExitStack() as es:
        def sb(nm, shape, dt):
            return es.enter_context(nc.sbuf_tensor(f"{name}_{nm}", shape, dt))
        xnT = sb("xnT", [128, KC, NTp], BF16)
        wb = [sb(f"wb{i}", [128, KC, 128], BF16) for i in range(2)]
        wt = [sb(f"wt{i}", [128, KC, 512], BF16) for i in range(2)]
        qs = [sb(f"qs{i}", [128, TM], F32) for i in range(2)]
        q2 = [sb(f"q2{i}", [128, TM], F32) for i in range(2)]
        rr = [sb(f"rr{i}", [128, TM], F32) for i in range(2)]
        qn = [sb(f"qn{i}", [128, TM], BF16) for i in range(2)]
        vst = [sb(f"vst{i}", [128, 8, 65], BF16) for i in range(2)]
        vsw = [sb(f"vsw{i}", [128, 2, 65], BF16) for i in range(2)]
        fz = [sb(f"fz{i}", [128, 16], F32) for i in range(2)]
        lf = [sb(f"lf{i}", [128, 16], F32) for i in range(2)]

        blocks = []
        for b in range(8):
            blocks.append((b, [(C_FQ + b * 128, 128, 0)], VP_FQN))
        for b in range(8):
            blocks.append((8 + b, [(C_FK + b * 128, 128, 0)], VP_FKN))
        for b in range(8):
            blocks.append((16 + b, [(C_SQ + b * 128, 128, 0)], VP_SQN))
        for g in range(2):
            blocks.append((24 + g, [(C_SK + 64 * g, 64, 0), (C_SK + 64 * g, 64, 64)], VP_SKN))
        ttiles = [(0, [(C_FV, 512, 0)]), (1, [(C_FV + 512, 512, 0)]), (2, [(C_FZ, 16, 0), (C_SV, 128, 16)])]

        def load_b(bi):
            s = bi % 2
            for (c0, ncol, d0) in blocks[bi][1]:
                ph.op("pool", lambda e, s=s, c0=c0, ncol=ncol, d0=d0: e.dma_start(
                    out=wb[s][:, :, d0:d0 + ncol], in_=Win_r[:, :, c0:c0 + ncol]),
                    writes=[("wb", s, d0)], dma=("wb", s))

        def load_t(k):
            s = k % 2
            for (c0, ncol, d0) in ttiles[k][1]:
                ph.op("pool", lambda e, s=s, c0=c0, ncol=ncol, d0=d0: e.dma_start(
                    out=wt[s][:, :, d0:d0 + ncol], in_=Win_r[:, :, c0:c0 + ncol]),
                    writes=[("wt", s, d0)], dma=("wt", s))

        load_b(0)
        load_b(1)
        for i in range(2):
            ph.op("dve", lambda e, i=i: e.memset(vst[i][:, :, :], 1.0), writes=[("vst", i)])
            ph.op("dve", lambda e, i=i: e.memset(vsw[i][:, :, :], 1.0), writes=[("vsw", i)])
        norm_stage(ph, sb, cfg, K, ps, hT, vp[:, VP_GM:VP_GM + KC], tiles, offs, xnT, TM)

        work = [(bi, ti) for bi in range(len(blocks)) for ti in range(len(tiles))]

        def part1(n):
            bi, ti = work[n]
            dst, parts, gcolidx = blocks[bi]
            s = bi % 2
            t0, nn = tiles[ti]
            off = offs[ti]
            q_b = 2 + n % 2
            k = n % 2
            for c in range(KC):
                ph.op("pe", lambda e, c=c: e.matmul(
                    ps[q_b][:, :nn], lhsT=wb[s][:, c, :], rhs=xnT[:, c, off:off + nn], start=(c == 0), stop=(c == KC - 1)),
                    reads=[("wb", s, 0), ("wb", s, 64), ("xn", ti, c)], writes=[("ps", q_b)])
            ph.op("act", lambda e: e.activation(out=qs[k][:, :nn], in_=ps[q_b][:, :nn], func=AF.Copy),
                  reads=[("ps", q_b)], writes=[("qs", k)])
            ph.op("dve", lambda e: e.tensor_tensor(out=q2[k][:, :nn], in0=qs[k][:, :nn], in1=qs[k][:, :nn], op=ALU.mult),
                  reads=[("qs", k)], writes=[("q2", k)])
            if ti == len(tiles) - 1:
                if bi + 2 < len(blocks):
                    load_b(bi + 2)
                elif bi + 2 == len(blocks):
                    load_t(0)
                elif bi + 2 == len(blocks) + 1:
                    load_t(1)

        def part2(n):
            bi, ti = work[n]
            dst, parts, gcolidx = blocks[bi]
            t0, nn = tiles[ti]
            m_b = 4 + n % 2
            k = n % 2
            ph.op("pe", lambda e: e.matmul(ps[m_b][:, :nn], lhsT=K["bd64"], rhs=q2[k][:, :nn], start=True, stop=True),
                  reads=[("q2", k)], writes=[("ps", m_b)])
            ph.op("act", lambda e: e.activation(out=rr[k][:, :nn], in_=ps[m_b][:, :nn], func=AF.Ln, bias=K["eps"], scale=1.0),
                  reads=[("ps", m_b)], writes=[("rr", k)])
            ph.op("act", lambda e: e.activation(out=rr[k][:, :nn], in_=rr[k][:, :nn], func=AF.Exp, scale=-0.5),
                  reads=[("rr", k)], writes=[("rr", k)])
            ph.op("dve", lambda e: e.scalar_tensor_tensor(
                out=qn[k][:, :nn], in0=qs[k][:, :nn], scalar=vp[:, gcolidx:gcolidx + 1], in1=rr[k][:, :nn],
                op0=ALU.mult, op1=ALU.mult),
                reads=[("qs", k), ("rr", k)], writes=[("qn", k)])
            ph.op("sp", lambda e: e.dma_start(out=qkT[dst, :, t0:t0 + nn], in_=qn[k][:, :nn]),
                  reads=[("qn", k)], writes=[("qkT", dst, ti)], dma=("qn", k))

        for n in range(len(work)):
            part1(n)
            if n >= 1:
                part2(n - 1)
        part2(len(work) - 1)

        tb = _rot(2)
        vr, wr, fr = _rot(2), _rot(2), _rot(2)
        for k3, (kk, parts) in enumerate(ttiles):
            s = k3 % 2
            ncols = sum(p[1] for p in parts)
            for (t0, ng, row0, off) in gs:
                t_b = 2 + tb()
                for c in range(KC):
                    ph.op("pe", lambda e, c=c, s=s, ng=ng, off=off, t_b=t_b, ncols=ncols: e.matmul(
                        ps[t_b][:ng, :ncols], lhsT=xnT[:, c, off:off + ng], rhs=wt[s][:, c, :ncols],
                        start=(c == 0), stop=(c == KC - 1)),
                        reads=[("wt", s, 0), ("wt", s, 16)] + [("xn", ti, c) for ti in range(len(tiles))],
                        writes=[("ps", t_b)])
                if kk < 2:
                    v = vr()
                    ph.op("act", lambda e, v=v, ng=ng, t_b=t_b: e.activation(
                        out=vst[v][:ng, :, 0:64], in_=ps[t_b][:ng, :].rearrange("p (h d) -> p h d", h=8), func=AF.Copy),
                        reads=[("ps", t_b), ("vst", v)], writes=[("vst", v)])
                    ph.op("sp", lambda e, v=v, ng=ng, row0=row0, kk=kk: e.dma_start(
                        out=vF_d[row0:row0 + ng, kk * 520:(kk + 1) * 520], in_=vst[v][:ng, :, :].rearrange("p h d -> p (h d)")),
                        reads=[("vst", v)], writes=[("vF_d", row0, kk)], dma=("vst", v))
                else:
                    w = wr()
                    f = fr()
                    ph.op("dve", lambda e, f=f, ng=ng, t_b=t_b: e.tensor_tensor(
                        out=fz[f][:ng, :], in0=ps[t_b][:ng, 0:16], in1=vp[:ng, VP_BF:VP_BF + 16], op=ALU.add),
                        reads=[("ps", t_b)], writes=[("fz", f)])
                    ph.op("act", lambda e, w=w, ng=ng, t_b=t_b: e.activation(
                        out=vsw[w][:ng, :, 0:64], in_=ps[t_b][:ng, 16:144].rearrange("p (h d) -> p h d", h=2), func=AF.Copy),
                        reads=[("ps", t_b), ("vsw", w)], writes=[("vsw", w)])
                    ph.op("act", lambda e, f=f, ng=ng: e.activation(out=fz[f][:ng, :], in_=fz[f][:ng, :], func=AF.Exp, scale=-1.0),
                          reads=[("fz", f)], writes=[("fz", f)])
                    ph.op("act", lambda e, f=f, ng=ng: e.activation(out=fz[f][:ng, :], in_=fz[f][:ng, :], func=AF.Ln,
                                                                 bias=K["one"][:ng, :], scale=1.0),
                          reads=[("fz", f)], writes=[("fz", f)])
                    ph.op("dve", lambda e, f=f, ng=ng: e.tensor_scalar(out=lf[f][:ng, :], in0=fz[f][:ng, :], scalar1=-1.0,
                                                                    scalar2=None, op0=ALU.mult),
                          reads=[("fz", f)], writes=[("lf", f)])
                    ph.op("sp", lambda e, f=f, ng=ng, row0=row0: e.dma_start(out=logf_d[row0:row0 + ng, :], in_=lf[f][:ng, :]),
                          reads=[("lf", f)], writes=[("logf_d", row0)], dma=("lf", f))
                    ph.op("sp", lambda e, w=w, ng=ng, row0=row0: e.dma_start(
                        out=vW_d[row0:row0 + ng, :], in_=vsw[w][:ng, :, :].rearrange("p h d -> p (h d)")),
                        reads=[("vsw", w)], writes=[("vW_d", row0)], dma=("vsw", w))
            if k3 + 2 < len(ttiles):
                load_t(k3 + 2)
        ph.emit()


def decay_phase(nc, cfg, K, ps, logf_d, Cb_hm, c_hm, name):
    ph = Phase(nc, name)
    with ExitStack() as es:
        L = es.enter_context(nc.sbuf_tensor(f"{name}_L", [128, NBLK, 16], F32))
        ph.op("dve", lambda e: e.memset(L[:, :, :], 0.0), writes=["L"])
        ph.op("sp", lambda e: e.dma_start(out=L[0:16, 0, :], in_=logf_d[0:16, :]), reads=["L"], writes=["L0"], dma="L0")
        ph.op("sp", lambda e: e.dma_start(out=L[:, 1:NBLK, :], in_=logf_d[128:NROW, :].rearrange("(b p) h -> p b h", p=128)),
              reads=["L"], writes=["L1"], dma="L1")
        Lf = L[:, :, :].rearrange("p b h -> p (b h)")
        ph.op("pe", lambda e: e.matmul(ps[0][:, 0:NBLK * 16], lhsT=K["ones32"], rhs=Lf, start=True, stop=True),
              reads=["L0", "L1"], writes=[("ps", 0)])
        ph.op("pe", lambda e: e.matmul(ps[1][:, 0:NBLK * 16], lhsT=K["tri32"], rhs=Lf, start=True, stop=True),
              reads=["L0", "L1"], writes=[("ps", 1)])
        ph.op("dve", lambda e: e.memset(Cb_hm[:, :, :], 0.0), writes=["Cb"])
        for i in range(1, NBLK):
            ph.op("dve", lambda e, i=i: e.tensor_tensor(out=Cb_hm[:, :, i], in0=Cb_hm[:, :, i - 1],
                                                     in1=ps[0][:, (i - 1) * 16:i * 16], op=ALU.add),
                  reads=["Cb", ("ps", 0)], writes=["Cb"])
        ph.op("dve", lambda e: e.tensor_tensor(out=c_hm[:, :, :], in0=ps[1][:, 0:NBLK * 16].rearrange("p (j h) -> p h j", h=16),
                                            in1=Cb_hm[:, :, :], op=ALU.add),
              reads=["Cb", ("ps", 1)], writes=["chm"])
        ph.emit()


QCHUNKS = [[0], [1, 2, 3, 4], [5, 6, 7, 8], [9, 10, 11, 12], [13, 14, 15, 16]]


def _load_tokmajor(ph, dst, src_d, width, key):
    ph.op("sp", lambda e: e.dma_start(out=dst[0:16, 0, :], in_=src_d[0:16, :]), writes=[(key, 0)], dma=(key, 0))
    ph.op("sp", lambda e: e.dma_start(out=dst[:, 1:NBLK, :], in_=src_d[128:NROW, :].rearrange("(b p) c -> p b c", p=128)),
          writes=[(key, 1)], dma=(key, 1))


def _store_ost(ph, ost_k, k, Oscr, col0):
    ph.op("sp", lambda e: e.dma_start(out=Oscr[0:16, col0:col0 + 128], in_=ost_k[0:16, 0, :]),
          reads=[("ost", k, i) for i in range(NBLK)], writes=[("Oscr", col0, 0)], dma=("ost", k))
    ph.op("sp", lambda e: e.dma_start(out=Oscr[128:NROW, col0:col0 + 128].rearrange("(b p) c -> p b c", p=128),
                                     in_=ost_k[:, 1:NBLK, :]),
          reads=[("ost", k, i) for i in range(NBLK)], writes=[("Oscr", col0, 1)], dma=("ost", k))


def fox_phase(nc, cfg, K, ps, qkT, vF_d, Cb_hm, c_hm, Oscr, name, npairs=8):
    NT = cfg.NT
    ph = Phase(nc, name)
    with ExitStack() as es:
        def sb(nm, shape, dt):
            return es.enter_context(nc.sbuf_tensor(f"{name}_{nm}", shape, dt))
        vF = sb("vF", [128, NBLK, NFH * 65], BF16)
        qT = [sb(f"qT{i}", [128, NT], BF16) for i in range(2)]
        kT = [sb(f"kT{i}", [128, NT], BF16) for i in range(2)]
        NB = [sb(f"NB{i}", [128, NBLK, NBLK], F32) for i in range(2)]
        pt = [sb(f"pt{i}", [128, 128], BF16) for i in range(12)]
        rec = [sb(f"rec{i}", [128, 4], F32) for i in range(4)]
        ost = [sb(f"ost{i}", [128, NBLK, 128], F32) for i in range(2)]
        _load_tokmajor(ph, vF, vF_d, NFH * 65, "vF")

        def load_pair(hp):
            k = hp % 2
            ph.op("sp", lambda e: e.dma_start(out=qT[k][:, :], in_=qkT[hp, :, :]), writes=[("qT", k)], dma=("qT", k))
            ph.op("sp", lambda e: e.dma_start(out=kT[k][:, :], in_=qkT[8 + hp, :, :]), writes=[("kT", k)], dma=("kT", k))

        def make_nb(h):
            nb = h % 2
            ph.op("dve", lambda e: e.tensor_tensor(
                out=NB[nb][:, :, :], in0=Cb_hm[:, h, :].unsqueeze(1).to_broadcast([128, NBLK, NBLK]),
                in1=c_hm[:, h, :].unsqueeze(2).to_broadcast([128, NBLK, NBLK]), op=ALU.subtract),
                writes=[("NB", nb)])

        items = []
        for hp in range(npairs):
            for hh in range(2):
                for ci, qc in enumerate(QCHUNKS):
                    for j in range(0, qc[-1] + 1):
                        items.append((hp, hh, ci, qc, j))
        ptr, rcr = _rot(12), _rot(4)
        SB = [0, 1, 2, 3, 6, 7]
        NS = len(SB)

        def emit_st(n):
            hp, hh, ci, qc, j = items[n]
            k, pb = hp % 2, 64 * hh
            if hh == 0 and ci == 0 and j == 0:
                if hp == 0:
                    load_pair(0)
                if hp + 1 < npairs:
                    load_pair(hp + 1)
            if ci == 0 and j == 0:
                make_nb(2 * hp + hh)
            iv = [i for i in qc if i >= j]
            k0, nk = blk_cols(j)
            c_lo = blk_cols(iv[0])[0]
            c_hi = blk_cols(iv[-1])[0] + blk_cols(iv[-1])[1]
            s_b = SB[n % NS]
            ph.op("pe", lambda e: e.matmul(
                ps[s_b][:nk, 0:c_hi - c_lo], lhsT=kT[k][pb:pb + 64, k0:k0 + nk], rhs=qT[k][pb:pb + 64, c_lo:c_hi],
                start=True, stop=True),
                reads=[("qT", k), ("kT", k)], writes=[("ps", s_b)])

        def emit_rest(n):
            hp, hh, ci, qc, j = items[n]
            h = 2 * hp + hh
            k, nb = hp % 2, h % 2
            o_b = 4 + (n_chunk[n] % 2)
            iv = [i for i in qc if i >= j]
            k0, nk = blk_cols(j)
            c_lo = blk_cols(iv[0])[0]
            s_b = SB[n % NS]
            for i in iv:
                q0, nq = blk_cols(i)
                a = qc.index(i)
                p = ptr()
                first = (j == 0 and i == qc[0])
                ph.op("act", lambda e, p=p, nq=nq, q0=q0, i=i: e.activation(
                    out=pt[p][:nk, :nq], in_=ps[s_b][:nk, q0 - c_lo:q0 - c_lo + nq], func=AF.Exp,
                    bias=NB[nb][:nk, j, i:i + 1], scale=0.125),
                    reads=[("ps", s_b), ("NB", nb)], writes=[("pt", p)])
                if i == j:
                    ph.op("pool", lambda e, p=p, nq=nq: e.tensor_tensor(
                        out=pt[p][:nk, :nq], in0=pt[p][:nk, :nq], in1=K["trimask"][:nk, :nq], op=ALU.mult),
                        reads=[("pt", p)], writes=[("pt", p)])
                ph.op("pe", lambda e, p=p, nq=nq, a=a, i=i, first=first: e.matmul(
                    ps[o_b][:nq, a * 65:(a + 1) * 65], lhsT=pt[p][:nk, :nq], rhs=vF[:nk, j, h * 65:(h + 1) * 65],
                    start=first, stop=(j == i), skip_group_check=True),
                    reads=[("pt", p), ("vF", 0), ("vF", 1)], writes=[("ps", o_b)])
            if j == qc[-1]:
                nqm = blk_cols(qc[0])[1]
                na = len(qc)
                r = rcr()
                ob3 = ps[o_b][:nqm, 0:na * 65].rearrange("p (a c) -> p a c", c=65)
                ph.op("dve", lambda e, r=r: e.reciprocal(out=rec[r][:nqm, 0:na], in_=ob3[:, :, 64]),
                      reads=[("ps", o_b)], writes=[("rec", r)])
                ph.op("dve", lambda e, r=r: e.tensor_tensor(
                    out=ost[k][:nqm, qc[0]:qc[0] + na, hh * 64:(hh + 1) * 64], in0=ob3[:, :, 0:64],
                    in1=rec[r][:nqm, 0:na].unsqueeze(2).to_broadcast([nqm, na, 64]), op=ALU.mult),
                    reads=[("ps", o_b), ("rec", r)], writes=[("ost", k, i) for i in qc])
                if hh == 1 and ci == len(QCHUNKS) - 1:
                    _store_ost(ph, ost[k], k, Oscr, hp * 128)

        n_chunk = []
        cc = -1
        for (hp, hh, ci, qc, j) in items:
            if j == 0:
                cc += 1
            n_chunk.append(cc)
        LA = 4
        for n in range(min(LA, len(items))):
            emit_st(n)
        for n in range(len(items)):
            if n + LA < len(items):
                emit_st(n + LA)
            emit_rest(n)
        ph.emit()


def swa_phase(nc, cfg, K, ps, qkT, vW_d, vp, tab2_d, tab0_d, Oscr, name, npairs=8):
    NT = cfg.NT
    ph = Phase(nc, name)
    with ExitStack() as es:
        def sb(nm, shape, dt):
            return es.enter_context(nc.sbuf_tensor(f"{name}_{nm}", shape, dt))
        vW = sb("vW", [128, NBLK, 2 * 65], BF16)
        qT = [sb(f"qT{i}", [128, NT], BF16) for i in range(2)]
        kd = [sb(f"kd{i}", [128, NT], BF16) for i in range(2)]
        tab2 = sb("tab2", [128, NSH, 256], F32)
        tab0 = sb("tab0", [128, NSH, 144], F32)
        esink = sb("esink", [128, NSH], F32)
        tt = [sb(f"tt{i}", [128, 256], F32) for i in range(5)]
        pt = [sb(f"pt{i}", [128, 256], BF16) for i in range(5)]
        den = [sb(f"den{i}", [128, 4], F32) for i in range(4)]
        ost = [sb(f"ost{i}", [128, NBLK, 128], F32) for i in range(2)]
        _load_tokmajor(ph, vW, vW_d, 2 * 65, "vW")
        ph.op("sp", lambda e: e.dma_start(out=tab2[:, :, :], in_=tab2_d), writes=["tab2"], dma="tab2")
        ph.op("sp", lambda e: e.dma_start(out=tab0[:, :, :], in_=tab0_d), writes=["tab0"], dma="tab0")
        ph.op("act", lambda e: e.activation(out=esink[:, :], in_=vp[:, VP_SINK:VP_SINK + NSH], func=AF.Exp), writes=["esink"])

        def load_pair(hp):
            k = hp % 2
            ph.op("sp", lambda e: e.dma_start(out=qT[k][:, :], in_=qkT[16 + hp, :, :]), writes=[("qT", k)], dma=("qT", k))

        def load_kd(g):
            ph.op("sp", lambda e: e.dma_start(out=kd[g][:, :], in_=qkT[24 + g, :, :]), writes=[("kd", g)], dma=("kd", g))

        load_kd(0)
        load_kd(1)
        load_pair(0)
        ttr, ptr, dnr = _rot(5), _rot(5), _rot(4)
        items = [(hp, hh, j) for hp in range(npairs) for hh in range(2) for j in range(NBLK)]
        SB = [0, 1, 2, 3, 6, 7]
        NS = len(SB)

        def obank(i):
            return 4 + ((i // 4) % 2), i % 4

        def geom(j):
            iv = [i for i in (j, j + 1) if i < NBLK]
            k0, nk = blk_cols(j)
            c_lo = blk_cols(iv[0])[0]
            c_hi = blk_cols(iv[-1])[0] + blk_cols(iv[-1])[1]
            return iv, k0, nk, c_lo, c_hi

        def emit_st(n):
            hp, hh, j = items[n]
            k, pb, g = hp % 2, 64 * hh, (2 * hp + hh) // 8
            if hh == 0 and j == 0 and hp + 1 < npairs:
                load_pair(hp + 1)
            iv, k0, nk, c_lo, c_hi = geom(j)
            s_b = SB[n % NS]
            ph.op("pe", lambda e: e.matmul(
                ps[s_b][:nk, 0:c_hi - c_lo], lhsT=kd[g][pb:pb + 64, k0:k0 + nk], rhs=qT[k][pb:pb + 64, c_lo:c_hi],
                start=True, stop=True),
                reads=[("qT", k), ("kd", g)], writes=[("ps", s_b)])

        def emit_rest(n):
            hp, hh, j = items[n]
            h = 2 * hp + hh
            k, g = hp % 2, h // 8
            iv, k0, nk, c_lo, c_hi = geom(j)
            ncol = c_hi - c_lo
            s_b = SB[n % NS]
            t = ttr()
            p = ptr()
            tab = tab0 if j == 0 else tab2
            ph.op("dve", lambda e: e.scalar_tensor_tensor(
                out=tt[t][:nk, :ncol], in0=ps[s_b][:nk, :ncol], scalar=0.125, in1=tab[:nk, h, :ncol],
                op0=ALU.mult, op1=ALU.add),
                reads=[("ps", s_b), "tab2", "tab0"], writes=[("tt", t)])
            ph.op("act", lambda e: e.activation(out=pt[p][:nk, :ncol], in_=tt[t][:nk, :ncol], func=AF.Exp),
                  reads=[("tt", t)], writes=[("pt", p)])
            for i in iv:
                q0, nq = blk_cols(i)
                o_b, a = obank(i)
                ph.op("pe", lambda e, nq=nq, q0=q0, o_b=o_b, a=a, st=(j == max(i - 1, 0)), i=i: e.matmul(
                    ps[o_b][:nq, a * 65:(a + 1) * 65], lhsT=pt[p][:nk, q0 - c_lo:q0 - c_lo + nq],
                    rhs=vW[:nk, j, g * 65:(g + 1) * 65], start=st, stop=(j == i), skip_group_check=True),
                    reads=[("pt", p), ("vW", 0), ("vW", 1)], writes=[("ps", o_b)])
            pending.append((n + DELAY, hp, hh, j))
            while pending and (pending[0][0] <= n or j == NBLK - 1):
                _, hp_f, hh_f, j_f = pending.pop(0)
                finalize(hp_f, hh_f, j_f)

        def finalize(hp, hh, j):
            h = 2 * hp + hh
            k = hp % 2
            if j % 4 == 3 or j == NBLK - 1:
                blks = [i for i in range(NBLK) if (i // 4) == (j // 4)]
                na = len(blks)
                o_b = obank(blks[0])[0]
                d = dnr()
                ob3 = ps[o_b][:, 0:na * 65].rearrange("p (a c) -> p a c", c=65)
                ph.op("dve", lambda e, d=d: e.tensor_scalar(out=den[d][:, 0:na], in0=ob3[:, :, 64], scalar1=esink[:, h:h + 1],
                                                          scalar2=None, op0=ALU.add),
                      reads=[("ps", o_b), "esink"], writes=[("den", d)])
                ph.op("dve", lambda e, d=d: e.reciprocal(out=den[d][:, 0:na], in_=den[d][:, 0:na]),
                      reads=[("den", d)], writes=[("den", d)])
                ph.op("dve", lambda e, d=d: e.tensor_tensor(
                    out=ost[k][:, blks[0]:blks[0] + na, hh * 64:(hh + 1) * 64], in0=ob3[:, :, 0:64],
                    in1=den[d][:, 0:na].unsqueeze(2).to_broadcast([128, na, 64]), op=ALU.mult),
                    reads=[("ps", o_b), ("den", d)], writes=[("ost", k, i) for i in blks])
            if hh == 1 and j == NBLK - 1:
                _store_ost(ph, ost[k], k, Oscr, 1024 + hp * 128)

        LA = 4
        DELAY = 2
        pending = []
        for n in range(min(LA, len(items))):
            emit_st(n)
        for n in range(len(items)):
            if n + LA < len(items):
                emit_st(n + LA)
            emit_rest(n)
        while pending:
            _, hp_f, hh_f, j_f = pending.pop(0)
            finalize(hp_f, hh_f, j_f)
        ph.emit()


def outproj_phase(nc, cfg, K, ps, hT, Wout, gout_d, Oscr, t_lo, t_hi, tiles, name):
    KC = cfg.KC
    offs, NTp = _offs(tiles)
    TM = max(n for _, n in tiles)
    Wo_r = Wout.rearrange("(c p) d -> p c d", p=128)
    gs = groups_of(t_lo, t_hi)
    ph = Phase(nc, name)
    with ExitStack() as es:
        def sb(nm, shape, dt):
            return es.enter_context(nc.sbuf_tensor(f"{name}_{nm}", shape, dt))
        onT = sb("onT", [128, KC, NTp], BF16)
        Ob = [sb(f"Ob{i}", [128, 2048], F32) for i in range(2)]
        on = [sb(f"on{i}", [128, 2048], BF16) for i in range(2)]
        junk = sb("junk", [128, 1024], BF16)
        ssq = [sb(f"ssq{i}", [128, 2], F32) for i in range(2)]
        wall = sb("wall", [128, KC, cfg.D], BF16)
        gout = sb("gout", [128, 2048], F32)
        ph.op("sp", lambda e: e.dma_start(out=gout[:, :], in_=gout_d), writes=["gout"], dma="gout")
        for q in range(4):
            ph.op("pool", lambda e, q=q: e.dma_start(out=wall[:, :, q * 512:(q + 1) * 512], in_=Wo_r[:, :, q * 512:(q + 1) * 512]),
                  writes=[("wall", q)], dma=("wall", q))
        tbr = _rot(2)

        def load_o(gi):
            t0, ng, row0, off = gs[gi]
            k = gi % 2
            ph.op("sp", lambda e: e.dma_start(out=Ob[k][:ng, :], in_=Oscr[row0:row0 + ng, :]), writes=[("Ob", k)], dma=("Ob", k))

        def front(gi):
            t0, ng, row0, off = gs[gi]
            k = gi % 2
            ph.op("dve", lambda e: e.memset(ssq[k][:, :], 0.0), writes=[("ssq", k)])
            for hf in range(2):
                ph.op("act", lambda e, hf=hf: e.activation(
                    out=junk[:ng, :], in_=Ob[k][:ng, hf * 1024:(hf + 1) * 1024], func=AF.Square,
                    accum_out=ssq[k][:ng, hf:hf + 1]),
                    reads=[("Ob", k), ("ssq", k)], writes=[("ssq", k), "junk"])
            ph.op("act", lambda e: e.activation(out=ssq[k][:ng, :], in_=ssq[k][:ng, :], func=AF.Ln,
                                                bias=K["eps"][:ng, :], scale=1.0 / 1024.0),
                  reads=[("ssq", k)], writes=[("ssq", k)])
            ph.op("act", lambda e: e.activation(out=ssq[k][:ng, :], in_=ssq[k][:ng, :], func=AF.Exp, scale=-0.5),
                  reads=[("ssq", k)], writes=[("ssq", k)])

        def back(gi):
            t0, ng, row0, off = gs[gi]
            k = gi % 2
            for hf in range(2):
                ph.op("dve", lambda e, hf=hf: e.scalar_tensor_tensor(
                    out=on[k][:ng, hf * 1024:(hf + 1) * 1024], in0=Ob[k][:ng, hf * 1024:(hf + 1) * 1024],
                    scalar=ssq[k][:ng, hf:hf + 1], in1=gout[:ng, hf * 1024:(hf + 1) * 1024],
                    op0=ALU.mult, op1=ALU.mult),
                    reads=[("Ob", k), ("ssq", k), "gout"], writes=[("on", k, hf)])
            for hf in range(2):
                t_b = 2 + tbr()
                pbf = ps[t_b][:, :].bitcast(BF16)
                for a in range(8):
                    f = hf * 8 + a
                    ph.op("pe", lambda e, f=f, a=a, pbf=pbf: e.transpose(
                        pbf[:, a * 128:a * 128 + ng], on[k][:ng, f * 128:(f + 1) * 128], K["idbf"][:ng, :ng]),
                        reads=[("on", k, hf)], writes=[("ps", t_b)])
                if hf == 0:
                    ph.op("act", lambda e, pbf=pbf, hf=hf: e.activation(
                        out=onT[:, hf * 8:hf * 8 + 8, off:off + ng], in_=pbf.rearrange("p (a t) -> p a t", a=8)[:, :, :ng],
                        func=AF.Copy), reads=[("ps", t_b)], writes=[("onT", gi, hf)])
                else:
                    ph.op("dve", lambda e, pbf=pbf, hf=hf: e.tensor_copy(
                        out=onT[:, hf * 8:hf * 8 + 8, off:off + ng], in_=pbf.rearrange("p (a t) -> p a t", a=8)[:, :, :ng]),
                        reads=[("ps", t_b)], writes=[("onT", gi, hf)])

        hold = [sb(f"hold{i}", [128, TM], F32) for i in range(2)]
        hnew = [sb(f"hnew{i}", [128, TM], F32) for i in range(2)]
        cover = []
        for ti, (t0, n) in enumerate(tiles):
            last = max(gi for gi, (g0, ng, row0, goff) in enumerate(gs) if goff < offs[ti] + n)
            cover.append(last)
        steps = [(ti, dc) for ti in range(len(tiles)) for dc in range(KC)]

        def load_hold(i):
            ti, dc = steps[i]
            t0, n = tiles[ti]
            k = i % 2
            ph.op("sp", lambda e: e.dma_start(out=hold[k][:, :n], in_=hT[dc, :, t0:t0 + n]),
                  reads=[("hT", 0, dc, ti)], writes=[("hold", k)], dma=("hold", k))

        load_o(0)
        if len(gs) > 1:
            load_o(1)
        front(0)
        gnext = 0
        dbr = _rot(2)
        si = 0
        load_hold(0)
        for ti, (t0, n) in enumerate(tiles):
            while gnext <= cover[ti]:
                if gnext + 1 < len(gs):
                    front(gnext + 1)
                back(gnext)
                if gnext + 2 < len(gs):
                    load_o(gnext + 2)
                gnext += 1
            off = offs[ti]
            need = [gi for gi, (g0, ng, row0, goff) in enumerate(gs) if goff < off + n and goff + ng > off]
            for dc in range(KC):
                d_b = 6 + dbr()
                k = si % 2
                if si + 1 < len(steps):
                    load_hold(si + 1)
                for fc in range(KC):
                    ph.op("pe", lambda e, fc=fc, dc=dc, d_b=d_b, n=n, off=off: e.matmul(
                        ps[d_b][:, :n], lhsT=wall[:, fc, dc * 128:(dc + 1) * 128], rhs=onT[:, fc, off:off + n],
                        start=(fc == 0), stop=(fc == KC - 1)),
                        reads=[("wall", dc // 4)] + [("onT", gi, fc // 8) for gi in need], writes=[("ps", d_b)])
                ph.op("dve", lambda e, k=k, d_b=d_b, n=n: e.scalar_tensor_tensor(
                    out=hnew[k][:, :n], in0=ps[d_b][:, :n], scalar=1.0, in1=hold[k][:, :n], op0=ALU.mult, op1=ALU.add),
                    reads=[("ps", d_b), ("hold", k)], writes=[("hnew", k)])
                ph.op("sp", lambda e, k=k, dc=dc, t0=t0, n=n: e.dma_start(out=hT[dc, :, t0:t0 + n], in_=hnew[k][:, :n]),
                      reads=[("hnew", k)], writes=[("hT", 0, dc, ti)], dma=("hnew", k))
                si += 1
        ph.emit()


HALVES = [
    (0, 1040, [(0, 344), (344, 344), (688, 352)]),
    (1040, 2064, [(1040, 344), (1384, 344), (1728, 336)]),
]
WNAMES = ["ffn1_w_gate", "ffn1_w_up", "ffn1_w_down", "w_in", "w_out", "ffn2_w_gate", "ffn2_w_up", "ffn2_w_down"]


def build_program(depth=4, debug=False, stages=None):
    cfg = Cfg()
    nc = bass.Bass("TRN2", target_bir_lowering=False)
    D, DFF, NT = cfg.D, cfg.DFF, cfg.NT

    def din(name, shape, dt=F32):
        return nc.dram_tensor(name, shape, dt, kind="ExternalInput").ap()

    def dscr(name, shape, dt):
        return nc.dram_tensor(name, shape, dt, kind="ExternalOutput" if debug else "Internal").ap()

    x = din("x", [2048, D])
    meta = din("meta_tokens", [16, D])
    W = {
        "ffn1_w_gate": din("ffn1_w_gate", [depth, D, DFF]), "ffn1_w_up": din("ffn1_w_up", [depth, D, DFF]),
        "ffn1_w_down": din("ffn1_w_down", [depth, DFF, D]), "w_in": din("w_in", [depth, D, INW]),
        "w_out": din("w_out", [depth, D, D]), "ffn2_w_gate": din("ffn2_w_gate", [depth, D, DFF]),
        "ffn2_w_up": din("ffn2_w_up", [depth, D, DFF]), "ffn2_w_down": din("ffn2_w_down", [depth, DFF, D]),
    }
    vpack = din("vpack", [depth, 128, VP_W])
    goutp = din("goutp", [depth, 128, 2048])
    ck_d = din("ck", [128, CK_W])
    cbf_d = din("cbf", [128, 256], BF16)
    tab2_d = din("tab2", [128, NSH, 256])
    tab0_d = din("tab0", [128, NSH, 144])
    out = nc.dram_tensor("out", [2048, D], F32, kind="ExternalOutput").ap()
    hT = dscr("hT", [cfg.KC, 128, NT], F32)
    qkT = dscr("qkT", [26, 128, NT], BF16)
    vF_d = dscr("vF_d", [NROW, NFH * 65], BF16)
    vW_d = dscr("vW_d", [NROW, 2 * 65], BF16)
    logf_d = dscr("logf_d", [NROW, 16], F32)
    Oscr = dscr("Oscr", [NROW, 2048], F32)

    with ExitStack() as es:
        ps = [es.enter_context(nc.psum_tensor(f"ps{i}", [128, 512], F32)) for i in range(8)]
        ck = es.enter_context(nc.sbuf_tensor("ck_sb", [128, CK_W], F32))
        cbf = es.enter_context(nc.sbuf_tensor("cbf_sb", [128, 256], BF16))
        vp = es.enter_context(nc.sbuf_tensor("vp_sb", [128, VP_W], F32))
        Cb_hm = es.enter_context(nc.sbuf_tensor("Cb_hm", [128, NFH, NBLK], F32))
        c_hm = es.enter_context(nc.sbuf_tensor("c_hm", [128, NFH, NBLK], F32))
        K = {
            "invD": ck[:, CK_INVD:CK_INVD + 128], "bd64": ck[:, CK_BD64:CK_BD64 + 128],
            "ones32": ck[:, CK_ONES:CK_ONES + 128], "tri32": ck[:, CK_TRI:CK_TRI + 128],
            "id32": ck[:, CK_ID:CK_ID + 128], "eps": ck[:, CK_EPS:CK_EPS + 1], "one": ck[:, CK_ONE:CK_ONE + 1],
            "idbf": cbf[:, 0:128], "trimask": cbf[:, 128:256],
        }
        ph = Phase(nc, "c0")
        ph.op("sp", lambda e: e.dma_start(out=ck[:, :], in_=ck_d), writes=["ck"], dma="ck")
        ph.op("sp", lambda e: e.dma_start(out=cbf[:, :], in_=cbf_d), writes=["cbf"], dma="cbf")
        ph.emit()
        init_phase(nc, cfg, K, ps, x, meta, hT, "ini")
        for l in range(depth):
            ph = Phase(nc, f"v{l}")
            ph.op("sp", lambda e, l=l: e.dma_start(out=vp[:, :], in_=vpack[l]), writes=["vp"], dma="vp")
            ph.emit()
            if stages is None or "ffn1" in stages:
                for hi, (lo, hi_, tiles) in enumerate(HALVES):
                    ffn_phase(nc, cfg, K, ps, hT, W["ffn1_w_gate"][l], W["ffn1_w_up"][l], W["ffn1_w_down"][l],
                              vp[:, VP_G1:VP_G1 + cfg.KC], tiles, f"a{l}{hi}")
            def on(st):
                return stages is None or "mix" in stages or st in stages
            if on("inproj"):
                for hi, (lo, hi_, tiles) in enumerate(HALVES):
                    inproj_phase(nc, cfg, K, ps, hT, W["w_in"][l], vp, qkT, vF_d, vW_d, logf_d, lo, hi_, tiles, f"i{l}{hi}")
            if on("decay"):
                decay_phase(nc, cfg, K, ps, logf_d, Cb_hm, c_hm, f"d{l}")
            if on("fox"):
                fox_phase(nc, cfg, K, ps, qkT, vF_d, Cb_hm, c_hm, Oscr, f"x{l}")
            if on("swa"):
                swa_phase(nc, cfg, K, ps, qkT, vW_d, vp, tab2_d, tab0_d, Oscr, f"w{l}")
            if on("outproj"):
                for hi, (lo, hi_, tiles) in enumerate(HALVES):
                    outproj_phase(nc, cfg, K, ps, hT, W["w_out"][l], goutp[l], Oscr, lo, hi_, tiles, f"o{l}{hi}")
            if stages is None or "ffn2" in stages:
                for hi, (lo, hi_, tiles) in enumerate(HALVES):
                    ffn_phase(nc, cfg, K, ps, hT, W["ffn2_w_gate"][l], W["ffn2_w_up"][l], W["ffn2_w_down"][l],
                              vp[:, VP_G2:VP_G2 + cfg.KC], tiles, f"b{l}{hi}")
        final_phase(nc, cfg, K, ps, hT, out, "fin")
    return nc


def host_consts():
    import ml_dtypes
    ck = np.zeros((128, CK_W), np.float32)
    ck[:, CK_INVD:CK_INVD + 128] = 1.0 / 2048.0
    bd = np.zeros((128, 128), np.float32)
    bd[:64, :64] = 1.0 / 64.0
    bd[64:, 64:] = 1.0 / 64.0
    ck[:, CK_BD64:CK_BD64 + 128] = bd
    ck[:, CK_ONES:CK_ONES + 128] = 1.0
    p = np.arange(128)
    tri = (p[:, None] <= p[None, :]).astype(np.float32)
    ck[:, CK_TRI:CK_TRI + 128] = tri
    ck[:, CK_ID:CK_ID + 128] = np.eye(128, dtype=np.float32)
    ck[:, CK_EPS] = 1e-6
    ck[:, CK_ONE] = 1.0
    cbf = np.zeros((128, 256), np.float32)
    cbf[:, 0:128] = np.eye(128)
    cbf[:, 128:256] = tri
    cbf = cbf.astype(ml_dtypes.bfloat16)
    slopes = (2.0 ** (-8.0 * np.arange(1, NSH + 1) / NSH)).astype(np.float64)
    NEG = -30000.0
    k = np.arange(128)[:, None]
    c = np.arange(256)[None, :]
    dist = np.where(c < 128, c - k, 128 + (c - 128) - k)
    ok = np.where(c < 128, c >= k, k > (c - 128))
    tab2 = np.where(ok[:, None, :], -slopes[None, :, None] * dist[:, None, :], NEG).astype(np.float32)
    c0 = np.arange(144)[None, :]
    dist0 = np.where(c0 < 16, c0 - k, 16 + (c0 - 16) - k)
    ok0 = np.where(c0 < 16, c0 >= k, dist0 < 128) & (k < 16)
    tab0 = np.where(ok0[:, None, :], -slopes[None, :, None] * dist0[:, None, :], NEG).astype(np.float32)
    return ck, cbf, np.ascontiguousarray(tab2), np.ascontiguousarray(tab0)


def host_vpack(inp, depth):
    vp = np.zeros((depth, 128, VP_W), np.float32)
    for l in range(depth):
        vp[l, :, VP_G1:VP_G1 + 16] = inp["ffn1_norm"][l].reshape(16, 128).T
        vp[l, :, VP_GM:VP_GM + 16] = inp["mix_norm"][l].reshape(16, 128).T
        vp[l, :, VP_G2:VP_G2 + 16] = inp["ffn2_norm"][l].reshape(16, 128).T
        vp[l, :, VP_FQN] = np.tile(inp["fox_q_norm"][l], 2)
        vp[l, :, VP_FKN] = np.tile(inp["fox_k_norm"][l], 2)
        vp[l, :, VP_SQN] = np.tile(inp["swa_q_norm"][l], 2)
        vp[l, :, VP_SKN] = np.tile(inp["swa_k_norm"][l], 2)
        vp[l, :, VP_BF:VP_BF + 16] = inp["b_forget"][l][None, :]
        vp[l, :, VP_SINK:VP_SINK + 16] = inp["swa_sinks"][l][None, :]
    return vp


def host_gout(inp, depth):
    g = np.zeros((depth, 128, 2048), np.float32)
    for l in range(depth):
        g[l, :, 0:1024] = inp["fox_out_norm"][l][None, :]
        g[l, :, 1024:2048] = inp["swa_out_norm"][l][None, :]
    return g


_NC_CACHE = {}


def run(inputs, depth=4, debug=False, stages=None, ncores=8, trace=False):
    inp = {k: np.asarray(v) for k, v in inputs.items()}
    key = (depth, debug, None if stages is None else tuple(stages))
    if key not in _NC_CACHE:
        _NC_CACHE[key] = build_program(depth, debug, stages)
    nc = _NC_CACHE[key]
    ck, cbf, tab2, tab0 = host_consts()
    vp = host_vpack(inp, depth)
    shared = {"meta_tokens": np.ascontiguousarray(inp["meta_tokens"], dtype=np.float32), "vpack": vp, "ck": ck, "cbf": cbf,
              "goutp": host_gout(inp, depth),
              "tab2": tab2, "tab0": tab0}
    for w in WNAMES:
        shared[w] = np.ascontiguousarray(inp[w][:depth], dtype=np.float32)
    in_maps = []
    for b in range(ncores):
        m = dict(shared)
        m["x"] = np.ascontiguousarray(inp["x"][b], dtype=np.float32)
        in_maps.append(m)
    res = run_bass_kernel_spmd(nc, in_maps, core_ids=list(range(ncores)), trace=trace)
    return res


def kernel(**inputs):
    res = run(inputs, depth=4)
    out = np.stack([np.asarray(r["out"], dtype=np.float32) for r in res.results], axis=0)
    return out
```
